# Optimizing a Trainium2 kernel written in Bass

```python
import math
import jax
import jax.numpy as jnp
from jax import lax
import numpy as np

D_MODEL = 2048
BATCH = 32
SEQ = 256
DEPTH = 2
DEC_BATCH = 4
DEC_SEQ = 4096
PAST_LEN = 256

GRID_W = 64
N_HEADS = 16
N_KV_HEADS = 4
KV_REP = N_HEADS // N_KV_HEADS
HEAD_DIM = 128
Q_W = N_HEADS * HEAD_DIM
KV_W = N_KV_HEADS * HEAD_DIM
WINDOW = 128
ATT_BLOCK = 128
ATT_SCALE = HEAD_DIM ** -0.5
ROPE_THETA = 10000.0
SSM_HEADS = 32
SSM_HEAD_DIM = 64
SSM_INNER = SSM_HEADS * SSM_HEAD_DIM
SSM_GROUPS = 4
SSM_STATE = 128
SSM_BC = SSM_GROUPS * SSM_STATE
SSM_CONV_CH = SSM_INNER + 2 * SSM_BC
SSM_CONV = 5
SSM_CHUNK = 128
SC_WIDTH = D_MODEL
SC_CONV = 3
N_BRANCH = 3
D_FF = ((8 * D_MODEL + 3 * 256 - 1) // (3 * 256)) * 256
N_MOD = 6
EPS = 1e-6
IN_SIZES = (Q_W, KV_W, KV_W, SSM_INNER, SSM_INNER, SSM_BC, SSM_BC, 2 * SSM_HEADS, SC_WIDTH, SC_WIDTH, SC_WIDTH, N_BRANCH * D_MODEL)
D_IN = Q_W + 2 * KV_W + 2 * SSM_INNER + 2 * SSM_BC + 2 * SSM_HEADS + 3 * SC_WIDTH + N_BRANCH * D_MODEL

kernel_name = 'hybrid_ssd_shortconv_swa_flow_step'


def split_cols(t, sizes):
    idx = np.cumsum(np.array(sizes))[:-1].tolist()
    return jnp.split(t, idx, axis=-1)


def rmsnorm(x, g):
    xf = x.astype(jnp.float32)
    y = xf * lax.rsqrt(jnp.mean(jnp.square(xf), axis=-1, keepdims=True) + EPS)
    return (y * g.astype(jnp.float32)).astype(x.dtype)


def dwconv(x, w):
    k = w.shape[0]
    return lax.conv_general_dilated(
        x, w[:, None, :].astype(x.dtype), window_strides=(1,),
        padding=[((k - 1) // 2, (k - 1) // 2)],
        dimension_numbers=('NWC', 'WIO', 'NWC'),
        feature_group_count=x.shape[-1])


def axial_rope(x):
    L = x.shape[1]
    rows = L // GRID_W
    row = jnp.repeat(jnp.arange(rows, dtype=jnp.float32), GRID_W)
    col = jnp.tile(jnp.arange(GRID_W, dtype=jnp.float32), rows)
    n_pairs = HEAD_DIM // 4
    inv = ROPE_THETA ** (-jnp.arange(n_pairs, dtype=jnp.float32) / n_pairs)
    ang = jnp.concatenate([row[:, None] * inv, col[:, None] * inv], axis=-1)
    cos = jnp.cos(ang)[None, :, None, :]
    sin = jnp.sin(ang)[None, :, None, :]
    xf = x.astype(jnp.float32).reshape(x.shape[:-1] + (HEAD_DIM // 2, 2))
    x1, x2 = xf[..., 0], xf[..., 1]
    out = jnp.stack([x1 * cos - x2 * sin, x1 * sin + x2 * cos], axis=-1)
    return out.reshape(x.shape).astype(x.dtype)


def sink_softmax(logits, sink):
    s = jnp.broadcast_to(sink.astype(jnp.float32).reshape(1, N_KV_HEADS, KV_REP, 1, 1), logits.shape[:-1] + (1,))
    p = jax.nn.softmax(jnp.concatenate([s, logits], axis=-1), axis=-1)
    return p[..., 1:]


def context_attention(q, k, v, sink):
    b, s = q.shape[:2]
    nb = s // ATT_BLOCK
    qb = jnp.moveaxis(q.reshape(b, nb, ATT_BLOCK, N_KV_HEADS, KV_REP, HEAD_DIM), 1, 0)

    def block(qi):
        logits = jnp.einsum('bqgrd,bkgd->bgrqk', qi, k).astype(jnp.float32) * ATT_SCALE
        p = sink_softmax(logits, sink).astype(v.dtype)
        return jnp.einsum('bgrqk,bkgd->bqgrd', p, v)

    o = lax.map(block, qb)
    return jnp.moveaxis(o, 0, 1).reshape(b, s, Q_W)


def latent_attention(q, k, v, ctx_k, ctx_v, sink):
    b, L = q.shape[:2]
    nb = L // ATT_BLOCK
    band = ATT_BLOCK + 2 * WINDOW
    kp = jnp.pad(k, ((0, 0), (WINDOW, WINDOW), (0, 0), (0, 0)))
    vp = jnp.pad(v, ((0, 0), (WINDOW, WINDOW), (0, 0), (0, 0)))
    qb = jnp.moveaxis(q.reshape(b, nb, ATT_BLOCK, N_KV_HEADS, KV_REP, HEAD_DIM), 1, 0)

    def block(args):
        i, qi = args
        start = i * ATT_BLOCK
        kb = lax.dynamic_slice_in_dim(kp, start, band, axis=1)
        vb = lax.dynamic_slice_in_dim(vp, start, band, axis=1)
        kabs = start - WINDOW + jnp.arange(band)
        qabs = start + jnp.arange(ATT_BLOCK)
        mask = (jnp.abs(kabs[None, :] - qabs[:, None]) <= WINDOW) & (kabs[None, :] >= 0) & (kabs[None, :] < L)
        s_loc = jnp.einsum('bqgrd,bkgd->bgrqk', qi, kb).astype(jnp.float32) * ATT_SCALE
        s_loc = jnp.where(mask, s_loc, -jnp.inf)
        s_ctx = jnp.einsum('bqgrd,bkgd->bgrqk', qi, ctx_k).astype(jnp.float32) * ATT_SCALE
        p = sink_softmax(jnp.concatenate([s_loc, s_ctx], axis=-1), sink).astype(v.dtype)
        return (jnp.einsum('bgrqk,bkgd->bqgrd', p[..., :band], vb)
                + jnp.einsum('bgrqk,bkgd->bqgrd', p[..., band:], ctx_v))

    o = lax.map(block, (jnp.arange(nb), qb))
    return jnp.moveaxis(o, 0, 1).reshape(b, L, Q_W)


def ssd_scan(x, dt, a_coef, bm, cm, h0):
    b, L = x.shape[:2]
    nc = L // SSM_CHUNK
    G, R, P, N, Q = SSM_GROUPS, SSM_HEADS // SSM_GROUPS, SSM_HEAD_DIM, SSM_STATE, SSM_CHUNK
    xdt = (x * dt[..., None]).reshape(b, nc, Q, G, R, P)
    bm = bm.reshape(b, nc, Q, G, N)
    cm = cm.reshape(b, nc, Q, G, N)
    a = jnp.moveaxis((dt * a_coef).reshape(b, nc, Q, G, R), 2, -1)
    acum = jnp.cumsum(a, axis=-1)
    seg = acum[..., :, None] - acum[..., None, :]
    lower = jnp.tril(jnp.ones((Q, Q), dtype=bool))
    decay = jnp.exp(jnp.where(lower, seg, -jnp.inf))
    cb = jnp.einsum('bcign,bcjgn->bcgij', cm, bm)
    y_diag = jnp.einsum('bcgrij,bcjgrp->bcigrp', cb[:, :, :, None] * decay, xdt)
    to_end = jnp.exp(acum[..., -1:] - acum)
    states = jnp.einsum('bcjgn,bcgrj,bcjgrp->bcgrpn', bm, to_end, xdt)
    chunk_decay = jnp.exp(acum[..., -1])

    def carry_step(h, inp):
        st, dec = inp
        return h * dec[..., None, None] + st, h

    h_last, h_in = lax.scan(carry_step, h0.reshape(b, G, R, P, N),
                            (jnp.moveaxis(states, 1, 0), jnp.moveaxis(chunk_decay, 1, 0)))
    h_in = jnp.moveaxis(h_in, 0, 1)
    y_off = jnp.einsum('bcign,bcgrpn,bcgri->bcigrp', cm, h_in, jnp.exp(acum))
    y = (y_diag + y_off).reshape(b, L, SSM_HEADS, P)
    return y, h_last.reshape(b, SSM_HEADS, P, N)


def ssd_mixer(z, xs, bm, cm, dt_raw, lw, h0):
    b, L, _ = xs.shape
    xbc = dwconv(jnp.concatenate([xs, bm, cm], axis=-1), lw['ssm_conv_w'])
    xbc = jax.nn.silu(xbc + lw['ssm_conv_b'].astype(xbc.dtype))
    xs, bm, cm = split_cols(xbc, (SSM_INNER, SSM_BC, SSM_BC))
    x4 = xs.reshape(b, L, SSM_HEADS, SSM_HEAD_DIM).astype(jnp.float32)
    bm = bm.reshape(b, L, SSM_GROUPS, SSM_STATE).astype(jnp.float32)
    cm = cm.reshape(b, L, SSM_GROUPS, SSM_STATE).astype(jnp.float32)
    dt = jax.nn.softplus(dt_raw.reshape(b, L, 2, SSM_HEADS).astype(jnp.float32)
                         + lw['ssm_dt_bias'].astype(jnp.float32))
    a_coef = -jnp.exp(lw['ssm_a_log'].astype(jnp.float32))
    h0 = h0.astype(jnp.float32)
    rev = lambda t: jnp.flip(t, axis=1)
    y_f, h_f = ssd_scan(x4, dt[:, :, 0], a_coef[0], bm, cm, h0[:, 0])
    y_b, h_b = ssd_scan(rev(x4), rev(dt[:, :, 1]), a_coef[1], rev(bm), rev(cm), h0[:, 1])
    y = y_f + rev(y_b) + x4 * lw['ssm_d'].astype(jnp.float32)[:, None]
    y = y.reshape(b, L, SSM_INNER).astype(z.dtype) * jax.nn.silu(z)
    return rmsnorm(y, lw['ssm_norm_g']), jnp.stack([h_f, h_b], axis=1).astype(z.dtype)


def trunk_layer(x, cond, lw, ctx_cache):
    b, L, _ = x.shape
    mod = (jax.nn.silu(cond) @ lw['w_mod'] + lw['b_mod']).reshape(cond.shape[0], 1, N_MOD, D_MODEL)
    shift_m, scale_m, gate_m, shift_f, scale_f, gate_f = (mod[:, :, i] for i in range(N_MOD))
    h = rmsnorm(x, lw['g_pre_mix']) * (1.0 + scale_m) + shift_m
    q, k, v, z, xs, bm, cm, dt_raw, sc_b, sc_c, sc_h, gates = split_cols(h @ lw['w_in'], IN_SIZES)
    q = q.reshape(b, L, N_HEADS, HEAD_DIM)
    k = k.reshape(b, L, N_KV_HEADS, HEAD_DIM)
    v = v.reshape(b, L, N_KV_HEADS, HEAD_DIM)
    if ctx_cache is None:
        y_att = context_attention(q, k, v, lw['sink'])
        h0 = jnp.zeros((b, 2, SSM_HEADS, SSM_HEAD_DIM, SSM_STATE), jnp.float32)
    else:
        ctx_k, ctx_v, h0 = ctx_cache
        y_att = latent_attention(axial_rope(q), axial_rope(k), v, ctx_k, ctx_v, lw['sink'])
    y_ssd, h_last = ssd_mixer(z, xs, bm, cm, dt_raw, lw, h0)
    y_sc = sc_b * dwconv(sc_c * sc_h, lw['sc_conv_w'])
    g = jax.nn.sigmoid(gates.reshape(b, L, N_BRANCH, D_MODEL))
    merged = (g[:, :, 0] * (y_att @ lw['w_att_out'])
              + g[:, :, 1] * (y_ssd @ lw['w_ssd_out'])
              + g[:, :, 2] * (y_sc @ lw['w_sc_out']))
    x = x + gate_m * rmsnorm(merged @ lw['w_o'], lw['g_post_mix'])
    h = rmsnorm(x, lw['g_pre_ffn']) * (1.0 + scale_f) + shift_f
    f_gate, f_up = jnp.split(h @ lw['w_gate_up'], 2, axis=-1)
    x = x + gate_f * rmsnorm((jax.nn.silu(f_gate) * f_up) @ lw['w_down'], lw['g_post_ffn'])
    new_ctx = (k, v, h_last) if ctx_cache is None else None
    return x, new_ctx


def setup_inputs(seed: int = 0) -> dict:
    key = jax.random.key(seed)
    ks = iter(jax.random.split(key, 32))
    f32 = jnp.float32

    def nrm(shape, scale):
        return jax.random.normal(next(ks), shape, f32) * scale

    def gain(shape):
        return 1.0 + nrm(shape, 0.02)

    x_prompt = nrm((BATCH, SEQ, D_MODEL), 1.0)
    x_sample = nrm((DEC_BATCH, DEC_SEQ, D_MODEL), 1.0)
    cache_k = nrm((DEC_BATCH, DEPTH, PAST_LEN, N_KV_HEADS, HEAD_DIM), 1.0)
    cache_v = nrm((DEC_BATCH, DEPTH, PAST_LEN, N_KV_HEADS, HEAD_DIM), 1.0)
    state_ssm = nrm((DEC_BATCH, DEPTH, 2, SSM_HEADS, SSM_HEAD_DIM, SSM_STATE), 0.5)
    c = nrm((DEC_BATCH, D_MODEL), 1.0)
    c_ctx = nrm((D_MODEL,), 1.0)
    w_mod = nrm((DEPTH, D_MODEL, N_MOD * D_MODEL), 0.5 * D_MODEL ** -0.5)
    b_mod = nrm((DEPTH, N_MOD * D_MODEL), 0.02)
    g_pre_mix = gain((DEPTH, D_MODEL))
    w_in = nrm((DEPTH, D_MODEL, D_IN), D_MODEL ** -0.5)
    sink = nrm((DEPTH, N_HEADS), 1.0)
    ssm_conv_w = nrm((DEPTH, SSM_CONV, SSM_CONV_CH), SSM_CONV ** -0.5)
    ssm_conv_b = nrm((DEPTH, SSM_CONV_CH), 0.02)
    dt0 = jnp.exp(jax.random.uniform(next(ks), (DEPTH, 2, SSM_HEADS), f32, math.log(1e-3), math.log(1e-1)))
    ssm_dt_bias = dt0 + jnp.log(-jnp.expm1(-dt0))
    ssm_a_log = jnp.log(jax.random.uniform(next(ks), (DEPTH, 2, SSM_HEADS), f32, 1.0, 16.0))
    ssm_d = gain((DEPTH, SSM_HEADS))
    ssm_norm_g = gain((DEPTH, SSM_INNER))
    sc_conv_w = nrm((DEPTH, SC_CONV, SC_WIDTH), SC_CONV ** -0.5)
    w_att_out = nrm((DEPTH, Q_W, D_MODEL), Q_W ** -0.5)
    w_ssd_out = nrm((DEPTH, SSM_INNER, D_MODEL), SSM_INNER ** -0.5)
    w_sc_out = nrm((DEPTH, SC_WIDTH, D_MODEL), SC_WIDTH ** -0.5)
    w_o = nrm((DEPTH, D_MODEL, D_MODEL), D_MODEL ** -0.5)
    g_post_mix = gain((DEPTH, D_MODEL))
    g_pre_ffn = gain((DEPTH, D_MODEL))
    w_gate_up = nrm((DEPTH, D_MODEL, 2 * D_FF), D_MODEL ** -0.5)
    w_down = nrm((DEPTH, D_FF, D_MODEL), D_FF ** -0.5)
    g_post_ffn = gain((DEPTH, D_MODEL))
    return {'x_prompt': x_prompt, 'x_sample': x_sample, 'cache_k': cache_k, 'cache_v': cache_v,
            'state_ssm': state_ssm, 'c': c, 'c_ctx': c_ctx, 'w_mod': w_mod, 'b_mod': b_mod,
            'g_pre_mix': g_pre_mix, 'w_in': w_in, 'sink': sink, 'ssm_conv_w': ssm_conv_w,
            'ssm_conv_b': ssm_conv_b, 'ssm_dt_bias': ssm_dt_bias, 'ssm_a_log': ssm_a_log,
            'ssm_d': ssm_d, 'ssm_norm_g': ssm_norm_g, 'sc_conv_w': sc_conv_w,
            'w_att_out': w_att_out, 'w_ssd_out': w_ssd_out, 'w_sc_out': w_sc_out, 'w_o': w_o,
            'g_post_mix': g_post_mix, 'g_pre_ffn': g_pre_ffn, 'w_gate_up': w_gate_up,
            'w_down': w_down, 'g_post_ffn': g_post_ffn}


def reference(x_prompt, x_sample, cache_k, cache_v, state_ssm, c, c_ctx, w_mod, b_mod, g_pre_mix,
              w_in, sink, ssm_conv_w, ssm_conv_b, ssm_dt_bias, ssm_a_log, ssm_d, ssm_norm_g,
              sc_conv_w, w_att_out, w_ssd_out, w_sc_out, w_o, g_post_mix, g_pre_ffn, w_gate_up,
              w_down, g_post_ffn):
    y_prompt = x_prompt
    y_sample = x_sample
    ks_out, vs_out, ss_out = [], [], []
    for l in range(DEPTH):
        lw = {'w_mod': w_mod[l], 'b_mod': b_mod[l], 'g_pre_mix': g_pre_mix[l], 'w_in': w_in[l],
              'sink': sink[l], 'ssm_conv_w': ssm_conv_w[l], 'ssm_conv_b': ssm_conv_b[l],
              'ssm_dt_bias': ssm_dt_bias[l], 'ssm_a_log': ssm_a_log[l], 'ssm_d': ssm_d[l],
              'ssm_norm_g': ssm_norm_g[l], 'sc_conv_w': sc_conv_w[l], 'w_att_out': w_att_out[l],
              'w_ssd_out': w_ssd_out[l], 'w_sc_out': w_sc_out[l], 'w_o': w_o[l],
              'g_post_mix': g_post_mix[l], 'g_pre_ffn': g_pre_ffn[l], 'w_gate_up': w_gate_up[l],
              'w_down': w_down[l], 'g_post_ffn': g_post_ffn[l]}
        y_prompt, (k_l, v_l, s_l) = trunk_layer(y_prompt, c_ctx[None, :], lw, None)
        ks_out.append(k_l)
        vs_out.append(v_l)
        ss_out.append(s_l)
        y_sample, _ = trunk_layer(y_sample, c, lw, (cache_k[:, l], cache_v[:, l], state_ssm[:, l]))
    new_cache_k = jnp.stack(ks_out, axis=1)
    new_cache_v = jnp.stack(vs_out, axis=1)
    new_state_ssm = jnp.stack(ss_out, axis=1)
    return (y_prompt, y_sample, new_cache_k, new_cache_v, new_state_ssm)
```

```python
import math
import numpy as np
import concourse.bass as bass
import concourse.mybir as mybir
from concourse.bass_utils import run_bass_kernel_spmd

F32 = mybir.dt.float32
BF16 = mybir.dt.bfloat16
AF = mybir.ActivationFunctionType
ALU = mybir.AluOpType

D = 2048
NH, NKV, HD = 16, 4, 128
SH, SP_, SN, SG = 32, 64, 128, 4
DFF = 5632
DIN = 20544
O_Q, O_K, O_V, O_Z, O_X, O_DT, O_SCB, O_SCC, O_SCH, O_G = 0, 2048, 2560, 3072, 5120, 8192, 8256, 10304, 12352, 14400
EPS = 1e-6
ATT_SCALE = HD ** -0.5
GRID_W = 64
NEG = -30000.0

PC_GPM, PC_GPF, PC_CW, PC_CB, PC_SCW, PC_BMOD, PC_N = 0, 16, 32, 152, 176, 224, 320
C_ID, C_UF, C_UB, C_NUF, C_NUB, C_ONES, C_MBF, C_MBB, C_MPREV, C_MNEXT, C_PMAT, C_N = range(12)


class Tok:
    __slots__ = ("key", "sem", "val", "eng")

    def __init__(self, key, sem, val, eng):
        self.key, self.sem, self.val, self.eng = key, sem, val, eng


class Res:
    __slots__ = ("w", "r")

    def __init__(self):
        self.w = None
        self.r = {}


class Tl:
    __slots__ = ("ap", "res")

    def __init__(self, ap, res=None):
        self.ap = ap
        self.res = res if res is not None else Res()


class Ring:
    def __init__(self, items):
        self.items = items
        self.i = 0

    def next(self):
        t = self.items[self.i % len(self.items)]
        self.i += 1
        return t


class _Rec:
    def __getattr__(self, name):
        def f(*a, **k):
            return (name, a, k)
        return f


_REC = _Rec()


class EngState:
    def __init__(self, name, sem, skip_self):
        self.name, self.sem, self.skip_self = name, sem, skip_self
        self.count = 0
        self.waited = {}
        self.stream = []
        self.dma_sems = []
        self.dma_n = 0


class Sched:
    ENGS = ("pe", "act", "dve", "pool", "sp")

    def __init__(self, nc, n_dma_slots=8):
        self.nc = nc
        self.ctx = []
        self.E = {}
        for name in self.ENGS:
            cm = nc.semaphore("sem_" + name)
            self.ctx.append(cm)
            self.E[name] = EngState(name, cm.__enter__(), skip_self=(name == "pe"))
        for name in ("sp", "pool", "act"):
            for i in range(n_dma_slots):
                cm = nc.semaphore(f"dsem_{name}_{i}")
                self.ctx.append(cm)
                self.E[name].dma_sems.append(cm.__enter__())
        self.nslot = n_dma_slots

    def _deps(self, E, reads, writes):
        best = {}

        def add(t):
            if t is not None and best.get(t.key, (0, None))[0] < t.val:
                best[t.key] = (t.val, t)

        for r in reads:
            add(r.w)
        for w in writes:
            add(w.w)
            for t in w.r.values():
                add(t)
        waits = []
        for key, (val, t) in best.items():
            if t.eng is E and E.skip_self:
                continue
            if E.waited.get(key, 0) >= val:
                continue
            E.waited[key] = val
            waits.append((t.sem, val))
        return waits

    @staticmethod
    def _update(tok, reads, writes):
        for r in reads:
            old = r.r.get(tok.key)
            if old is None or old.val < tok.val:
                r.r[tok.key] = tok
        for w in writes:
            w.w = tok
            w.r = {}

    def op(self, eng, fn, reads=(), writes=()):
        E = self.E[eng]
        waits = self._deps(E, reads, writes)
        E.count += 1
        tok = Tok(("e", eng), E.sem, E.count, E)
        name, a, k = fn(_REC)
        E.stream.append((waits, lambda e, name=name, a=a, k=k: getattr(e, name)(*a, **k), (E.sem, 1)))
        self._update(tok, reads, writes)
        return tok

    def dma(self, q, out, in_, reads=(), writes=(), **kw):
        E = self.E[q]
        waits = self._deps(E, reads, writes)
        slot = E.dma_n % self.nslot
        gen = E.dma_n // self.nslot
        E.dma_n += 1
        sem = E.dma_sems[slot]
        key = ("d", q, slot)
        if gen > 0 and E.waited.get(key, 0) < 16 * gen:
            waits.append((sem, 16 * gen))
            E.waited[key] = 16 * gen
        tok = Tok(key, sem, 16 * (gen + 1), None)

        def fn(e, out=out, in_=in_, kw=kw):
            return e.dma_start(out=out, in_=in_, **kw)

        E.stream.append((waits, fn, (sem, 16)))
        self._update(tok, reads, writes)
        return tok

    def barrier(self):
        toks = []
        for name, E in self.E.items():
            if E.count > 0:
                toks.append(Tok(("e", name), E.sem, E.count, E))
            for slot in range(min(E.dma_n, self.nslot)):
                n_on = (E.dma_n - 1 - slot) // self.nslot + 1
                toks.append(Tok(("d", name, slot), E.dma_sems[slot], 16 * n_on, None))
        for name, E in self.E.items():
            for t in toks:
                if t.eng is E:
                    continue
                if E.waited.get(t.key, 0) >= t.val:
                    continue
                E.waited[t.key] = t.val
                E.stream.append(([(t.sem, t.val)], None, None))

    def emit(self):
        def run(eng, E):
            for waits, fn, inc in E.stream:
                for sem, val in waits:
                    eng.wait_ge(sem, val)
                if fn is not None:
                    fn(eng).then_inc(inc[0], inc[1])

        S = self
        with self.nc.Block() as block:
            @block.tensor
            def _(e):
                run(e, S.E["pe"])

            @block.scalar
            def _(e):
                run(e, S.E["act"])

            @block.vector
            def _(e):
                run(e, S.E["dve"])

            @block.gpsimd
            def _(e):
                run(e, S.E["pool"])

            @block.sync
            def _(e):
                run(e, S.E["sp"])


class Arena:
    def __init__(self, nc, nbytes):
        self.words = nbytes // 4
        self.t = nc.alloc_sbuf_tensor("arena", [128, self.words], F32)
        self.off = 0
        self.marks = []

    def alloc(self, shape, dtype, P=128):
        esz = 2 if dtype == BF16 else 4
        n = int(np.prod(shape[1:]))
        nwords = ((n * esz + 3) // 4 + 15) // 16 * 16
        off = self.off
        assert off + nwords <= self.words, f"SBUF overflow {off}+{nwords}>{self.words} {shape}"
        self.off += nwords
        a = self.t[0:shape[0], off:off + nwords]
        if dtype != F32:
            a = a.bitcast(dtype)
        a = a[:, 0:n]
        if len(shape) == 3:
            a = a.rearrange("p (a b) -> p a b", a=shape[1])
        elif len(shape) == 4:
            a = a.rearrange("p (a b c) -> p a b c", a=shape[1], b=shape[2])
        return a

    def tl(self, shape, dtype):
        return Tl(self.alloc(shape, dtype))

    def ring(self, n, shape, dtype):
        return Ring([self.tl(shape, dtype) for _ in range(n)])

    def mark(self):
        self.marks.append(self.off)

    def release(self):
        self.off = self.marks.pop()


def bc_last(a, n):
    return bass.AP(a.tensor, a.offset, [list(x) for x in a.ap] + [[0, n]])


def bc_mid(a, n):
    ap = [list(x) for x in a.ap]
    return bass.AP(a.tensor, a.offset, [ap[0], [0, n]] + ap[1:])


def dram_bc(a, P=128):
    ap = [list(x) for x in a.ap]
    return bass.AP(a.tensor, a.offset, [[0, P], ap[-1]])


class Cfg:
    def __init__(self, NPS=4, LS=4096, DEPTH=2, TT=1024, TT5=512, debug=False, stop=99):
        self.NPS, self.LS, self.DEPTH, self.TT, self.TT5, self.debug = NPS, LS, DEPTH, TT, TT5, debug
        self.stop = stop
        self.LP = 256
        self.NP = NPS * 256
        self.NT = self.NP + LS


class KB:
    def __init__(self, cfg):
        self.cfg = cfg
        c = cfg
        nc = self.nc = bass.Bass("TRN2", target_bir_lowering=False)
        L = c.DEPTH

        def inp(name, shape):
            return nc.dram_tensor(name, list(shape), F32, kind="ExternalInput").ap()

        def outp(name, shape):
            return nc.dram_tensor(name, list(shape), F32, kind="ExternalOutput").ap()

        self.dbg = {}

        def scr(name, shape, dt=BF16):
            if c.debug:
                a = nc.dram_tensor(name, list(shape), dt, kind="ExternalOutput").ap()
                self.dbg[name] = a
                return a
            return nc.dram_tensor(name, list(shape), dt).ap()

        self.xp = inp("xp", [c.NP, D])
        self.xs = inp("xs", [c.LS, D])
        self.ck = inp("ck", [L, 256, 512])
        self.cv = inp("cv", [L, 256, 512])
        self.st = inp("st", [L, 2, SH * SP_, SN])
        self.condT = inp("condT", [128, 16, 2])
        self.pcol = inp("pcol", [L, 128, PC_N])
        self.cst = inp("cst", [128, C_N * 128])
        self.ropec = inp("ropec", [128, c.LS])
        self.ropes = inp("ropes", [128, c.LS])
        self.w_mod = inp("w_mod", [L, D, 6 * D])
        self.b_mod = inp("b_mod", [L, 6 * D])
        self.w_in = inp("w_in", [L, D, DIN])
        self.sink = inp("sink", [L, NH])
        self.dt_bias = inp("ssm_dt_bias", [L, 64])
        self.a_log = inp("ssm_a_log", [L, 64])
        self.ssm_d = inp("ssm_d", [L, SH])
        self.ssm_norm_g = inp("ssm_norm_g", [L, D])
        self.w_att_out = inp("w_att_out", [L, D, D])
        self.w_ssd_out = inp("w_ssd_out", [L, D, D])
        self.w_sc_out = inp("w_sc_out", [L, D, D])
        self.w_o = inp("w_o", [L, D, D])
        self.g_post_mix = inp("g_post_mix", [L, D])
        self.g_post_ffn = inp("g_post_ffn", [L, D])
        self.w_gate_up = inp("w_gate_up", [L, D, 2 * DFF])
        self.w_down = inp("w_down", [L, DFF, D])

        self.yp = outp("yp", [c.NP, D])
        self.ys = outp("ys", [c.LS, D])
        self.ock = outp("ock", [c.NPS, L, 256, 512])
        self.ocv = outp("ocv", [c.NPS, L, 256, 512])
        self.ost = outp("ost", [c.NPS, L, 2, SH * SP_, SN])

        NT = c.NT
        self.X1 = scr("X1", [NT, D], F32)
        self.XM = scr("XM", [NT, D], F32)
        self.MOD = scr("MOD", [L, 2, 6 * D], F32)
        self.QT = scr("QT", [D, NT])
        self.KT = scr("KT", [512, NT])
        self.V = scr("V", [NT, 512])
        self.SZ = scr("SZ", [NT, D])
        self.XBC = scr("XBC", [3072, NT])
        self.XCT = scr("XCT", [1024, NT])
        self.XTOK = scr("XTOK", [NT, 2560])
        self.DT = scr("DT", [NT, 64], F32)
        self.SCB = scr("SCB", [D, NT])
        self.SCC = scr("SCC", [D, NT])
        self.SCH = scr("SCH", [D, NT])
        self.G = scr("G", [3 * D, NT])
        self.YATT = scr("YATT", [D, NT])
        self.YSSD = scr("YSSD", [D, NT])
        self.YF = scr("YF", [NT, D], F32)

        self.S = Sched(nc)
        self.A = Arena(nc, 204 * 1024)
        self.banks = Ring([Tl(nc.alloc_psum_tensor(f"ps{i}", [128, 512], F32)[:]) for i in range(8)])
        self.build()

    def bank(self):
        return self.banks.next()

    def end_scope(self):
        self.A.release()
        self.S.barrier()

    def xin(self, l, t0, T):
        c = self.cfg
        if l == 0:
            return self.xp[t0:t0 + T, :] if t0 < c.NP else self.xs[t0 - c.NP:t0 - c.NP + T, :]
        return self.X1[t0:t0 + T, :]

    def xout(self, l, t0, T):
        c = self.cfg
        if l == c.DEPTH - 1:
            return self.yp[t0:t0 + T, :] if t0 < c.NP else self.ys[t0 - c.NP:t0 - c.NP + T, :]
        return self.X1[t0:t0 + T, :]

    def cblk(self, i, dt=F32):
        a = (self.cf if dt == F32 else self.cb)
        return a.ap[:, i * 128:(i + 1) * 128]

    def wload(self, w_ap, KC, bw):
        wt = self.wbufs.next()
        v = wt.ap[:, 0:KC * bw].rearrange("p (c n) -> p c n", c=KC)
        self.S.dma("pool", v, w_ap.rearrange("(c p) n -> p c n", p=128), writes=[wt.res])
        return v, wt.res

    def gemm_fm(self, actT, KC, T, w_ap, epi, blkw=512):
        S = self.S
        ncols = w_ap.shape[1]
        for blk in range(0, ncols, blkw):
            bw = min(blkw, ncols - blk)
            wv, wres = self.wload(w_ap[:, blk:blk + bw], KC, bw)
            for f in range(bw // 128):
                for tb in range((T + 511) // 512):
                    ts = min(512, T - tb * 512)
                    bk = self.bank()
                    for kc in range(KC):
                        S.op("pe", lambda e, bk=bk, wv=wv, kc=kc, f=f, tb=tb, ts=ts: e.matmul(
                            bk.ap[:, 0:ts], lhsT=wv[:, kc, f * 128:(f + 1) * 128],
                            rhs=actT.ap[:, kc, tb * 512:tb * 512 + ts], start=(kc == 0), stop=(kc == KC - 1)),
                            reads=[wres, actT.res], writes=[bk.res])
                    epi(blk // 128 + f, tb, ts, bk)

    def gemm_tm(self, actT, KC, T, w_ap, epi, blkw=512):
        S = self.S
        ncols = w_ap.shape[1]
        for blk in range(0, ncols, blkw):
            bw = min(blkw, ncols - blk)
            wv, wres = self.wload(w_ap[:, blk:blk + bw], KC, bw)
            for m in range(T // 128):
                bk = self.bank()
                for kc in range(KC):
                    S.op("pe", lambda e, bk=bk, wv=wv, kc=kc, m=m, bw=bw: e.matmul(
                        bk.ap[:, 0:bw], lhsT=actT.ap[:, kc, m * 128:(m + 1) * 128],
                        rhs=wv[:, kc, 0:bw], start=(kc == 0), stop=(kc == KC - 1)),
                        reads=[wres, actT.res], writes=[bk.res])
                epi(blk, bw, m, bk)

    def rstd_from_ssq(self, s):
        S = self.S
        S.op("dve", lambda e: e.tensor_scalar(out=s.ap[:, 1:2], in0=s.ap[:, 0:1], scalar1=1.0 / D, scalar2=EPS,
                                              op0=ALU.mult, op1=ALU.add), reads=[s.res], writes=[s.res])
        S.op("act", lambda e: e.activation(out=s.ap[:, 2:3], in_=s.ap[:, 1:2], func=AF.Sqrt), reads=[s.res], writes=[s.res])
        S.op("dve", lambda e: e.reciprocal(out=s.ap[:, 3:4], in_=s.ap[:, 2:3]), reads=[s.res], writes=[s.res])

    def ssq(self, x, s, junk):
        S = self.S
        S.op("dve", lambda e: e.memset(s.ap[:, 0:1], 0.0), writes=[s.res])
        S.op("act", lambda e: e.activation(out=junk.ap, in_=x.ap, func=AF.Square, accum_out=s.ap[:, 0:1]),
             reads=[x.res], writes=[junk.res, s.res])

    def norm_T(self, x, hT, m, Acol, Bcol, s, junk, xn, tmp):
        S = self.S
        self.ssq(x, s, junk)
        self.rstd_from_ssq(s)
        S.op("act", lambda e: e.activation(out=xn.ap, in_=x.ap, func=AF.Identity, scale=s.ap[:, 3:4]),
             reads=[x.res, s.res], writes=[xn.res])
        idb = self.cblk(C_ID, BF16)
        for j in range(2):
            bk = self.bank()
            pb = bk.ap.bitcast(BF16)
            for kk in range(8):
                kc = j * 8 + kk
                S.op("pe", lambda e, pb=pb, kk=kk, kc=kc: e.transpose(
                    out=pb[:, kk * 128:(kk + 1) * 128], in_=xn.ap[:, kc * 128:(kc + 1) * 128], identity=idb),
                    reads=[xn.res, self.cb.res], writes=[bk.res])
            pv = pb.rearrange("p (a b) -> p a b", a=8)
            S.op("dve", lambda e, pv=pv, j=j: e.tensor_tensor(
                out=tmp.ap, in0=pv, in1=bc_last(Acol.ap[:, j * 8:(j + 1) * 8], 128), op=ALU.mult),
                reads=[bk.res, Acol.res], writes=[tmp.res])
            S.op("dve", lambda e, j=j: e.tensor_tensor(
                out=hT.ap[:, j * 8:(j + 1) * 8, m * 128:(m + 1) * 128], in0=tmp.ap,
                in1=bc_last(Bcol.ap[:, j * 8:(j + 1) * 8], 128), op=ALU.add),
                reads=[tmp.res, Bcol.res], writes=[hT.res])

    def modcols(self, l, r, i_shift, i_scale, gcol0):
        S, A = self.S, self.A
        Acol, Bcol = A.tl([128, 16], F32), A.tl([128, 16], F32)
        mc = self.modcol
        S.op("dve", lambda e: e.tensor_scalar(out=Acol.ap, in0=mc.ap[:, i_scale * 16:(i_scale + 1) * 16, r], scalar1=1.0,
                                              scalar2=None, op0=ALU.add), reads=[mc.res], writes=[Acol.res])
        S.op("dve", lambda e: e.tensor_tensor(out=Acol.ap, in0=Acol.ap, in1=self.pc.ap[:, gcol0:gcol0 + 16], op=ALU.mult),
             reads=[Acol.res, self.pc.res], writes=[Acol.res])
        S.op("dve", lambda e: e.tensor_copy(out=Bcol.ap, in_=mc.ap[:, i_shift * 16:(i_shift + 1) * 16, r]),
             reads=[mc.res], writes=[Bcol.res])
        return Acol, Bcol

    def gate_row(self, l, r, i_gate, g_ap, Gt, t2):
        S, A = self.S, self.A
        S.dma("sp", Gt.ap, dram_bc(self.MOD[l, r, i_gate * D:(i_gate + 1) * D]), writes=[Gt.res])
        S.dma("sp", t2.ap, dram_bc(g_ap[l, :]), writes=[t2.res])
        S.op("dve", lambda e: e.tensor_tensor(out=Gt.ap, in0=Gt.ap, in1=t2.ap, op=ALU.mult), reads=[Gt.res, t2.res], writes=[Gt.res])
        return Gt

    def build(self):
        c, S, A = self.cfg, self.S, self.A
        self.cf = A.tl([128, C_N * 128], F32)
        self.cb = A.tl([128, C_N * 128], BF16)
        S.dma("sp", self.cf.ap, self.cst, writes=[self.cf.res])
        S.dma("pool", self.cb.ap, self.cst, writes=[self.cb.res])
        self.wbufs = Ring([A.tl([128, 16 * 512], BF16) for _ in range(3)])
        self.pc = A.tl([128, PC_N], F32)
        self.modcol = A.tl([128, 96, 2], F32)
        self.tiles = []
        if c.NP > 0:
            for t0 in range(0, c.NP, c.TT):
                self.tiles.append((t0, min(c.TT, c.NP - t0), 0, "p"))
        for t0 in range(0, c.LS, c.TT):
            self.tiles.append((c.NP + t0, min(c.TT, c.LS - t0), 1, "s"))
        self.seqs = [(i * 256, 256, "p", i) for i in range(c.NPS)] + [(c.NP, c.LS, "s", 0)]
        for l in range(c.DEPTH):
            self.phase0(l)
            S.barrier()
            if c.stop < 1:
                break
            for tile in self.tiles:
                self.phase1(l, tile)
                S.barrier()
            if c.stop < 2:
                break
            self.attention(l)
            if c.stop < 3:
                break
            self.ssd_conv(l)
            S.barrier()
            if c.stop < 4:
                break
            self.ssd_sweep(l, 0)
            S.barrier()
            self.ssd_sweep(l, 1)
            S.barrier()
            if c.stop < 5:
                break
            for (t0, T, r, kind) in self.tiles:
                for u0 in range(0, T, c.TT5):
                    self.phase56(l, (t0 + u0, min(c.TT5, T - u0), r, kind))
                    S.barrier()
        S.barrier()
        S.emit()

    def phase0(self, l):
        S, A = self.S, self.A
        A.mark()
        S.dma("sp", self.pc.ap, self.pcol[l], writes=[self.pc.res])
        ct = A.tl([128, 16, 2], F32)
        sc = A.tl([128, 16, 2], BF16)
        S.dma("sp", ct.ap, self.condT, writes=[ct.res])
        S.op("act", lambda e: e.activation(out=sc.ap, in_=ct.ap, func=AF.Silu), reads=[ct.res], writes=[sc.res])
        brow = A.ring(2, [2, 512], F32)
        orow = A.ring(2, [2, 512], F32)
        mc = self.modcol
        for n in range(24):
            wv, wres = self.wload(self.w_mod[l][:, n * 512:(n + 1) * 512], 16, 512)
            bk = self.bank()
            for kc in range(16):
                S.op("pe", lambda e, bk=bk, wv=wv, kc=kc: e.matmul(bk.ap[0:2, :], lhsT=sc.ap[:, kc, :], rhs=wv[:, kc, :],
                                                                 start=(kc == 0), stop=(kc == 15)),
                     reads=[wres, sc.res], writes=[bk.res])
            br = brow.next()
            S.dma("sp", br.ap, dram_bc(self.b_mod[l, n * 512:(n + 1) * 512], 2), writes=[br.res])
            orr = orow.next()
            S.op("dve", lambda e, bk=bk, br=br, orr=orr: e.tensor_tensor(out=orr.ap, in0=bk.ap[0:2, :], in1=br.ap, op=ALU.add),
                 reads=[bk.res, br.res], writes=[orr.res])
            S.dma("act", self.MOD[l, :, n * 512:(n + 1) * 512], orr.ap, reads=[orr.res])
            bk2 = self.bank()
            for f in range(4):
                for kc in range(16):
                    S.op("pe", lambda e, bk2=bk2, wv=wv, kc=kc, f=f: e.matmul(
                        bk2.ap[:, f * 2:f * 2 + 2], lhsT=wv[:, kc, f * 128:(f + 1) * 128], rhs=sc.ap[:, kc, :],
                        start=(kc == 0 and f == 0), stop=(kc == 15 and f == 3)), reads=[wres, sc.res], writes=[bk2.res])
            S.op("dve", lambda e, bk2=bk2, n=n: e.tensor_tensor(
                out=mc.ap[:, n * 4:(n + 1) * 4, :], in0=bk2.ap[:, 0:8].rearrange("p (a b) -> p a b", a=4),
                in1=bc_last(self.pc.ap[:, PC_BMOD + n * 4:PC_BMOD + (n + 1) * 4], 2), op=ALU.add),
                reads=[bk2.res, self.pc.res], writes=[mc.res])
        A.release()

    def phase1(self, l, tile):
        c, S, A = self.cfg, self.S, self.A
        t0, T, r, kind = tile
        A.mark()
        hT = A.tl([128, 16, T], BF16)
        Acol, Bcol = self.modcols(l, r, 0, 1, PC_GPM)
        A.mark()
        xt = A.ring(2, [128, D], F32)
        xn = A.ring(2, [128, D], BF16)
        junk = A.tl([128, D], BF16)
        tmp = A.ring(2, [128, 8, 128], F32)
        st = A.ring(2, [128, 4], F32)
        xsrc = self.xin(l, t0, T)
        for m in range(T // 128):
            x = xt.next()
            S.dma("sp", x.ap, xsrc[m * 128:(m + 1) * 128, :], writes=[x.res])
            self.norm_T(x, hT, m, Acol, Bcol, st.next(), junk, xn.next(), tmp.next())
        self.end_scope()
        if kind == "s":
            s0 = t0 - c.NP
            cos, sin = A.tl([128, T], F32), A.tl([128, T], F32)
            S.dma("sp", cos.ap, self.ropec[:, s0:s0 + T], writes=[cos.res])
            S.dma("sp", sin.ap, self.ropes[:, s0:s0 + T], writes=[sin.res])
        stg = A.ring(4, [128, 512], BF16)
        stq = A.ring(2, [128, 512], BF16)
        f32s = A.ring(4, [128, 512], F32)
        W = self.w_in[l]
        pm = self.cblk(C_PMAT, BF16)

        def epi_copy(dst, func=AF.Copy):
            def epi(fb, tb, ts, bk):
                o = stg.next()
                S.op("act", lambda e: e.activation(out=o.ap[:, 0:ts], in_=bk.ap[:, 0:ts], func=func), reads=[bk.res], writes=[o.res])
                S.dma("act", dst[fb * 128:(fb + 1) * 128, t0 + tb * 512:t0 + tb * 512 + ts], o.ap[:, 0:ts], reads=[o.res])
            return epi

        def epi_rope(dst):
            def epi(fb, tb, ts, bk):
                qb = stq.next()
                S.op("act", lambda e: e.activation(out=qb.ap[:, 0:ts], in_=bk.ap[:, 0:ts], func=AF.Copy), reads=[bk.res], writes=[qb.res])
                b2 = self.bank()
                S.op("pe", lambda e: e.matmul(b2.ap[:, 0:ts], lhsT=pm, rhs=qb.ap[:, 0:ts], start=True, stop=True),
                     reads=[qb.res, self.cb.res], writes=[b2.res])
                t1, t2, o = f32s.next(), f32s.next(), stg.next()
                cs = slice(tb * 512, tb * 512 + ts)
                S.op("dve", lambda e: e.tensor_tensor(out=t1.ap[:, 0:ts], in0=qb.ap[:, 0:ts], in1=cos.ap[:, cs], op=ALU.mult),
                     reads=[qb.res, cos.res], writes=[t1.res])
                S.op("dve", lambda e: e.tensor_tensor(out=t2.ap[:, 0:ts], in0=b2.ap[:, 0:ts], in1=sin.ap[:, cs], op=ALU.mult),
                     reads=[b2.res, sin.res], writes=[t2.res])
                S.op("dve", lambda e: e.tensor_tensor(out=o.ap[:, 0:ts], in0=t1.ap[:, 0:ts], in1=t2.ap[:, 0:ts], op=ALU.add),
                     reads=[t1.res, t2.res], writes=[o.res])
                S.dma("act", dst[fb * 128:(fb + 1) * 128, t0 + tb * 512:t0 + tb * 512 + ts], o.ap[:, 0:ts], reads=[o.res])
            return epi

        import os
        en = lambda k: k in os.environ.get("P1", "q,k,x,sc,g,v,kc,z,dt").split(",")
        qk_epi = epi_rope if kind == "s" else epi_copy
        if en("q"):
            self.gemm_fm(hT, 16, T, W[:, O_Q:O_Q + 2048], qk_epi(self.QT))
        if en("k"):
            self.gemm_fm(hT, 16, T, W[:, O_K:O_K + 512], qk_epi(self.KT))
        if en("x"):
            self.gemm_fm(hT, 16, T, W[:, O_X:O_X + 3072], epi_copy(self.XBC))
        if en("sc"):
            self.gemm_fm(hT, 16, T, W[:, O_SCB:O_SCB + 2048], epi_copy(self.SCB))
            self.gemm_fm(hT, 16, T, W[:, O_SCC:O_SCC + 2048], epi_copy(self.SCC))
            self.gemm_fm(hT, 16, T, W[:, O_SCH:O_SCH + 2048], epi_copy(self.SCH))
        if en("g"):
            self.gemm_fm(hT, 16, T, W[:, O_G:O_G + 6144], epi_copy(self.G, AF.Sigmoid))

        def epi_tm(dst, func=AF.Copy, dt=BF16, out32=None):
            def epi(blk, bw, m, bk):
                if dst is not None:
                    o = stg.next() if dt == BF16 else f32s.next()
                    S.op("act", lambda e: e.activation(out=o.ap[:, 0:bw], in_=bk.ap[:, 0:bw], func=func), reads=[bk.res], writes=[o.res])
                    S.dma("act", dst[t0 + m * 128:t0 + (m + 1) * 128, blk:blk + bw], o.ap[:, 0:bw], reads=[o.res])
                if out32 is not None:
                    o2 = f32s.next()
                    S.op("act", lambda e: e.activation(out=o2.ap[:, 0:bw], in_=bk.ap[:, 0:bw], func=AF.Copy), reads=[bk.res], writes=[o2.res])
                    tok = t0 + m * 128
                    S.dma("act", out32[tok // 256, l, tok % 256:tok % 256 + 128, blk:blk + bw], o2.ap[:, 0:bw], reads=[o2.res])
            return epi

        if en("v"):
            self.gemm_tm(hT, 16, T, W[:, O_V:O_V + 512], epi_tm(self.V, out32=self.ocv if kind == "p" else None))
        if kind == "p" and en("kc"):
            self.gemm_tm(hT, 16, T, W[:, O_K:O_K + 512], epi_tm(None, out32=self.ock))
        if en("z"):
            self.gemm_tm(hT, 16, T, W[:, O_Z:O_Z + 2048], epi_tm(self.SZ, AF.Silu))
        if en("dt"):
            self.gemm_tm(hT, 16, T, W[:, O_DT:O_DT + 64], epi_tm(self.DT, dt=F32))
        A.release()

    def attention(self, l):
        c, S, A = self.cfg, self.S, self.A
        A.mark()
        esink = A.tl([128, NH], F32)
        S.dma("sp", esink.ap, dram_bc(self.sink[l, :]), writes=[esink.res])
        S.op("act", lambda e: e.activation(out=esink.ap, in_=esink.ap, func=AF.Exp), reads=[esink.res], writes=[esink.res])
        onesb = self.cblk(C_ONES, BF16)
        idb = self.cblk(C_ID, BF16)
        mprev, mnext = self.cblk(C_MPREV, BF16), self.cblk(C_MNEXT, BF16)
        for (tok0, L, kind, si) in self.seqs:
            nb = L // 128
            for g in range(NKV):
                A.mark()
                KTt = A.tl([128, L], BF16)
                Vt = A.tl([128, nb, 128], BF16)
                Q = A.tl([128, 4, L], BF16)
                Y = A.tl([128, 4, L], BF16)
                pT = A.ring(3, [128, 4, 128], BF16)
                dn = A.ring(2, [128, 4, 128], F32)
                S.dma("sp", KTt.ap, self.KT[g * 128:(g + 1) * 128, tok0:tok0 + L], writes=[KTt.res])
                S.dma("sp", Vt.ap, self.V[tok0:tok0 + L, g * 128:(g + 1) * 128].rearrange("(b p) d -> p b d", p=128), writes=[Vt.res])
                S.dma("sp", Q.ap, self.QT[4 * g * 128:(4 * g + 4) * 128, tok0:tok0 + L].rearrange("(h p) t -> p h t", p=128), writes=[Q.res])
                if kind == "s":
                    ckt = A.tl([128, 2, 128], BF16)
                    cvt = A.tl([128, 2, 128], BF16)
                    cKT = A.tl([128, 256], BF16)
                    S.dma("pool", ckt.ap, self.ck[l, :, g * 128:(g + 1) * 128].rearrange("(b p) d -> p b d", p=128), writes=[ckt.res])
                    S.dma("pool", cvt.ap, self.cv[l, :, g * 128:(g + 1) * 128].rearrange("(b p) d -> p b d", p=128), writes=[cvt.res])
                    bk = self.bank()
                    pb = bk.ap.bitcast(BF16)
                    for b in range(2):
                        S.op("pe", lambda e, b=b, pb=pb: e.transpose(out=pb[:, b * 128:(b + 1) * 128], in_=ckt.ap[:, b, :], identity=idb),
                             reads=[ckt.res, self.cb.res], writes=[bk.res])
                    S.op("act", lambda e, pb=pb: e.activation(out=cKT.ap, in_=pb[:, 0:256], func=AF.Copy), reads=[bk.res], writes=[cKT.res])
                for i in range(nb):
                    kbs = []
                    if kind == "s":
                        if i > 0:
                            kbs.append((KTt.ap[:, (i - 1) * 128:i * 128], Vt.ap[:, i - 1, :], mprev, [KTt.res, Vt.res]))
                        kbs.append((KTt.ap[:, i * 128:(i + 1) * 128], Vt.ap[:, i, :], None, [KTt.res, Vt.res]))
                        if i < nb - 1:
                            kbs.append((KTt.ap[:, (i + 1) * 128:(i + 2) * 128], Vt.ap[:, i + 1, :], mnext, [KTt.res, Vt.res]))
                        for b in range(2):
                            kbs.append((cKT.ap[:, b * 128:(b + 1) * 128], cvt.ap[:, b, :], None, [cKT.res, cvt.res]))
                    else:
                        for b in range(nb):
                            kbs.append((KTt.ap[:, b * 128:(b + 1) * 128], Vt.ap[:, b, :], None, [KTt.res, Vt.res]))
                    bo, bd = self.bank(), self.bank()
                    qv = Q.ap[:, :, i * 128:(i + 1) * 128]
                    for idx, (kap, vap, mask, rr) in enumerate(kbs):
                        bs = self.bank()
                        bsv = bs.ap.rearrange("p (h q) -> p h q", h=4)
                        S.op("pe", lambda e, bsv=bsv, kap=kap, qv=qv: e.matmul(bsv, lhsT=kap, rhs=qv, start=True, stop=True),
                             reads=rr + [Q.res], writes=[bs.res])
                        p = pT.next()
                        S.op("act", lambda e, p=p, bsv=bsv: e.activation(out=p.ap, in_=bsv, func=AF.Exp, scale=ATT_SCALE),
                             reads=[bs.res], writes=[p.res])
                        if mask is not None:
                            S.op("dve", lambda e, p=p, mask=mask: e.tensor_tensor(out=p.ap, in0=p.ap, in1=bc_mid(mask, 4), op=ALU.mult),
                                 reads=[p.res, self.cb.res], writes=[p.res])
                        first, last = idx == 0, idx == len(kbs) - 1
                        S.op("pe", lambda e, p=p, vap=vap, bo=bo, first=first, last=last: e.matmul(
                            bo.ap.rearrange("p (h q) -> p h q", h=4), lhsT=vap, rhs=p.ap, start=first, stop=last),
                            reads=rr + [p.res], writes=[bo.res])
                        S.op("pe", lambda e, p=p, bd=bd, first=first, last=last: e.matmul(
                            bd.ap.rearrange("p (h q) -> p h q", h=4), lhsT=onesb, rhs=p.ap, start=first, stop=last),
                            reads=[p.res, self.cb.res], writes=[bd.res])
                    d = dn.next()
                    S.op("dve", lambda e, d=d, bd=bd: e.tensor_tensor(out=d.ap, in0=bd.ap.rearrange("p (h q) -> p h q", h=4),
                                                                      in1=bc_last(esink.ap[:, 4 * g:4 * g + 4], 128), op=ALU.add),
                         reads=[bd.res, esink.res], writes=[d.res])
                    S.op("dve", lambda e, d=d: e.reciprocal(out=d.ap, in_=d.ap), reads=[d.res], writes=[d.res])
                    S.op("dve", lambda e, d=d, bo=bo, i=i: e.tensor_tensor(
                        out=Y.ap[:, :, i * 128:(i + 1) * 128], in0=bo.ap.rearrange("p (h q) -> p h q", h=4), in1=d.ap, op=ALU.mult),
                        reads=[bo.res, d.res], writes=[Y.res])
                S.dma("act", self.YATT[4 * g * 128:(4 * g + 4) * 128, tok0:tok0 + L].rearrange("(h p) t -> p h t", p=128), Y.ap, reads=[Y.res])
                self.end_scope()
        self.end_scope()

    def ssd_conv(self, l):
        c, S, A = self.cfg, self.S, self.A
        A.mark()
        idb = self.cblk(C_ID, BF16)
        pc = self.pc
        xin = A.ring(3, [128, 516], BF16)
        acc = A.ring(2, [128, 512], F32)
        xc = A.ring(3, [128, 512], BF16)
        xtok = A.ring(2, [128, 4, 2560], BF16)
        for (tok0, L, kind, si) in self.seqs:
            for b0 in range(0, L, 512):
                TB = min(512, L - b0)
                ns = TB // 128
                xt = xtok.next()
                for cc in range(24):
                    xi = xin.next()
                    lo = max(b0 - 2, 0)
                    hi = min(b0 + TB + 2, L)
                    if b0 == 0:
                        S.op("dve", lambda e, xi=xi: e.memset(xi.ap[:, 0:2], 0.0), writes=[xi.res])
                    if b0 + TB == L:
                        S.op("dve", lambda e, xi=xi, TB=TB: e.memset(xi.ap[:, TB + 2:TB + 4], 0.0), writes=[xi.res])
                    S.dma("sp", xi.ap[:, lo - (b0 - 2):hi - (b0 - 2)], self.XBC[cc * 128:(cc + 1) * 128, tok0 + lo:tok0 + hi], writes=[xi.res])
                    a = acc.next()
                    S.op("dve", lambda e, a=a, xi=xi, cc=cc, TB=TB: e.tensor_scalar(
                        out=a.ap[:, 0:TB], in0=xi.ap[:, 0:TB], scalar1=pc.ap[:, PC_CW + cc * 5:PC_CW + cc * 5 + 1], scalar2=None, op0=ALU.mult),
                        reads=[xi.res, pc.res], writes=[a.res])
                    for k in range(1, 5):
                        S.op("dve", lambda e, a=a, xi=xi, cc=cc, k=k, TB=TB: e.scalar_tensor_tensor(
                            out=a.ap[:, 0:TB], in0=xi.ap[:, k:k + TB], scalar=pc.ap[:, PC_CW + cc * 5 + k:PC_CW + cc * 5 + k + 1],
                            in1=a.ap[:, 0:TB], op0=ALU.mult, op1=ALU.add), reads=[xi.res, pc.res, a.res], writes=[a.res])
                    x = xc.next()
                    S.op("act", lambda e, a=a, x=x, cc=cc, TB=TB: e.activation(
                        out=x.ap[:, 0:TB], in_=a.ap[:, 0:TB], func=AF.Silu, bias=pc.ap[:, PC_CB + cc:PC_CB + cc + 1]),
                        reads=[a.res, pc.res], writes=[x.res])
                    if cc >= 16:
                        S.dma("act", self.XCT[(cc - 16) * 128:(cc - 15) * 128, tok0 + b0:tok0 + b0 + TB], x.ap[:, 0:TB], reads=[x.res])
                    if cc < 20:
                        bk = self.bank()
                        pb = bk.ap.bitcast(BF16)
                        for s in range(ns):
                            S.op("pe", lambda e, pb=pb, s=s, x=x: e.transpose(out=pb[:, s * 128:(s + 1) * 128], in_=x.ap[:, s * 128:(s + 1) * 128], identity=idb),
                                 reads=[x.res, self.cb.res], writes=[bk.res])
                        S.op("act", lambda e, pb=pb, xt=xt, cc=cc, ns=ns: e.activation(
                            out=xt.ap[:, 0:ns, cc * 128:(cc + 1) * 128], in_=pb[:, 0:ns * 128].rearrange("p (s d) -> p s d", s=ns), func=AF.Copy),
                            reads=[bk.res], writes=[xt.res])
                S.dma("act", self.XTOK[tok0 + b0:tok0 + b0 + TB, :].rearrange("(s p) d -> p s d", p=128), xt.ap[:, 0:ns, :], reads=[xt.res])
        A.release()

    def ssd_sweep(self, l, d):
        c, S, A = self.cfg, self.S, self.A
        A.mark()
        idb, idf = self.cblk(C_ID, BF16), self.cblk(C_ID, F32)
        U = self.cblk(C_UF if d == 0 else C_UB, F32)
        NU = self.cblk(C_NUF if d == 0 else C_NUB, F32)
        MB = self.cblk(C_MBF if d == 0 else C_MBB, BF16)
        onesf = self.cblk(C_ONES, F32)
        cres = [self.cf.res, self.cb.res]
        dtb = A.tl([128, 32], F32)
        acoef = A.tl([128, 32], F32)
        S.dma("sp", dtb.ap, dram_bc(self.dt_bias[l, d * 32:(d + 1) * 32]), writes=[dtb.res])
        S.dma("sp", acoef.ap, dram_bc(self.a_log[l, d * 32:(d + 1) * 32]), writes=[acoef.res])
        S.op("act", lambda e: e.activation(out=acoef.ap, in_=acoef.ap, func=AF.Exp), reads=[acoef.res], writes=[acoef.res])
        S.op("dve", lambda e: e.tensor_scalar(out=acoef.ap, in0=acoef.ap, scalar1=-1.0, scalar2=None, op0=ALU.mult), reads=[acoef.res], writes=[acoef.res])
        if d == 1:
            Db = A.tl([128, 32], F32)
            gn = A.tl([128, D], F32)
            S.dma("sp", Db.ap, dram_bc(self.ssm_d[l, :]), writes=[Db.res])
            S.dma("sp", gn.ap, dram_bc(self.ssm_norm_g[l, :]), writes=[gn.res])
        St = A.tl([128, 4, 512], F32)
        Sb = A.tl([128, 4, 512], BF16)
        xtk = A.ring(2, [128, 2560], BF16)
        bct = A.ring(2, [128, 8, 128], BF16)
        dtr = A.ring(2, [128, 32], F32)
        sm = A.ring(2, [128, 6, 32], F32)
        abc = A.ring(1, [128, 32, 128], F32)
        xdt = A.ring(2, [128, D], BF16)
        xdte = A.ring(1, [128, D], BF16)
        cbT = A.ring(2, [128, 4, 128], F32)
        dec = A.ring(2, [128, 4, 128], F32)
        LT = A.ring(1, [128, 32, 128], BF16)
        ych = A.ring(1, [128, D], F32)
        t512 = A.ring(2, [128, 512], F32)
        if d == 1:
            yf = A.ring(1, [128, D], F32)
            szt = A.ring(1, [128, D], BF16)
            ybf = A.ring(1, [128, D], BF16)
            junk = A.tl([128, D], BF16)
            s4 = A.ring(2, [128, 4], F32)
            yTt = A.ring(1, [128, 16, 128], BF16)
        f32o = A.ring(2, [128, 128], F32)
        for (tok0, L, kind, si) in self.seqs:
            nch = L // 128
            if kind == "p":
                S.op("dve", lambda e: e.memset(St.ap, 0.0), writes=[St.res])
            else:
                for j in range(16):
                    ld = f32o.next()
                    S.dma("sp", ld.ap, self.st[l, d, j * 128:(j + 1) * 128, :], writes=[ld.res])
                    bk = self.bank()
                    S.op("pe", lambda e, bk=bk, ld=ld: e.transpose(out=bk.ap[:, 0:128], in_=ld.ap, identity=idf),
                         reads=[ld.res, self.cf.res], writes=[bk.res])
                    S.op("act", lambda e, bk=bk, j=j: e.activation(out=St.ap[:, j // 4, (j % 4) * 128:(j % 4 + 1) * 128], in_=bk.ap[:, 0:128], func=AF.Copy),
                         reads=[bk.res], writes=[St.res])
            S.op("act", lambda e: e.activation(out=Sb.ap, in_=St.ap, func=AF.Copy), reads=[St.res], writes=[Sb.res])
            order = range(nch) if d == 0 else range(nch - 1, -1, -1)
            for ci in order:
                tk = tok0 + ci * 128
                xt, bc, dr, m = xtk.next(), bct.next(), dtr.next(), sm.next()
                S.dma("sp", xt.ap, self.XTOK[tk:tk + 128, :], writes=[xt.res])
                S.dma("sp", bc.ap, self.XCT[:, tk:tk + 128].rearrange("(c p) t -> p c t", p=128), writes=[bc.res])
                S.dma("sp", dr.ap, self.DT[tk:tk + 128, d * 32:(d + 1) * 32], writes=[dr.res])
                dt_, a_, ac_, E_, te_, cd_ = (m.ap[:, i, :] for i in range(6))
                S.op("dve", lambda e, dr=dr, dt_=dt_: e.tensor_tensor(out=dt_, in0=dr.ap, in1=dtb.ap, op=ALU.add), reads=[dr.res, dtb.res], writes=[m.res])
                S.op("act", lambda e, dt_=dt_: e.activation(out=dt_, in_=dt_, func=AF.Exp), reads=[m.res], writes=[m.res])
                S.op("act", lambda e, dt_=dt_: e.activation(out=dt_, in_=dt_, func=AF.Ln, bias=1.0), reads=[m.res], writes=[m.res])
                S.op("dve", lambda e, dt_=dt_, a_=a_: e.tensor_tensor(out=a_, in0=dt_, in1=acoef.ap, op=ALU.mult), reads=[m.res, acoef.res], writes=[m.res])
                xd = xdt.next()
                S.op("dve", lambda e, xd=xd, xt=xt, dt_=dt_: e.tensor_tensor(
                    out=xd.ap.rearrange("p (h q) -> p h q", h=32), in0=xt.ap[:, 0:D].rearrange("p (h q) -> p h q", h=32),
                    in1=bc_last(dt_, 64), op=ALU.mult), reads=[xt.res, m.res], writes=[xd.res])
                ab = abc.next()
                S.op("dve", lambda e, ab=ab, a_=a_: e.tensor_copy(out=ab.ap, in_=bc_last(a_, 128)), reads=[m.res], writes=[ab.res])
                bk = self.bank()
                S.op("pe", lambda e, bk=bk, a_=a_: e.matmul(bk.ap[:, 0:32], lhsT=U, rhs=a_, start=True, stop=True), reads=[m.res] + cres, writes=[bk.res])
                S.op("pe", lambda e, bk=bk, a_=a_: e.matmul(bk.ap[:, 32:64], lhsT=onesf, rhs=a_, start=True, stop=True), reads=[m.res] + cres, writes=[bk.res])
                S.op("dve", lambda e, bk=bk, ac_=ac_: e.tensor_scalar(out=ac_, in0=bk.ap[:, 0:32], scalar1=1.0, scalar2=None, op0=ALU.mult), reads=[bk.res], writes=[m.res])
                S.op("act", lambda e, bk=bk, E_=E_: e.activation(out=E_, in_=bk.ap[:, 0:32], func=AF.Exp), reads=[bk.res], writes=[m.res])
                S.op("act", lambda e, bk=bk, cd_=cd_: e.activation(out=cd_, in_=bk.ap[:, 32:64], func=AF.Exp), reads=[bk.res], writes=[m.res])
                S.op("dve", lambda e, bk=bk, ac_=ac_, te_=te_: e.tensor_tensor(out=te_, in0=bk.ap[:, 32:64], in1=ac_, op=ALU.subtract), reads=[bk.res, m.res], writes=[m.res])
                S.op("act", lambda e, te_=te_: e.activation(out=te_, in_=te_, func=AF.Exp), reads=[m.res], writes=[m.res])
                bkc = self.bank()
                for g in range(4):
                    S.op("pe", lambda e, bkc=bkc, bc=bc, g=g: e.matmul(bkc.ap[:, g * 128:(g + 1) * 128], lhsT=bc.ap[:, g, :], rhs=bc.ap[:, 4 + g, :], start=True, stop=True),
                         reads=[bc.res], writes=[bkc.res])
                cbt = cbT.next()
                S.op("act", lambda e, bkc=bkc, cbt=cbt: e.activation(out=cbt.ap, in_=bkc.ap.rearrange("p (g i) -> p g i", g=4), func=AF.Copy), reads=[bkc.res], writes=[cbt.res])
                lt = LT.next()
                for q in range(8):
                    g = q // 2
                    bs = self.bank()
                    for hh in range(4):
                        h = q * 4 + hh
                        S.op("pe", lambda e, bs=bs, ab=ab, h=h, hh=hh: e.matmul(bs.ap[:, hh * 128:(hh + 1) * 128], lhsT=ab.ap[:, h, :], rhs=U, start=(hh == 0), stop=False),
                             reads=[ab.res] + cres, writes=[bs.res])
                    bsv = bs.ap.rearrange("p (h i) -> p h i", h=4)
                    S.op("pe", lambda e, bsv=bsv, ab=ab, q=q: e.matmul(bsv, lhsT=NU, rhs=ab.ap[:, q * 4:(q + 1) * 4, :], start=False, stop=False),
                         reads=[ab.res] + cres, writes=[bs.res])
                    S.op("pe", lambda e, bsv=bsv: e.matmul(bsv, lhsT=idb, rhs=bc_mid(MB, 4), start=False, stop=True), reads=cres, writes=[bs.res])
                    dc = dec.next()
                    S.op("act", lambda e, dc=dc, bsv=bsv: e.activation(out=dc.ap, in_=bsv, func=AF.Exp), reads=[bs.res], writes=[dc.res])
                    S.op("dve", lambda e, dc=dc, lt=lt, cbt=cbt, q=q, g=g: e.tensor_tensor(
                        out=lt.ap[:, q * 4:(q + 1) * 4, :], in0=dc.ap, in1=bc_mid(cbt.ap[:, g, :], 4), op=ALU.mult),
                        reads=[dc.res, cbt.res], writes=[lt.res])
                xe = xdte.next()
                S.op("dve", lambda e, xe=xe, xd=xd, te_=te_: e.tensor_tensor(
                    out=xe.ap.rearrange("p (h q) -> p h q", h=32), in0=xd.ap.rearrange("p (h q) -> p h q", h=32),
                    in1=bc_last(te_, 64), op=ALU.mult), reads=[xd.res, m.res], writes=[xe.res])
                yc = ych.next()
                for g in range(4):
                    by, bo = self.bank(), self.bank()
                    for hh in range(8):
                        h = g * 8 + hh
                        S.op("pe", lambda e, by=by, lt=lt, xd=xd, h=h, hh=hh: e.matmul(
                            by.ap[:, hh * 64:(hh + 1) * 64], lhsT=lt.ap[:, h, :], rhs=xd.ap[:, h * 64:(h + 1) * 64], start=True, stop=True),
                            reads=[lt.res, xd.res], writes=[by.res])
                    S.op("pe", lambda e, bo=bo, bc=bc, g=g: e.matmul(bo.ap, lhsT=bc.ap[:, 4 + g, :], rhs=Sb.ap[:, g, :], start=True, stop=True),
                         reads=[bc.res, Sb.res], writes=[bo.res])
                    t5 = t512.next()
                    S.op("dve", lambda e, t5=t5, bo=bo, E_=E_, g=g: e.tensor_tensor(
                        out=t5.ap.rearrange("p (h q) -> p h q", h=8), in0=bo.ap.rearrange("p (h q) -> p h q", h=8),
                        in1=bc_last(E_[:, g * 8:(g + 1) * 8], 64), op=ALU.mult), reads=[bo.res, m.res], writes=[t5.res])
                    S.op("dve", lambda e, t5=t5, by=by, yc=yc, g=g: e.tensor_tensor(
                        out=yc.ap[:, g * 512:(g + 1) * 512], in0=by.ap, in1=t5.ap, op=ALU.add), reads=[by.res, t5.res], writes=[yc.res])
                for g in range(4):
                    bst = self.bank()
                    S.op("pe", lambda e, bst=bst, xt=xt, xe=xe, g=g: e.matmul(
                        bst.ap, lhsT=xt.ap[:, D + g * 128:D + (g + 1) * 128], rhs=xe.ap[:, g * 512:(g + 1) * 512], start=True, stop=True),
                        reads=[xt.res, xe.res], writes=[bst.res])
                    S.op("dve", lambda e, cd_=cd_, g=g: e.tensor_tensor(
                        out=St.ap[:, g, :].rearrange("p (h q) -> p h q", h=8), in0=St.ap[:, g, :].rearrange("p (h q) -> p h q", h=8),
                        in1=bc_last(cd_[:, g * 8:(g + 1) * 8], 64), op=ALU.mult), reads=[St.res, m.res], writes=[St.res])
                    S.op("dve", lambda e, bst=bst, g=g: e.tensor_tensor(out=St.ap[:, g, :], in0=St.ap[:, g, :], in1=bst.ap, op=ALU.add),
                         reads=[St.res, bst.res], writes=[St.res])
                S.op("act", lambda e: e.activation(out=Sb.ap, in_=St.ap, func=AF.Copy), reads=[St.res], writes=[Sb.res])
                if d == 0:
                    S.dma("act", self.YF[tk:tk + 128, :], yc.ap, reads=[yc.res])
                else:
                    y0, sz, yb, s, yT = yf.next(), szt.next(), ybf.next(), s4.next(), yTt.next()
                    S.dma("sp", y0.ap, self.YF[tk:tk + 128, :], writes=[y0.res])
                    S.dma("sp", sz.ap, self.SZ[tk:tk + 128, :], writes=[sz.res])
                    S.op("dve", lambda e, yc=yc, y0=y0: e.tensor_tensor(out=yc.ap, in0=yc.ap, in1=y0.ap, op=ALU.add), reads=[yc.res, y0.res], writes=[yc.res])
                    S.op("dve", lambda e, y0=y0, xt=xt: e.tensor_tensor(
                        out=y0.ap.rearrange("p (h q) -> p h q", h=32), in0=xt.ap[:, 0:D].rearrange("p (h q) -> p h q", h=32),
                        in1=bc_last(Db.ap, 64), op=ALU.mult), reads=[xt.res, Db.res], writes=[y0.res])
                    S.op("dve", lambda e, yc=yc, y0=y0: e.tensor_tensor(out=yc.ap, in0=yc.ap, in1=y0.ap, op=ALU.add), reads=[yc.res, y0.res], writes=[yc.res])
                    S.op("dve", lambda e, yc=yc, sz=sz: e.tensor_tensor(out=yc.ap, in0=yc.ap, in1=sz.ap, op=ALU.mult), reads=[yc.res, sz.res], writes=[yc.res])
                    self.ssq(yc, s, junk)
                    self.rstd_from_ssq(s)
                    S.op("dve", lambda e, yc=yc, yb=yb, s=s: e.scalar_tensor_tensor(
                        out=yb.ap, in0=yc.ap, scalar=s.ap[:, 3:4], in1=gn.ap, op0=ALU.mult, op1=ALU.mult),
                        reads=[yc.res, s.res, gn.res], writes=[yb.res])
                    for j in range(2):
                        bk = self.bank()
                        pb = bk.ap.bitcast(BF16)
                        for kk in range(8):
                            kc = j * 8 + kk
                            S.op("pe", lambda e, pb=pb, kk=kk, kc=kc, yb=yb: e.transpose(
                                out=pb[:, kk * 128:(kk + 1) * 128], in_=yb.ap[:, kc * 128:(kc + 1) * 128], identity=idb),
                                reads=[yb.res, self.cb.res], writes=[bk.res])
                        S.op("act", lambda e, pb=pb, yT=yT, j=j: e.activation(
                            out=yT.ap[:, j * 8:(j + 1) * 8, :], in_=pb.rearrange("p (a b) -> p a b", a=8), func=AF.Copy),
                            reads=[bk.res], writes=[yT.res])
                    S.dma("act", self.YSSD[:, tk:tk + 128].rearrange("(c p) t -> p c t", p=128), yT.ap, reads=[yT.res])
            if kind == "p":
                for j in range(16):
                    bk = self.bank()
                    S.op("pe", lambda e, bk=bk, j=j: e.transpose(out=bk.ap[:, 0:128], in_=St.ap[:, j // 4, (j % 4) * 128:(j % 4 + 1) * 128], identity=idf),
                         reads=[St.res, self.cf.res], writes=[bk.res])
                    o = f32o.next()
                    S.op("act", lambda e, bk=bk, o=o: e.activation(out=o.ap, in_=bk.ap[:, 0:128], func=AF.Copy), reads=[bk.res], writes=[o.res])
                    S.dma("act", self.ost[si, l, d, j * 128:(j + 1) * 128, :], o.ap, reads=[o.res])
        A.release()

    def phase56(self, l, tile):
        c, S, A = self.cfg, self.S, self.A
        t0, T, r, kind = tile
        nm = T // 128
        A.mark()
        pc = self.pc
        idb, idf = self.cblk(C_ID, BF16), self.cblk(C_ID, F32)
        big = A.tl([128, 16 * T], F32)
        merged = Tl(big.ap.rearrange("p (c t) -> p c t", c=16), big.res)
        oacc = Tl(big.ap.rearrange("p (m f) -> p m f", m=nm), big.res)
        yT = A.tl([128, 16, T], BF16)
        Gf = A.tl([128, D], F32)
        A.mark()
        Gm = A.tl([128, D], F32)
        gtmp = A.tl([128, D], F32)
        Acol, Bcol = self.modcols(l, r, 3, 4, PC_GPF)
        self.gate_row(l, r, 2, self.g_post_mix, Gm, gtmp)
        self.gate_row(l, r, 5, self.g_post_ffn, Gf, gtmp)
        gt = A.ring(3, [128, 512], BF16)
        tmp = A.ring(2, [128, 512], F32)
        if kind == "p":
            sq0, sqL = (t0 // 256) * 256, 256
        else:
            sq0, sqL = c.NP, c.LS

        def epi_merge(b):
            def epi(fb, tb, ts, bk):
                g = gt.next()
                S.dma("sp", g.ap[:, 0:ts], self.G[b * D + fb * 128:b * D + (fb + 1) * 128, t0 + tb * 512:t0 + tb * 512 + ts], writes=[g.res])
                dst = merged.ap[:, fb, tb * 512:tb * 512 + ts]
                if b == 0:
                    S.op("dve", lambda e: e.tensor_tensor(out=dst, in0=bk.ap[:, 0:ts], in1=g.ap[:, 0:ts], op=ALU.mult),
                         reads=[bk.res, g.res], writes=[merged.res])
                else:
                    t = tmp.next()
                    S.op("dve", lambda e: e.tensor_tensor(out=t.ap[:, 0:ts], in0=bk.ap[:, 0:ts], in1=g.ap[:, 0:ts], op=ALU.mult),
                         reads=[bk.res, g.res], writes=[t.res])
                    S.op("dve", lambda e: e.tensor_tensor(out=dst, in0=dst, in1=t.ap[:, 0:ts], op=ALU.add),
                         reads=[merged.res, t.res], writes=[merged.res])
            return epi

        for b, (src, W) in enumerate(((self.YATT, self.w_att_out), (self.YSSD, self.w_ssd_out), (None, self.w_sc_out))):
            if src is not None:
                S.dma("sp", yT.ap, src[:, t0:t0 + T].rearrange("(c p) t -> p c t", p=128), writes=[yT.res])
            else:
                A.mark()
                cct = A.ring(2, [128, T + 2], BF16)
                cht = A.ring(2, [128, T + 2], BF16)
                cbt_ = A.ring(2, [128, T], BF16)
                ut = A.ring(2, [128, T + 2], F32)
                at = A.ring(2, [128, T], F32)
                nsq = T // sqL if kind == "p" and T > sqL else 1
                for cc in range(16):
                    cc_, ch_, cb_, u, a = cct.next(), cht.next(), cbt_.next(), ut.next(), at.next()
                    rows = slice(cc * 128, (cc + 1) * 128)
                    S.dma("sp", cb_.ap, self.SCB[rows, t0:t0 + T], writes=[cb_.res])
                    if kind == "p":
                        segs = [(s * 256, 256) for s in range(T // 256)]
                    else:
                        segs = [(0, T)]
                    S.op("dve", lambda e, cc_=cc_: e.memset(cc_.ap[:, 0:1], 0.0), writes=[cc_.res])
                    S.op("dve", lambda e, cc_=cc_: e.memset(cc_.ap[:, T + 1:T + 2], 0.0), writes=[cc_.res])
                    S.op("dve", lambda e, ch_=ch_: e.memset(ch_.ap[:, 0:1], 0.0), writes=[ch_.res])
                    S.op("dve", lambda e, ch_=ch_: e.memset(ch_.ap[:, T + 1:T + 2], 0.0), writes=[ch_.res])
                    lo = max(t0 - 1, sq0) if kind == "s" else t0
                    hi = min(t0 + T + 1, sq0 + sqL) if kind == "s" else t0 + T
                    S.dma("sp", cc_.ap[:, 1 + lo - t0:1 + hi - t0], self.SCC[rows, lo:hi], writes=[cc_.res])
                    S.dma("sp", ch_.ap[:, 1 + lo - t0:1 + hi - t0], self.SCH[rows, lo:hi], writes=[ch_.res])
                    S.op("dve", lambda e, u=u, cc_=cc_, ch_=ch_: e.tensor_tensor(out=u.ap, in0=cc_.ap, in1=ch_.ap, op=ALU.mult),
                         reads=[cc_.res, ch_.res], writes=[u.res])
                    for (o0, ln) in segs:
                        w0 = PC_SCW + cc * 3
                        first_lo = 1 if (kind == "p") else 0
                        S.op("dve", lambda e, a=a, u=u, o0=o0, ln=ln, w0=w0: e.tensor_scalar(
                            out=a.ap[:, o0:o0 + ln], in0=u.ap[:, o0 + 1:o0 + 1 + ln], scalar1=pc.ap[:, w0 + 1:w0 + 2], scalar2=None, op0=ALU.mult),
                            reads=[u.res, pc.res], writes=[a.res])
                        sk = 1 if kind == "p" else 0
                        S.op("dve", lambda e, a=a, u=u, o0=o0, ln=ln, w0=w0, sk=sk: e.scalar_tensor_tensor(
                            out=a.ap[:, o0 + sk:o0 + ln], in0=u.ap[:, o0 + sk:o0 + ln], scalar=pc.ap[:, w0:w0 + 1],
                            in1=a.ap[:, o0 + sk:o0 + ln], op0=ALU.mult, op1=ALU.add), reads=[u.res, pc.res, a.res], writes=[a.res])
                        S.op("dve", lambda e, a=a, u=u, o0=o0, ln=ln, w0=w0, sk=sk: e.scalar_tensor_tensor(
                            out=a.ap[:, o0:o0 + ln - sk], in0=u.ap[:, o0 + 2:o0 + 2 + ln - sk], scalar=pc.ap[:, w0 + 2:w0 + 3],
                            in1=a.ap[:, o0:o0 + ln - sk], op0=ALU.mult, op1=ALU.add), reads=[u.res, pc.res, a.res], writes=[a.res])
                    S.op("dve", lambda e, a=a, cb_=cb_, cc=cc: e.tensor_tensor(out=yT.ap[:, cc, :], in0=a.ap, in1=cb_.ap, op=ALU.mult),
                         reads=[a.res, cb_.res], writes=[yT.res])
                self.end_scope()
            self.gemm_fm(yT, 16, T, W[l], epi_merge(b))
        for cc in range(16):
            S.op("act", lambda e, cc=cc: e.activation(out=yT.ap[:, cc, :], in_=merged.ap[:, cc, :], func=AF.Copy),
                 reads=[merged.res], writes=[yT.res])

        def epi_o(blk, bw, m, bk):
            S.op("act", lambda e: e.activation(out=oacc.ap[:, m, blk:blk + bw], in_=bk.ap[:, 0:bw], func=AF.Copy), reads=[bk.res], writes=[oacc.res])

        self.gemm_tm(yT, 16, T, self.w_o[l], epi_o)
        A.mark()
        xt = A.ring(2, [128, D], F32)
        xn = A.ring(2, [128, D], BF16)
        junk = A.tl([128, D], BF16)
        tmpn = A.ring(2, [128, 8, 128], F32)
        st = A.ring(4, [128, 4], F32)
        xsrc = self.xin(l, t0, T)
        h2T = yT
        for m in range(nm):
            x, s = xt.next(), st.next()
            S.dma("sp", x.ap, xsrc[m * 128:(m + 1) * 128, :], writes=[x.res])
            o = Tl(oacc.ap[:, m, :], oacc.res)
            self.ssq(o, s, junk)
            self.rstd_from_ssq(s)
            S.op("dve", lambda e, o=o, s=s: e.scalar_tensor_tensor(out=o.ap, in0=o.ap, scalar=s.ap[:, 3:4], in1=Gm.ap, op0=ALU.mult, op1=ALU.mult),
                 reads=[oacc.res, s.res, Gm.res], writes=[oacc.res])
            S.op("dve", lambda e, o=o, x=x: e.tensor_tensor(out=x.ap, in0=o.ap, in1=x.ap, op=ALU.add), reads=[oacc.res, x.res], writes=[x.res])
            S.dma("act", self.XM[t0 + m * 128:t0 + (m + 1) * 128, :], x.ap, reads=[x.res])
            self.norm_T(x, h2T, m, Acol, Bcol, st.next(), junk, xn.next(), tmpn.next())
        A.release()
        self.end_scope()
        actT = A.tl([128, 44, T], BF16)
        sg = A.ring(2, [128, 512], F32)
        Wgu = self.w_gate_up[l]
        for fb4 in range(0, 44, 4):
            nf = min(4, 44 - fb4)
            wg, wgr = self.wload(Wgu[:, fb4 * 128:(fb4 + nf) * 128], 16, nf * 128)
            wu, wur = self.wload(Wgu[:, DFF + fb4 * 128:DFF + (fb4 + nf) * 128], 16, nf * 128)
            for f in range(nf):
                for tb in range((T + 511) // 512):
                    ts = min(512, T - tb * 512)
                    bg, bu = self.bank(), self.bank()
                    for (bk, wv, wr) in ((bg, wg, wgr), (bu, wu, wur)):
                        for kc in range(16):
                            S.op("pe", lambda e, bk=bk, wv=wv, kc=kc, f=f, tb=tb, ts=ts: e.matmul(
                                bk.ap[:, 0:ts], lhsT=wv[:, kc, f * 128:(f + 1) * 128], rhs=h2T.ap[:, kc, tb * 512:tb * 512 + ts],
                                start=(kc == 0), stop=(kc == 15)), reads=[wr, h2T.res], writes=[bk.res])
                    sgt = sg.next()
                    S.op("act", lambda e, sgt=sgt, bg=bg, ts=ts: e.activation(out=sgt.ap[:, 0:ts], in_=bg.ap[:, 0:ts], func=AF.Silu), reads=[bg.res], writes=[sgt.res])
                    S.op("dve", lambda e, sgt=sgt, bu=bu, ts=ts, fb=fb4 + f, tb=tb: e.tensor_tensor(
                        out=actT.ap[:, fb, tb * 512:tb * 512 + ts], in0=bu.ap[:, 0:ts], in1=sgt.ap[:, 0:ts], op=ALU.mult),
                        reads=[bu.res, sgt.res], writes=[actT.res])
        o2T = A.ring(2, [128, 512], F32)

        def wload_down(fb):
            wt = self.wbufs.next()
            v = wt.ap[:, 0:44 * 128].rearrange("p (c n) -> p c n", c=44)
            S.dma("pool", v, self.w_down[l][:, fb * 128:(fb + 1) * 128].rearrange("(c p) n -> p c n", p=128), writes=[wt.res])
            return v, wt.res

        for fb in range(16):
            wv, wr = wload_down(fb)
            for tb in range((T + 511) // 512):
                ts = min(512, T - tb * 512)
                bk = self.bank()
                for kc in range(44):
                    S.op("pe", lambda e, bk=bk, wv=wv, kc=kc, tb=tb, ts=ts: e.matmul(
                        bk.ap[:, 0:ts], lhsT=wv[:, kc, :], rhs=actT.ap[:, kc, tb * 512:tb * 512 + ts], start=(kc == 0), stop=(kc == 43)),
                        reads=[wr, actT.res], writes=[bk.res])
                ot = o2T.next()
                S.op("act", lambda e, ot=ot, bk=bk, ts=ts: e.activation(out=ot.ap[:, 0:ts], in_=bk.ap[:, 0:ts], func=AF.Copy), reads=[bk.res], writes=[ot.res])
                b2 = self.bank()
                for s_ in range(ts // 128):
                    S.op("pe", lambda e, b2=b2, ot=ot, s_=s_: e.transpose(out=b2.ap[:, s_ * 128:(s_ + 1) * 128], in_=ot.ap[:, s_ * 128:(s_ + 1) * 128], identity=idf),
                         reads=[ot.res, self.cf.res], writes=[b2.res])
                m0 = tb * 4
                S.op("dve", lambda e, b2=b2, ts=ts, m0=m0, fb=fb: e.tensor_scalar(
                    out=oacc.ap[:, m0:m0 + ts // 128, fb * 128:(fb + 1) * 128], in0=b2.ap[:, 0:ts].rearrange("p (s d) -> p s d", s=ts // 128),
                    scalar1=1.0, scalar2=None, op0=ALU.mult),
                    reads=[b2.res], writes=[oacc.res])
        A.mark()
        xt = A.ring(2, [128, D], F32)
        junk = A.tl([128, D], BF16)
        st = A.ring(2, [128, 4], F32)
        xdst = self.xout(l, t0, T)
        for m in range(nm):
            x, s = xt.next(), st.next()
            S.dma("sp", x.ap, self.XM[t0 + m * 128:t0 + (m + 1) * 128, :], writes=[x.res])
            o = Tl(oacc.ap[:, m, :], oacc.res)
            self.ssq(o, s, junk)
            self.rstd_from_ssq(s)
            S.op("dve", lambda e, o=o, s=s: e.scalar_tensor_tensor(out=o.ap, in0=o.ap, scalar=s.ap[:, 3:4], in1=Gf.ap, op0=ALU.mult, op1=ALU.mult),
                 reads=[oacc.res, s.res, Gf.res], writes=[oacc.res])
            S.op("dve", lambda e, o=o, x=x: e.tensor_tensor(out=x.ap, in0=o.ap, in1=x.ap, op=ALU.add), reads=[oacc.res, x.res], writes=[x.res])
            S.dma("act", xdst[m * 128:(m + 1) * 128, :], x.ap, reads=[x.res])
        A.release()
        A.release()


def make_consts(LS):
    t = np.arange(128)
    cst = np.zeros((C_N, 128, 128), np.float32)
    cst[C_ID] = np.eye(128)
    cst[C_UF] = (t[:, None] <= t[None, :])
    cst[C_UB] = (t[:, None] >= t[None, :])
    cst[C_NUF] = -cst[C_UF]
    cst[C_NUB] = -cst[C_UB]
    cst[C_ONES] = 1.0
    cst[C_MBF] = np.where(t[None, :] >= t[:, None], 0.0, NEG)
    cst[C_MBB] = np.where(t[None, :] <= t[:, None], 0.0, NEG)
    cst[C_MPREV] = (t[:, None] >= t[None, :])
    cst[C_MNEXT] = (t[:, None] <= t[None, :])
    pm = np.zeros((128, 128), np.float32)
    for i in range(64):
        pm[2 * i + 1, 2 * i] = 1.0
        pm[2 * i, 2 * i + 1] = 1.0
    cst[C_PMAT] = pm
    cst = np.ascontiguousarray(cst.transpose(1, 0, 2).reshape(128, C_N * 128))
    pos = np.arange(LS)
    row = (pos // GRID_W).astype(np.float32)
    col = (pos % GRID_W).astype(np.float32)
    n_pairs = HD // 4
    inv = (10000.0 ** (-np.arange(n_pairs, dtype=np.float32) / n_pairs)).astype(np.float32)
    ang = np.concatenate([row[:, None] * inv, col[:, None] * inv], axis=-1).astype(np.float32)
    cos = np.cos(ang).astype(np.float32)
    sin = np.sin(ang).astype(np.float32)
    ropec = np.zeros((128, LS), np.float32)
    ropes = np.zeros((128, LS), np.float32)
    ropec[0::2] = cos.T
    ropec[1::2] = cos.T
    ropes[0::2] = -sin.T
    ropes[1::2] = sin.T
    return cst, ropec, ropes


def col16(v):
    return np.ascontiguousarray(v.reshape(-1, 128).T)


def make_pcol(inp, L):
    pcol = np.zeros((L, 128, PC_N), np.float32)
    for l in range(L):
        pcol[l, :, PC_GPM:PC_GPM + 16] = col16(inp["g_pre_mix"][l])
        pcol[l, :, PC_GPF:PC_GPF + 16] = col16(inp["g_pre_ffn"][l])
        cw = inp["ssm_conv_w"][l]
        for k in range(5):
            pcol[l, :, PC_CW + k:PC_CW + 120:5] = col16(cw[k])
        pcol[l, :, PC_CB:PC_CB + 24] = col16(inp["ssm_conv_b"][l])
        sw = inp["sc_conv_w"][l]
        for k in range(3):
            pcol[l, :, PC_SCW + k:PC_SCW + 48:3] = col16(sw[k])
        pcol[l, :, PC_BMOD:PC_BMOD + 96] = col16(inp["b_mod"][l])
    return pcol


_NC_CACHE = {}
SIM_HOOK = None


def get_nc(cfg_key):
    if cfg_key not in _NC_CACHE:
        _NC_CACHE[cfg_key] = KB(Cfg(*cfg_key))
    return _NC_CACHE[cfg_key]


def run(inp, NPS, LS, DEPTH, TT, TT5, n_cores, debug=False, stop=99):
    f = lambda a: np.ascontiguousarray(np.asarray(a, dtype=np.float32))
    inp = {k: f(v) for k, v in inp.items()}
    kb = KB(Cfg(NPS, LS, DEPTH, TT, TT5, debug, stop))
    cst, ropec, ropes = make_consts(LS)
    pcol = make_pcol(inp, DEPTH)
    nb = inp["x_sample"].shape[0]
    shared = {
        "pcol": pcol, "cst": cst, "ropec": ropec, "ropes": ropes,
        "w_mod": inp["w_mod"], "b_mod": inp["b_mod"], "w_in": inp["w_in"], "sink": inp["sink"],
        "ssm_dt_bias": inp["ssm_dt_bias"].reshape(DEPTH, 64), "ssm_a_log": inp["ssm_a_log"].reshape(DEPTH, 64),
        "ssm_d": inp["ssm_d"], "ssm_norm_g": inp["ssm_norm_g"], "w_att_out": inp["w_att_out"],
        "w_ssd_out": inp["w_ssd_out"], "w_sc_out": inp["w_sc_out"], "w_o": inp["w_o"],
        "g_post_mix": inp["g_post_mix"], "g_post_ffn": inp["g_post_ffn"], "w_gate_up": inp["w_gate_up"],
        "w_down": inp["w_down"],
    }
    in_maps = []
    for i in range(n_cores):
        b = i % nb
        cond = np.stack([inp["c_ctx"], inp["c"][b]], axis=0)
        condT = np.ascontiguousarray(cond.reshape(2, 16, 128).transpose(2, 1, 0))
        m = dict(shared)
        m["xp"] = np.ascontiguousarray(inp["x_prompt"][i * NPS:(i + 1) * NPS].reshape(NPS * 256, D))
        m["xs"] = inp["x_sample"][b]
        m["ck"] = np.ascontiguousarray(inp["cache_k"][b].reshape(DEPTH, 256, 512))
        m["cv"] = np.ascontiguousarray(inp["cache_v"][b].reshape(DEPTH, 256, 512))
        m["st"] = np.ascontiguousarray(inp["state_ssm"][b].reshape(DEPTH, 2, SH * SP_, SN))
        m["condT"] = condT
        in_maps.append(m)
    if SIM_HOOK is not None:
        R = SIM_HOOK(kb.nc, in_maps)
    else:
        res = run_bass_kernel_spmd(kb.nc, in_maps, core_ids=list(range(n_cores)))
        R = res.results
    y_prompt = np.concatenate([R[i]["yp"].reshape(NPS, 256, D) for i in range(n_cores)], axis=0)
    y_sample = np.stack([R[b]["ys"] for b in range(nb)], axis=0)
    nck = np.concatenate([R[i]["ock"].reshape(NPS, DEPTH, 256, NKV, HD) for i in range(n_cores)], axis=0)
    ncv = np.concatenate([R[i]["ocv"].reshape(NPS, DEPTH, 256, NKV, HD) for i in range(n_cores)], axis=0)
    nst = np.concatenate([R[i]["ost"].reshape(NPS, DEPTH, 2, SH, SP_, SN) for i in range(n_cores)], axis=0)
    outs = (y_prompt, y_sample, nck, ncv, nst)
    if debug:
        return outs, R
    return outs


def kernel(**inputs):
    return run(inputs, NPS=4, LS=4096, DEPTH=2, TT=1024, TT5=512, n_cores=8)
```

```python
import math
import numpy as np
import concourse.bass as bass
import concourse.mybir as mybir
from concourse.bass_utils import run_bass_kernel_spmd

F32 = mybir.dt.float32
BF16 = mybir.dt.bfloat16
AF = mybir.ActivationFunctionType
ALU = mybir.AluOpType

D = 2048
NH, NKV, HD = 16, 4, 128
SH, SP_, SN, SG = 32, 64, 128, 4
DFF = 5632
DIN = 20544
O_Q, O_K, O_V, O_Z, O_X, O_DT, O_SCB, O_SCC, O_SCH, O_G = 0, 2048, 2560, 3072, 5120, 8192, 8256, 10304, 12352, 14400
EPS = 1e-6
ATT_SCALE = HD ** -0.5
GRID_W = 64
NEG = -30000.0
import os as _os
POOL_ENG = _os.environ.get('POOL_ENG', 'pool')

PC_GPM, PC_GPF, PC_CW, PC_CB, PC_SCW, PC_BMOD, PC_N = 0, 16, 32, 152, 176, 224, 320
C_ID, C_UF, C_UB, C_NUF, C_NUB, C_ONES, C_MBF, C_MBB, C_MPREV, C_MNEXT, C_PMAT, C_N = range(12)


class Tok:
    __slots__ = ("key", "sem", "val", "eng")

    def __init__(self, key, sem, val, eng):
        self.key, self.sem, self.val, self.eng = key, sem, val, eng


class Res:
    __slots__ = ("w", "r")

    def __init__(self):
        self.w = None
        self.r = {}


class Tl:
    __slots__ = ("ap", "res")

    def __init__(self, ap, res=None):
        self.ap = ap
        self.res = res if res is not None else Res()


class Ring:
    def __init__(self, items):
        self.items = items
        self.i = 0

    def next(self):
        t = self.items[self.i % len(self.items)]
        self.i += 1
        return t


class _Rec:
    def __getattr__(self, name):
        def f(*a, **k):
            return (name, a, k)
        return f


_REC = _Rec()


class EngState:
    def __init__(self, name, sem, skip_self):
        self.name, self.sem, self.skip_self = name, sem, skip_self
        self.count = 0
        self.waited = {}
        self.stream = []
        self.dma_sems = []
        self.dma_n = 0


class Sched:
    ENGS = ("pe", "act", "dve", "pool", "sp")

    def __init__(self, nc, n_dma_slots=8):
        self.nc = nc
        self.ctx = []
        self.E = {}
        for name in self.ENGS:
            cm = nc.semaphore("sem_" + name)
            self.ctx.append(cm)
            self.E[name] = EngState(name, cm.__enter__(), skip_self=(name == "pe"))
        for name in ("sp", "pool", "act"):
            for i in range(n_dma_slots):
                cm = nc.semaphore(f"dsem_{name}_{i}")
                self.ctx.append(cm)
                self.E[name].dma_sems.append(cm.__enter__())
        self.nslot = n_dma_slots

    def _deps(self, E, reads, writes):
        best = {}

        def add(t):
            if t is not None and best.get(t.key, (0, None))[0] < t.val:
                best[t.key] = (t.val, t)

        for r in reads:
            add(r.w)
        for w in writes:
            add(w.w)
            for t in w.r.values():
                add(t)
        waits = []
        for key, (val, t) in best.items():
            if t.eng is E and E.skip_self:
                continue
            if E.waited.get(key, 0) >= val:
                continue
            E.waited[key] = val
            waits.append((t.sem, val))
        return waits

    @staticmethod
    def _update(tok, reads, writes):
        for r in reads:
            old = r.r.get(tok.key)
            if old is None or old.val < tok.val:
                r.r[tok.key] = tok
        for w in writes:
            w.w = tok
            w.r = {}

    def op(self, eng, fn, reads=(), writes=()):
        E = self.E[eng]
        waits = self._deps(E, reads, writes)
        E.count += 1
        tok = Tok(("e", eng), E.sem, E.count, E)
        name, a, k = fn(_REC)
        E.stream.append((waits, lambda e, name=name, a=a, k=k: getattr(e, name)(*a, **k), (E.sem, 1)))
        self._update(tok, reads, writes)
        return tok

    def dma(self, q, out, in_, reads=(), writes=(), **kw):
        E = self.E[q]
        waits = self._deps(E, reads, writes)
        slot = E.dma_n % self.nslot
        gen = E.dma_n // self.nslot
        E.dma_n += 1
        sem = E.dma_sems[slot]
        key = ("d", q, slot)
        if gen > 0 and E.waited.get(key, 0) < 16 * gen:
            waits.append((sem, 16 * gen))
            E.waited[key] = 16 * gen
        tok = Tok(key, sem, 16 * (gen + 1), None)

        def fn(e, out=out, in_=in_, kw=kw):
            return e.dma_start(out=out, in_=in_, **kw)

        E.stream.append((waits, fn, (sem, 16)))
        self._update(tok, reads, writes)
        return tok

    def barrier(self):
        toks = []
        for name, E in self.E.items():
            if E.count > 0:
                toks.append(Tok(("e", name), E.sem, E.count, E))
            for slot in range(min(E.dma_n, self.nslot)):
                n_on = (E.dma_n - 1 - slot) // self.nslot + 1
                toks.append(Tok(("d", name, slot), E.dma_sems[slot], 16 * n_on, None))
        for name, E in self.E.items():
            for t in toks:
                if t.eng is E:
                    continue
                if E.waited.get(t.key, 0) >= t.val:
                    continue
                E.waited[t.key] = t.val
                E.stream.append(([(t.sem, t.val)], None, None))

    def emit(self):
        def run(eng, E):
            for waits, fn, inc in E.stream:
                for sem, val in waits:
                    eng.wait_ge(sem, val)
                if fn is not None:
                    fn(eng).then_inc(inc[0], inc[1])

        S = self
        with self.nc.Block() as block:
            @block.tensor
            def _(e):
                run(e, S.E["pe"])

            @block.scalar
            def _(e):
                run(e, S.E["act"])

            @block.vector
            def _(e):
                run(e, S.E["dve"])

            @block.gpsimd
            def _(e):
                run(e, S.E["pool"])

            @block.sync
            def _(e):
                run(e, S.E["sp"])


class Arena:
    def __init__(self, nc, nbytes):
        self.words = nbytes // 4
        self.t = nc.alloc_sbuf_tensor("arena", [128, self.words], F32)
        self.off = 0
        self.marks = []

    def alloc(self, shape, dtype, P=128):
        esz = 2 if dtype == BF16 else 4
        n = int(np.prod(shape[1:]))
        nwords = ((n * esz + 3) // 4 + 15) // 16 * 16
        off = self.off
        assert off + nwords <= self.words, f"SBUF overflow {off}+{nwords}>{self.words} {shape}"
        self.off += nwords
        a = self.t[0:shape[0], off:off + nwords]
        if dtype != F32:
            a = a.bitcast(dtype)
        a = a[:, 0:n]
        if len(shape) == 3:
            a = a.rearrange("p (a b) -> p a b", a=shape[1])
        elif len(shape) == 4:
            a = a.rearrange("p (a b c) -> p a b c", a=shape[1], b=shape[2])
        return a

    def tl(self, shape, dtype):
        return Tl(self.alloc(shape, dtype))

    def ring(self, n, shape, dtype):
        return Ring([self.tl(shape, dtype) for _ in range(n)])

    def mark(self):
        self.marks.append(self.off)

    def release(self):
        self.off = self.marks.pop()


def bc_last(a, n):
    return bass.AP(a.tensor, a.offset, [list(x) for x in a.ap] + [[0, n]])


def bc_mid(a, n):
    ap = [list(x) for x in a.ap]
    return bass.AP(a.tensor, a.offset, [ap[0], [0, n]] + ap[1:])


def dram_bc(a, P=128):
    ap = [list(x) for x in a.ap]
    return bass.AP(a.tensor, a.offset, [[0, P], ap[-1]])


class Cfg:
    def __init__(self, NPS=4, LS=4096, DEPTH=2, TT=1024, TT5=512, debug=False, stop=99):
        self.NPS, self.LS, self.DEPTH, self.TT, self.TT5, self.debug = NPS, LS, DEPTH, TT, TT5, debug
        self.stop = stop
        self.LP = 256
        self.NP = NPS * 256
        self.NT = self.NP + LS


class KB:
    def __init__(self, cfg):
        self.cfg = cfg
        c = cfg
        nc = self.nc = bass.Bass("TRN2", target_bir_lowering=False)
        L = c.DEPTH

        def inp(name, shape):
            return nc.dram_tensor(name, list(shape), F32, kind="ExternalInput").ap()

        def outp(name, shape):
            return nc.dram_tensor(name, list(shape), F32, kind="ExternalOutput").ap()

        self.dbg = {}

        def scr(name, shape, dt=BF16):
            if c.debug:
                a = nc.dram_tensor(name, list(shape), dt, kind="ExternalOutput").ap()
                self.dbg[name] = a
                return a
            return nc.dram_tensor(name, list(shape), dt).ap()

        self.xp = inp("xp", [c.NP, D])
        self.xs = inp("xs", [c.LS, D])
        self.ck = inp("ck", [L, 256, 512])
        self.cv = inp("cv", [L, 256, 512])
        self.st = inp("st", [L, 2, SH * SP_, SN])
        self.condT = inp("condT", [128, 16, 2])
        self.pcol = inp("pcol", [L, 128, PC_N])
        self.cst = inp("cst", [128, C_N * 128])
        self.ropec = inp("ropec", [128, c.LS])
        self.ropes = inp("ropes", [128, c.LS])
        self.w_mod = inp("w_mod", [L, D, 6 * D])
        self.b_mod = inp("b_mod", [L, 6 * D])
        self.w_in = inp("w_in", [L, D, DIN])
        self.sink = inp("sink", [L, NH])
        self.dt_bias = inp("ssm_dt_bias", [L, 64])
        self.a_log = inp("ssm_a_log", [L, 64])
        self.ssm_d = inp("ssm_d", [L, SH])
        self.ssm_norm_g = inp("ssm_norm_g", [L, D])
        self.w_att_out = inp("w_att_out", [L, D, D])
        self.w_ssd_out = inp("w_ssd_out", [L, D, D])
        self.w_sc_out = inp("w_sc_out", [L, D, D])
        self.w_o = inp("w_o", [L, D, D])
        self.g_post_mix = inp("g_post_mix", [L, D])
        self.g_post_ffn = inp("g_post_ffn", [L, D])
        self.w_gate_up = inp("w_gate_up", [L, D, 2 * DFF])
        self.w_down = inp("w_down", [L, DFF, D])

        self.yp = outp("yp", [c.NP, D])
        self.ys = outp("ys", [c.LS, D])
        self.ock = outp("ock", [c.NPS, L, 256, 512])
        self.ocv = outp("ocv", [c.NPS, L, 256, 512])
        self.ost = outp("ost", [c.NPS, L, 2, SH * SP_, SN])

        NT = c.NT
        self.X1 = scr("X1", [NT, D], F32)
        self.XM = scr("XM", [NT, D], F32)
        self.MOD = scr("MOD", [L, 2, 6 * D], F32)
        self.QT = scr("QT", [D, NT])
        self.KT = scr("KT", [512, NT])
        self.V = scr("V", [NT, 512])
        self.SZ = scr("SZ", [NT, D])
        self.XBC = scr("XBC", [3072, NT])
        self.XCT = scr("XCT", [1024, NT])
        self.XTOK = scr("XTOK", [NT, 2560])
        self.DT = scr("DT", [NT, 64], F32)
        self.SCB = scr("SCB", [D, NT])
        self.SCC = scr("SCC", [D, NT])
        self.SCH = scr("SCH", [D, NT])
        self.G = scr("G", [3 * D, NT])
        self.YATT = scr("YATT", [D, NT])
        self.YSSD = scr("YSSD", [D, NT])
        self.YF = scr("YF", [NT, D], F32)
        self.WC = nc.dram_tensor("WC", [64, 128, 8192], BF16).ap()

        self.S = Sched(nc)
        self.A = Arena(nc, 204 * 1024)
        self.banks = Ring([Tl(nc.alloc_psum_tensor(f"ps{i}", [128, 512], F32)[:]) for i in range(8)])
        self.build()

    def bank(self):
        return self.banks.next()

    def end_scope(self):
        self.A.release()
        self.S.barrier()

    def xin(self, l, t0, T):
        c = self.cfg
        if l == 0:
            return self.xp[t0:t0 + T, :] if t0 < c.NP else self.xs[t0 - c.NP:t0 - c.NP + T, :]
        return self.X1[t0:t0 + T, :]

    def xout(self, l, t0, T):
        c = self.cfg
        if l == c.DEPTH - 1:
            return self.yp[t0:t0 + T, :] if t0 < c.NP else self.ys[t0 - c.NP:t0 - c.NP + T, :]
        return self.X1[t0:t0 + T, :]

    def cblk(self, i, dt=F32):
        a = (self.cf if dt == F32 else self.cb)
        return a.ap[:, i * 128:(i + 1) * 128]

    def wload(self, w_ap, KC, bw, cache=False):
        wt = self.wbufs.next()
        n = KC * bw
        v = wt.ap[:, 0:n].rearrange("p (c n) -> p c n", c=KC)
        if not cache:
            self.S.dma("pool", v, w_ap.rearrange("(c p) n -> p c n", p=128), writes=[wt.res])
            return v, wt.res
        slot = self.wc_n
        self.wc_n += 1
        if self.wc_first:
            self.S.dma("pool", v, w_ap.rearrange("(c p) n -> p c n", p=128), writes=[wt.res])
            self.S.dma("sp", self.WC[slot, :, 0:n], wt.ap[:, 0:n], reads=[wt.res])
        else:
            self.S.dma("pool", wt.ap[:, 0:n], self.WC[slot, :, 0:n], writes=[wt.res])
        return v, wt.res

    def gemm_fm(self, actT, KC, T, w_ap, epi, blkw=512, cache=False):
        S = self.S
        ncols = w_ap.shape[1]
        for blk in range(0, ncols, blkw):
            bw = min(blkw, ncols - blk)
            wv, wres = self.wload(w_ap[:, blk:blk + bw], KC, bw, cache)
            for f in range(bw // 128):
                for tb in range((T + 511) // 512):
                    ts = min(512, T - tb * 512)
                    bk = self.bank()
                    for kc in range(KC):
                        S.op("pe", lambda e, bk=bk, wv=wv, kc=kc, f=f, tb=tb, ts=ts: e.matmul(
                            bk.ap[:, 0:ts], lhsT=wv[:, kc, f * 128:(f + 1) * 128],
                            rhs=actT.ap[:, kc, tb * 512:tb * 512 + ts], start=(kc == 0), stop=(kc == KC - 1)),
                            reads=[wres, actT.res], writes=[bk.res])
                    epi(blk // 128 + f, tb, ts, bk)

    def gemm_tm(self, actT, KC, T, w_ap, epi, blkw=512, cache=False):
        S = self.S
        ncols = w_ap.shape[1]
        for blk in range(0, ncols, blkw):
            bw = min(blkw, ncols - blk)
            wv, wres = self.wload(w_ap[:, blk:blk + bw], KC, bw, cache)
            for m in range(T // 128):
                bk = self.bank()
                for kc in range(KC):
                    S.op("pe", lambda e, bk=bk, wv=wv, kc=kc, m=m, bw=bw: e.matmul(
                        bk.ap[:, 0:bw], lhsT=actT.ap[:, kc, m * 128:(m + 1) * 128],
                        rhs=wv[:, kc, 0:bw], start=(kc == 0), stop=(kc == KC - 1)),
                        reads=[wres, actT.res], writes=[bk.res])
                epi(blk, bw, m, bk)

    def rstd_from_ssq(self, s):
        S = self.S
        S.op("dve", lambda e: e.tensor_scalar(out=s.ap[:, 1:2], in0=s.ap[:, 0:1], scalar1=1.0 / D, scalar2=EPS,
                                              op0=ALU.mult, op1=ALU.add), reads=[s.res], writes=[s.res])
        S.op("act", lambda e: e.activation(out=s.ap[:, 2:3], in_=s.ap[:, 1:2], func=AF.Sqrt), reads=[s.res], writes=[s.res])
        S.op("dve", lambda e: e.reciprocal(out=s.ap[:, 3:4], in_=s.ap[:, 2:3]), reads=[s.res], writes=[s.res])

    def ssq(self, x, s, junk):
        S = self.S
        S.op("dve", lambda e: e.memset(s.ap[:, 0:1], 0.0), writes=[s.res])
        S.op("act", lambda e: e.activation(out=junk.ap, in_=x.ap, func=AF.Square, accum_out=s.ap[:, 0:1]),
             reads=[x.res], writes=[junk.res, s.res])

    def norm_T(self, x, hT, m, Acol, Bcol, s, junk, xn, tmp):
        S = self.S
        self.ssq(x, s, junk)
        self.rstd_from_ssq(s)
        S.op("act", lambda e: e.activation(out=xn.ap, in_=x.ap, func=AF.Identity, scale=s.ap[:, 3:4]),
             reads=[x.res, s.res], writes=[xn.res])
        idb = self.cblk(C_ID, BF16)
        for j in range(2):
            bk = self.bank()
            pb = bk.ap.bitcast(BF16)
            for kk in range(8):
                kc = j * 8 + kk
                S.op("pe", lambda e, pb=pb, kk=kk, kc=kc: e.transpose(
                    out=pb[:, kk * 128:(kk + 1) * 128], in_=xn.ap[:, kc * 128:(kc + 1) * 128], identity=idb),
                    reads=[xn.res, self.cb.res], writes=[bk.res])
            pv = pb.rearrange("p (a b) -> p a b", a=8)
            S.op("dve", lambda e, pv=pv, j=j: e.tensor_tensor(
                out=tmp.ap, in0=pv, in1=bc_last(Acol.ap[:, j * 8:(j + 1) * 8], 128), op=ALU.mult),
                reads=[bk.res, Acol.res], writes=[tmp.res])
            S.op("dve", lambda e, j=j: e.tensor_tensor(
                out=hT.ap[:, j * 8:(j + 1) * 8, m * 128:(m + 1) * 128], in0=tmp.ap,
                in1=bc_last(Bcol.ap[:, j * 8:(j + 1) * 8], 128), op=ALU.add),
                reads=[tmp.res, Bcol.res], writes=[hT.res])

    def modcols(self, l, r, i_shift, i_scale, gcol0):
        S, A = self.S, self.A
        Acol, Bcol = A.tl([128, 16], F32), A.tl([128, 16], F32)
        mc = self.modcol
        S.op("dve", lambda e: e.tensor_scalar(out=Acol.ap, in0=mc.ap[:, i_scale * 16:(i_scale + 1) * 16, r], scalar1=1.0,
                                              scalar2=None, op0=ALU.add), reads=[mc.res], writes=[Acol.res])
        S.op("dve", lambda e: e.tensor_tensor(out=Acol.ap, in0=Acol.ap, in1=self.pc.ap[:, gcol0:gcol0 + 16], op=ALU.mult),
             reads=[Acol.res, self.pc.res], writes=[Acol.res])
        S.op("dve", lambda e: e.tensor_copy(out=Bcol.ap, in_=mc.ap[:, i_shift * 16:(i_shift + 1) * 16, r]),
             reads=[mc.res], writes=[Bcol.res])
        return Acol, Bcol

    def gate_row(self, l, r, i_gate, g_ap, Gt, t2):
        S, A = self.S, self.A
        S.dma("sp", Gt.ap, dram_bc(self.MOD[l, r, i_gate * D:(i_gate + 1) * D]), writes=[Gt.res])
        S.dma("sp", t2.ap, dram_bc(g_ap[l, :]), writes=[t2.res])
        S.op("dve", lambda e: e.tensor_tensor(out=Gt.ap, in0=Gt.ap, in1=t2.ap, op=ALU.mult), reads=[Gt.res, t2.res], writes=[Gt.res])
        return Gt

    def build(self):
        c, S, A = self.cfg, self.S, self.A
        self.cf = A.tl([128, C_N * 128], F32)
        self.cb = A.tl([128, C_N * 128], BF16)
        S.dma("sp", self.cf.ap, self.cst, writes=[self.cf.res])
        S.dma("pool", self.cb.ap, self.cst, writes=[self.cb.res])
        self.wbufs = None
        self.wc_n = 0
        self.wc_first = False
        self.pc = A.tl([128, PC_N], F32)
        self.modcol = A.tl([128, 96, 2], F32)
        self.tiles = []
        if c.NP > 0:
            for t0 in range(0, c.NP, c.TT):
                self.tiles.append((t0, min(c.TT, c.NP - t0), 0, "p"))
        for t0 in range(0, c.LS, c.TT):
            self.tiles.append((c.NP + t0, min(c.TT, c.LS - t0), 1, "s"))
        self.seqs = [(i * 256, 256, "p", i) for i in range(c.NPS)] + [(c.NP, c.LS, "s", 0)]
        for l in range(c.DEPTH):
            self.phase0(l)
            S.barrier()
            if c.stop < 1:
                break
            for tile in self.tiles:
                self.phase1(l, tile)
                S.barrier()
            if c.stop < 2:
                break
            self.attention(l)
            if c.stop < 3:
                break
            self.ssd_conv(l)
            S.barrier()
            if c.stop < 4:
                break
            self.ssd_sweep(l, 0)
            S.barrier()
            self.ssd_sweep(l, 1)
            S.barrier()
            if c.stop < 5:
                break
            self.wc_first = True
            for (t0, T, r, kind) in self.tiles:
                for u0 in range(0, T, c.TT5):
                    self.phase56(l, (t0 + u0, min(c.TT5, T - u0), r, kind))
                    S.barrier()
                    self.wc_first = False
        S.barrier()
        S.emit()

    def phase0(self, l):
        S, A = self.S, self.A
        A.mark()
        self.wbufs = A.ring(3, [128, 16 * 512], BF16)
        S.dma("sp", self.pc.ap, self.pcol[l], writes=[self.pc.res])
        ct = A.tl([128, 16, 2], F32)
        sc = A.tl([128, 16, 2], BF16)
        S.dma("sp", ct.ap, self.condT, writes=[ct.res])
        S.op("act", lambda e: e.activation(out=sc.ap, in_=ct.ap, func=AF.Silu), reads=[ct.res], writes=[sc.res])
        brow = A.ring(2, [2, 512], F32)
        orow = A.ring(2, [2, 512], F32)
        mc = self.modcol
        for n in range(24):
            wv, wres = self.wload(self.w_mod[l][:, n * 512:(n + 1) * 512], 16, 512)
            bk = self.bank()
            for kc in range(16):
                S.op("pe", lambda e, bk=bk, wv=wv, kc=kc: e.matmul(bk.ap[0:2, :], lhsT=sc.ap[:, kc, :], rhs=wv[:, kc, :],
                                                                 start=(kc == 0), stop=(kc == 15)),
                     reads=[wres, sc.res], writes=[bk.res])
            br = brow.next()
            S.dma("sp", br.ap, dram_bc(self.b_mod[l, n * 512:(n + 1) * 512], 2), writes=[br.res])
            orr = orow.next()
            S.op("dve", lambda e, bk=bk, br=br, orr=orr: e.tensor_tensor(out=orr.ap, in0=bk.ap[0:2, :], in1=br.ap, op=ALU.add),
                 reads=[bk.res, br.res], writes=[orr.res])
            S.dma("act", self.MOD[l, :, n * 512:(n + 1) * 512], orr.ap, reads=[orr.res])
            bk2 = self.bank()
            for f in range(4):
                for kc in range(16):
                    S.op("pe", lambda e, bk2=bk2, wv=wv, kc=kc, f=f: e.matmul(
                        bk2.ap[:, f * 2:f * 2 + 2], lhsT=wv[:, kc, f * 128:(f + 1) * 128], rhs=sc.ap[:, kc, :],
                        start=(kc == 0 and f == 0), stop=(kc == 15 and f == 3)), reads=[wres, sc.res], writes=[bk2.res])
            S.op("dve", lambda e, bk2=bk2, n=n: e.tensor_tensor(
                out=mc.ap[:, n * 4:(n + 1) * 4, :], in0=bk2.ap[:, 0:8].rearrange("p (a b) -> p a b", a=4),
                in1=bc_last(self.pc.ap[:, PC_BMOD + n * 4:PC_BMOD + (n + 1) * 4], 2), op=ALU.add),
                reads=[bk2.res, self.pc.res], writes=[mc.res])
        A.release()

    def phase1(self, l, tile):
        c, S, A = self.cfg, self.S, self.A
        t0, T, r, kind = tile
        A.mark()
        self.wbufs = A.ring(3, [128, 16 * 512], BF16)
        hT = A.tl([128, 16, T], BF16)
        Acol, Bcol = self.modcols(l, r, 0, 1, PC_GPM)
        A.mark()
        xt = A.ring(2, [128, D], F32)
        xn = A.ring(2, [128, D], BF16)
        junk = A.tl([128, D], BF16)
        tmp = A.ring(2, [128, 8, 128], F32)
        st = A.ring(2, [128, 4], F32)
        xsrc = self.xin(l, t0, T)
        for m in range(T // 128):
            x = xt.next()
            S.dma("sp", x.ap, xsrc[m * 128:(m + 1) * 128, :], writes=[x.res])
            self.norm_T(x, hT, m, Acol, Bcol, st.next(), junk, xn.next(), tmp.next())
        self.end_scope()
        if kind == "s":
            s0 = t0 - c.NP
            cos, sin = A.tl([128, T], F32), A.tl([128, T], F32)
            S.dma("sp", cos.ap, self.ropec[:, s0:s0 + T], writes=[cos.res])
            S.dma("sp", sin.ap, self.ropes[:, s0:s0 + T], writes=[sin.res])
        stg = A.ring(4, [128, 512], BF16)
        stq = A.ring(2, [128, 512], BF16)
        f32s = A.ring(4, [128, 512], F32)
        W = self.w_in[l]
        pm = self.cblk(C_PMAT, BF16)

        def epi_copy(dst, func=AF.Copy):
            def epi(fb, tb, ts, bk):
                o = stg.next()
                S.op("act", lambda e: e.activation(out=o.ap[:, 0:ts], in_=bk.ap[:, 0:ts], func=func), reads=[bk.res], writes=[o.res])
                S.dma("act", dst[fb * 128:(fb + 1) * 128, t0 + tb * 512:t0 + tb * 512 + ts], o.ap[:, 0:ts], reads=[o.res])
            return epi

        def epi_rope(dst):
            def epi(fb, tb, ts, bk):
                qb = stq.next()
                S.op("act", lambda e: e.activation(out=qb.ap[:, 0:ts], in_=bk.ap[:, 0:ts], func=AF.Copy), reads=[bk.res], writes=[qb.res])
                b2 = self.bank()
                S.op("pe", lambda e: e.matmul(b2.ap[:, 0:ts], lhsT=pm, rhs=qb.ap[:, 0:ts], start=True, stop=True),
                     reads=[qb.res, self.cb.res], writes=[b2.res])
                t1, t2, o = f32s.next(), f32s.next(), stg.next()
                cs = slice(tb * 512, tb * 512 + ts)
                S.op("dve", lambda e: e.tensor_tensor(out=t1.ap[:, 0:ts], in0=qb.ap[:, 0:ts], in1=cos.ap[:, cs], op=ALU.mult),
                     reads=[qb.res, cos.res], writes=[t1.res])
                S.op("dve", lambda e: e.tensor_tensor(out=t2.ap[:, 0:ts], in0=b2.ap[:, 0:ts], in1=sin.ap[:, cs], op=ALU.mult),
                     reads=[b2.res, sin.res], writes=[t2.res])
                S.op("dve", lambda e: e.tensor_tensor(out=o.ap[:, 0:ts], in0=t1.ap[:, 0:ts], in1=t2.ap[:, 0:ts], op=ALU.add),
                     reads=[t1.res, t2.res], writes=[o.res])
                S.dma("act", dst[fb * 128:(fb + 1) * 128, t0 + tb * 512:t0 + tb * 512 + ts], o.ap[:, 0:ts], reads=[o.res])
            return epi

        import os
        en = lambda k: k in os.environ.get("P1", "q,k,x,sc,g,v,kc,z,dt").split(",")
        qk_epi = epi_rope if kind == "s" else epi_copy
        if en("q"):
            self.gemm_fm(hT, 16, T, W[:, O_Q:O_Q + 2048], qk_epi(self.QT))
        if en("k"):
            self.gemm_fm(hT, 16, T, W[:, O_K:O_K + 512], qk_epi(self.KT))
        if en("x"):
            self.gemm_fm(hT, 16, T, W[:, O_X:O_X + 3072], epi_copy(self.XBC))
        if en("sc"):
            self.gemm_fm(hT, 16, T, W[:, O_SCB:O_SCB + 2048], epi_copy(self.SCB))
            self.gemm_fm(hT, 16, T, W[:, O_SCC:O_SCC + 2048], epi_copy(self.SCC))
            self.gemm_fm(hT, 16, T, W[:, O_SCH:O_SCH + 2048], epi_copy(self.SCH))
        if en("g"):
            self.gemm_fm(hT, 16, T, W[:, O_G:O_G + 6144], epi_copy(self.G, AF.Sigmoid))

        def epi_tm(dst, func=AF.Copy, dt=BF16, out32=None):
            def epi(blk, bw, m, bk):
                if dst is not None:
                    o = stg.next() if dt == BF16 else f32s.next()
                    S.op("act", lambda e: e.activation(out=o.ap[:, 0:bw], in_=bk.ap[:, 0:bw], func=func), reads=[bk.res], writes=[o.res])
                    S.dma("act", dst[t0 + m * 128:t0 + (m + 1) * 128, blk:blk + bw], o.ap[:, 0:bw], reads=[o.res])
                if out32 is not None:
                    o2 = f32s.next()
                    S.op("act", lambda e: e.activation(out=o2.ap[:, 0:bw], in_=bk.ap[:, 0:bw], func=AF.Copy), reads=[bk.res], writes=[o2.res])
                    tok = t0 + m * 128
                    S.dma("act", out32[tok // 256, l, tok % 256:tok % 256 + 128, blk:blk + bw], o2.ap[:, 0:bw], reads=[o2.res])
            return epi

        if en("v"):
            self.gemm_tm(hT, 16, T, W[:, O_V:O_V + 512], epi_tm(self.V, out32=self.ocv if kind == "p" else None))
        if kind == "p" and en("kc"):
            self.gemm_tm(hT, 16, T, W[:, O_K:O_K + 512], epi_tm(None, out32=self.ock))
        if en("z"):
            self.gemm_tm(hT, 16, T, W[:, O_Z:O_Z + 2048], epi_tm(self.SZ, AF.Silu))
        if en("dt"):
            self.gemm_tm(hT, 16, T, W[:, O_DT:O_DT + 64], epi_tm(self.DT, dt=F32))
        A.release()

    def attention(self, l):
        c, S, A = self.cfg, self.S, self.A
        A.mark()
        esink = A.tl([128, NH], F32)
        S.dma("sp", esink.ap, dram_bc(self.sink[l, :]), writes=[esink.res])
        S.op("act", lambda e: e.activation(out=esink.ap, in_=esink.ap, func=AF.Exp), reads=[esink.res], writes=[esink.res])
        onesb = self.cblk(C_ONES, BF16)
        idb = self.cblk(C_ID, BF16)
        mprev, mnext = self.cblk(C_MPREV, BF16), self.cblk(C_MNEXT, BF16)
        for (tok0, L, kind, si) in self.seqs:
            nb = L // 128
            for g in range(NKV):
                A.mark()
                KTt = A.tl([128, L], BF16)
                Vt = A.tl([128, nb, 128], BF16)
                Q = A.tl([128, 4, L], BF16)
                Y = A.tl([128, 4, L], BF16)
                pT = A.ring(6, [128, 4, 128], BF16)
                dn = A.ring(2, [128, 4, 128], F32)
                S.dma("sp", KTt.ap, self.KT[g * 128:(g + 1) * 128, tok0:tok0 + L], writes=[KTt.res])
                S.dma("sp", Vt.ap, self.V[tok0:tok0 + L, g * 128:(g + 1) * 128].rearrange("(b p) d -> p b d", p=128), writes=[Vt.res])
                S.dma("sp", Q.ap, self.QT[4 * g * 128:(4 * g + 4) * 128, tok0:tok0 + L].rearrange("(h p) t -> p h t", p=128), writes=[Q.res])
                if kind == "s":
                    ckt = A.tl([128, 2, 128], BF16)
                    cvt = A.tl([128, 2, 128], BF16)
                    cKT = A.tl([128, 256], BF16)
                    S.dma("pool", ckt.ap, self.ck[l, :, g * 128:(g + 1) * 128].rearrange("(b p) d -> p b d", p=128), writes=[ckt.res])
                    S.dma("pool", cvt.ap, self.cv[l, :, g * 128:(g + 1) * 128].rearrange("(b p) d -> p b d", p=128), writes=[cvt.res])
                    bk = self.bank()
                    pb = bk.ap.bitcast(BF16)
                    for b in range(2):
                        S.op("pe", lambda e, b=b, pb=pb: e.transpose(out=pb[:, b * 128:(b + 1) * 128], in_=ckt.ap[:, b, :], identity=idb),
                             reads=[ckt.res, self.cb.res], writes=[bk.res])
                    S.op("act", lambda e, pb=pb: e.activation(out=cKT.ap, in_=pb[:, 0:256], func=AF.Copy), reads=[bk.res], writes=[cKT.res])
                for i in range(nb):
                    kbs = []
                    if kind == "s":
                        if i > 0:
                            kbs.append((KTt.ap[:, (i - 1) * 128:i * 128], Vt.ap[:, i - 1, :], mprev, [KTt.res, Vt.res]))
                        kbs.append((KTt.ap[:, i * 128:(i + 1) * 128], Vt.ap[:, i, :], None, [KTt.res, Vt.res]))
                        if i < nb - 1:
                            kbs.append((KTt.ap[:, (i + 1) * 128:(i + 2) * 128], Vt.ap[:, i + 1, :], mnext, [KTt.res, Vt.res]))
                        for b in range(2):
                            kbs.append((cKT.ap[:, b * 128:(b + 1) * 128], cvt.ap[:, b, :], None, [cKT.res, cvt.res]))
                    else:
                        for b in range(nb):
                            kbs.append((KTt.ap[:, b * 128:(b + 1) * 128], Vt.ap[:, b, :], None, [KTt.res, Vt.res]))
                    qv = Q.ap[:, :, i * 128:(i + 1) * 128]
                    sbanks = []
                    for idx, (kap, vap, mask, rr) in enumerate(kbs):
                        bs = self.bank()
                        bsv = bs.ap.rearrange("p (h q) -> p h q", h=4)
                        S.op("pe", lambda e, bsv=bsv, kap=kap, qv=qv: e.matmul(bsv, lhsT=kap, rhs=qv, start=True, stop=True),
                             reads=rr + [Q.res], writes=[bs.res])
                        sbanks.append((bs, bsv))
                    bo, bd = self.bank(), self.bank()
                    for idx, (kap, vap, mask, rr) in enumerate(kbs):
                        bs, bsv = sbanks[idx]
                        p = pT.next()
                        S.op("act", lambda e, p=p, bsv=bsv: e.activation(out=p.ap, in_=bsv, func=AF.Exp, scale=ATT_SCALE),
                             reads=[bs.res], writes=[p.res])
                        if mask is not None:
                            S.op("dve", lambda e, p=p, mask=mask: e.tensor_tensor(out=p.ap, in0=p.ap, in1=bc_mid(mask, 4), op=ALU.mult),
                                 reads=[p.res, self.cb.res], writes=[p.res])
                        first, last = idx == 0, idx == len(kbs) - 1
                        S.op("pe", lambda e, p=p, vap=vap, bo=bo, first=first, last=last: e.matmul(
                            bo.ap.rearrange("p (h q) -> p h q", h=4), lhsT=vap, rhs=p.ap, start=first, stop=last),
                            reads=rr + [p.res], writes=[bo.res])
                        S.op("pe", lambda e, p=p, bd=bd, first=first, last=last: e.matmul(
                            bd.ap.rearrange("p (h q) -> p h q", h=4), lhsT=onesb, rhs=p.ap, start=first, stop=last),
                            reads=[p.res, self.cb.res], writes=[bd.res])
                    d = dn.next()
                    S.op("dve", lambda e, d=d, bd=bd: e.tensor_tensor(out=d.ap, in0=bd.ap.rearrange("p (h q) -> p h q", h=4),
                                                                      in1=bc_last(esink.ap[:, 4 * g:4 * g + 4], 128), op=ALU.add),
                         reads=[bd.res, esink.res], writes=[d.res])
                    S.op("dve", lambda e, d=d: e.reciprocal(out=d.ap, in_=d.ap), reads=[d.res], writes=[d.res])
                    S.op("dve", lambda e, d=d, bo=bo, i=i: e.tensor_tensor(
                        out=Y.ap[:, :, i * 128:(i + 1) * 128], in0=bo.ap.rearrange("p (h q) -> p h q", h=4), in1=d.ap, op=ALU.mult),
                        reads=[bo.res, d.res], writes=[Y.res])
                S.dma("act", self.YATT[4 * g * 128:(4 * g + 4) * 128, tok0:tok0 + L].rearrange("(h p) t -> p h t", p=128), Y.ap, reads=[Y.res])
                self.end_scope()
        self.end_scope()

    def ssd_conv(self, l):
        c, S, A = self.cfg, self.S, self.A
        A.mark()
        idb = self.cblk(C_ID, BF16)
        pc = self.pc
        xin = A.ring(3, [128, 516], BF16)
        acc = A.ring(2, [128, 512], F32)
        xc = A.ring(3, [128, 512], BF16)
        xtok = A.ring(2, [128, 4, 2560], BF16)
        for (tok0, L, kind, si) in self.seqs:
            for b0 in range(0, L, 512):
                TB = min(512, L - b0)
                ns = TB // 128
                xt = xtok.next()
                for cc in range(24):
                    xi = xin.next()
                    lo = max(b0 - 2, 0)
                    hi = min(b0 + TB + 2, L)
                    if b0 == 0:
                        S.op("dve", lambda e, xi=xi: e.memset(xi.ap[:, 0:2], 0.0), writes=[xi.res])
                    if b0 + TB == L:
                        S.op("dve", lambda e, xi=xi, TB=TB: e.memset(xi.ap[:, TB + 2:TB + 4], 0.0), writes=[xi.res])
                    S.dma("sp", xi.ap[:, lo - (b0 - 2):hi - (b0 - 2)], self.XBC[cc * 128:(cc + 1) * 128, tok0 + lo:tok0 + hi], writes=[xi.res])
                    a = acc.next()
                    S.op("dve", lambda e, a=a, xi=xi, cc=cc, TB=TB: e.tensor_scalar(
                        out=a.ap[:, 0:TB], in0=xi.ap[:, 0:TB], scalar1=pc.ap[:, PC_CW + cc * 5:PC_CW + cc * 5 + 1], scalar2=None, op0=ALU.mult),
                        reads=[xi.res, pc.res], writes=[a.res])
                    for k in range(1, 5):
                        S.op("dve", lambda e, a=a, xi=xi, cc=cc, k=k, TB=TB: e.scalar_tensor_tensor(
                            out=a.ap[:, 0:TB], in0=xi.ap[:, k:k + TB], scalar=pc.ap[:, PC_CW + cc * 5 + k:PC_CW + cc * 5 + k + 1],
                            in1=a.ap[:, 0:TB], op0=ALU.mult, op1=ALU.add), reads=[xi.res, pc.res, a.res], writes=[a.res])
                    x = xc.next()
                    S.op("act", lambda e, a=a, x=x, cc=cc, TB=TB: e.activation(
                        out=x.ap[:, 0:TB], in_=a.ap[:, 0:TB], func=AF.Silu, bias=pc.ap[:, PC_CB + cc:PC_CB + cc + 1]),
                        reads=[a.res, pc.res], writes=[x.res])
                    if cc >= 16:
                        S.dma("act", self.XCT[(cc - 16) * 128:(cc - 15) * 128, tok0 + b0:tok0 + b0 + TB], x.ap[:, 0:TB], reads=[x.res])
                    if cc < 20:
                        bk = self.bank()
                        pb = bk.ap.bitcast(BF16)
                        for s in range(ns):
                            S.op("pe", lambda e, pb=pb, s=s, x=x: e.transpose(out=pb[:, s * 128:(s + 1) * 128], in_=x.ap[:, s * 128:(s + 1) * 128], identity=idb),
                                 reads=[x.res, self.cb.res], writes=[bk.res])
                        S.op("act", lambda e, pb=pb, xt=xt, cc=cc, ns=ns: e.activation(
                            out=xt.ap[:, 0:ns, cc * 128:(cc + 1) * 128], in_=pb[:, 0:ns * 128].rearrange("p (s d) -> p s d", s=ns), func=AF.Copy),
                            reads=[bk.res], writes=[xt.res])
                S.dma("act", self.XTOK[tok0 + b0:tok0 + b0 + TB, :].rearrange("(s p) d -> p s d", p=128), xt.ap[:, 0:ns, :], reads=[xt.res])
        A.release()

    def ssd_sweep(self, l, d):
        c, S, A = self.cfg, self.S, self.A
        A.mark()
        idb, idf = self.cblk(C_ID, BF16), self.cblk(C_ID, F32)
        U = self.cblk(C_UF if d == 0 else C_UB, F32)
        NU = self.cblk(C_NUF if d == 0 else C_NUB, F32)
        MB = self.cblk(C_MBF if d == 0 else C_MBB, BF16)
        onesf = self.cblk(C_ONES, F32)
        cres = [self.cf.res, self.cb.res]
        dtb = A.tl([128, 32], F32)
        acoef = A.tl([128, 32], F32)
        S.dma("sp", dtb.ap, dram_bc(self.dt_bias[l, d * 32:(d + 1) * 32]), writes=[dtb.res])
        S.dma("sp", acoef.ap, dram_bc(self.a_log[l, d * 32:(d + 1) * 32]), writes=[acoef.res])
        S.op("act", lambda e: e.activation(out=acoef.ap, in_=acoef.ap, func=AF.Exp), reads=[acoef.res], writes=[acoef.res])
        S.op("dve", lambda e: e.tensor_scalar(out=acoef.ap, in0=acoef.ap, scalar1=-1.0, scalar2=None, op0=ALU.mult), reads=[acoef.res], writes=[acoef.res])
        if d == 1:
            Db = A.tl([128, 32], F32)
            gn = A.tl([128, D], F32)
            S.dma("sp", Db.ap, dram_bc(self.ssm_d[l, :]), writes=[Db.res])
            S.dma("sp", gn.ap, dram_bc(self.ssm_norm_g[l, :]), writes=[gn.res])
        St = A.tl([128, 4, 512], F32)
        Sb = A.tl([128, 4, 512], BF16)
        xtk = A.ring(2, [128, 2560], BF16)
        bct = A.ring(2, [128, 8, 128], BF16)
        dtr = A.ring(2, [128, 32], F32)
        sm = A.ring(2, [128, 6, 32], F32)
        abc = A.ring(2, [128, 32, 128], F32)
        xdt = A.ring(2, [128, D], BF16)
        xdte = A.ring(2, [128, D], BF16)
        cbT = A.ring(2, [128, 4, 128], F32)
        dec = A.ring(2, [128, 4, 128], F32)
        LT = A.ring(2, [128, 32, 128], BF16)
        ych = A.ring(2, [128, D], F32)
        t512 = A.ring(2, [128, 512], F32)
        if d == 1:
            yf = A.ring(2, [128, D], F32)
            szt = A.ring(2, [128, D], BF16)
            ybf = A.ring(2, [128, D], BF16)
            junk = A.tl([128, D], BF16)
            s4 = A.ring(2, [128, 4], F32)
            yTt = A.ring(2, [128, 16, 128], BF16)
        f32o = A.ring(2, [128, 128], F32)
        for (tok0, L, kind, si) in self.seqs:
            nch = L // 128
            if kind == "p":
                S.op("dve", lambda e: e.memset(St.ap, 0.0), writes=[St.res])
            else:
                for j in range(16):
                    ld = f32o.next()
                    S.dma("sp", ld.ap, self.st[l, d, j * 128:(j + 1) * 128, :], writes=[ld.res])
                    bk = self.bank()
                    S.op("pe", lambda e, bk=bk, ld=ld: e.transpose(out=bk.ap[:, 0:128], in_=ld.ap, identity=idf),
                         reads=[ld.res, self.cf.res], writes=[bk.res])
                    S.op("act", lambda e, bk=bk, j=j: e.activation(out=St.ap[:, j // 4, (j % 4) * 128:(j % 4 + 1) * 128], in_=bk.ap[:, 0:128], func=AF.Copy),
                         reads=[bk.res], writes=[St.res])
            S.op("act", lambda e: e.activation(out=Sb.ap, in_=St.ap, func=AF.Copy), reads=[St.res], writes=[Sb.res])
            order = list(range(nch)) if d == 0 else list(range(nch - 1, -1, -1))

            def stageA(ci):
                tk = tok0 + ci * 128
                xt, bc, dr, m = xtk.next(), bct.next(), dtr.next(), sm.next()
                S.dma("sp", xt.ap, self.XTOK[tk:tk + 128, :], writes=[xt.res])
                S.dma("sp", bc.ap, self.XCT[:, tk:tk + 128].rearrange("(c p) t -> p c t", p=128), writes=[bc.res])
                S.dma("sp", dr.ap, self.DT[tk:tk + 128, d * 32:(d + 1) * 32], writes=[dr.res])
                dt_, a_, ac_, E_, te_, cd_ = (m.ap[:, i, :] for i in range(6))
                S.op("dve", lambda e, dr=dr, dt_=dt_: e.tensor_tensor(out=dt_, in0=dr.ap, in1=dtb.ap, op=ALU.add), reads=[dr.res, dtb.res], writes=[m.res])
                S.op("act", lambda e, dt_=dt_: e.activation(out=dt_, in_=dt_, func=AF.Exp), reads=[m.res], writes=[m.res])
                S.op("act", lambda e, dt_=dt_: e.activation(out=dt_, in_=dt_, func=AF.Ln, bias=1.0), reads=[m.res], writes=[m.res])
                S.op("dve", lambda e, dt_=dt_, a_=a_: e.tensor_tensor(out=a_, in0=dt_, in1=acoef.ap, op=ALU.mult), reads=[m.res, acoef.res], writes=[m.res])
                xd = xdt.next()
                S.op(POOL_ENG, lambda e, xd=xd, xt=xt, dt_=dt_: e.tensor_tensor(
                    out=xd.ap.rearrange("p (h q) -> p h q", h=32), in0=xt.ap[:, 0:D].rearrange("p (h q) -> p h q", h=32),
                    in1=bc_last(dt_, 64), op=ALU.mult), reads=[xt.res, m.res], writes=[xd.res])
                ab = abc.next()
                S.op(POOL_ENG, lambda e, ab=ab, a_=a_: e.tensor_copy(out=ab.ap, in_=bc_last(a_, 128)), reads=[m.res], writes=[ab.res])
                bk = self.bank()
                S.op("pe", lambda e, bk=bk, a_=a_: e.matmul(bk.ap[:, 0:32], lhsT=U, rhs=a_, start=True, stop=True), reads=[m.res] + cres, writes=[bk.res])
                S.op("pe", lambda e, bk=bk, a_=a_: e.matmul(bk.ap[:, 32:64], lhsT=onesf, rhs=a_, start=True, stop=True), reads=[m.res] + cres, writes=[bk.res])
                S.op("dve", lambda e, bk=bk, ac_=ac_: e.tensor_scalar(out=ac_, in0=bk.ap[:, 0:32], scalar1=1.0, scalar2=None, op0=ALU.mult), reads=[bk.res], writes=[m.res])
                S.op("act", lambda e, bk=bk, E_=E_: e.activation(out=E_, in_=bk.ap[:, 0:32], func=AF.Exp), reads=[bk.res], writes=[m.res])
                S.op("act", lambda e, bk=bk, cd_=cd_: e.activation(out=cd_, in_=bk.ap[:, 32:64], func=AF.Exp), reads=[bk.res], writes=[m.res])
                S.op("dve", lambda e, bk=bk, ac_=ac_, te_=te_: e.tensor_tensor(out=te_, in0=bk.ap[:, 32:64], in1=ac_, op=ALU.subtract), reads=[bk.res, m.res], writes=[m.res])
                S.op("act", lambda e, te_=te_: e.activation(out=te_, in_=te_, func=AF.Exp), reads=[m.res], writes=[m.res])
                bkc = self.bank()
                for g in range(4):
                    S.op("pe", lambda e, bkc=bkc, bc=bc, g=g: e.matmul(bkc.ap[:, g * 128:(g + 1) * 128], lhsT=bc.ap[:, g, :], rhs=bc.ap[:, 4 + g, :], start=True, stop=True),
                         reads=[bc.res], writes=[bkc.res])
                cbt = cbT.next()
                S.op("act", lambda e, bkc=bkc, cbt=cbt: e.activation(out=cbt.ap, in_=bkc.ap.rearrange("p (g i) -> p g i", g=4), func=AF.Copy), reads=[bkc.res], writes=[cbt.res])
                lt = LT.next()
                for q in range(8):
                    g = q // 2
                    bs = self.bank()
                    for hh in range(4):
                        h = q * 4 + hh
                        S.op("pe", lambda e, bs=bs, ab=ab, h=h, hh=hh: e.matmul(bs.ap[:, hh * 128:(hh + 1) * 128], lhsT=ab.ap[:, h, :], rhs=U, start=(hh == 0), stop=False),
                             reads=[ab.res] + cres, writes=[bs.res])
                    bsv = bs.ap.rearrange("p (h i) -> p h i", h=4)
                    S.op("pe", lambda e, bsv=bsv, ab=ab, q=q: e.matmul(bsv, lhsT=NU, rhs=ab.ap[:, q * 4:(q + 1) * 4, :], start=False, stop=False),
                         reads=[ab.res] + cres, writes=[bs.res])
                    S.op("pe", lambda e, bsv=bsv: e.matmul(bsv, lhsT=idb, rhs=bc_mid(MB, 4), start=False, stop=True), reads=cres, writes=[bs.res])
                    dc = dec.next()
                    S.op("act", lambda e, dc=dc, bsv=bsv: e.activation(out=dc.ap, in_=bsv, func=AF.Exp), reads=[bs.res], writes=[dc.res])
                    S.op("dve", lambda e, dc=dc, lt=lt, cbt=cbt, q=q, g=g: e.tensor_tensor(
                        out=lt.ap[:, q * 4:(q + 1) * 4, :], in0=dc.ap, in1=bc_mid(cbt.ap[:, g, :], 4), op=ALU.mult),
                        reads=[dc.res, cbt.res], writes=[lt.res])
                xe = xdte.next()
                S.op(POOL_ENG, lambda e, xe=xe, xd=xd, te_=te_: e.tensor_tensor(
                    out=xe.ap.rearrange("p (h q) -> p h q", h=32), in0=xd.ap.rearrange("p (h q) -> p h q", h=32),
                    in1=bc_last(te_, 64), op=ALU.mult), reads=[xd.res, m.res], writes=[xe.res])
                return (tk, xt, bc, m, xd, xe, lt)

            def stageB(pack):
                tk, xt, bc, m, xd, xe, lt = pack
                dt_, a_, ac_, E_, te_, cd_ = (m.ap[:, i, :] for i in range(6))
                yc = ych.next()
                for g in range(4):
                    by, bo = self.bank(), self.bank()
                    for hh in range(8):
                        h = g * 8 + hh
                        S.op("pe", lambda e, by=by, lt=lt, xd=xd, h=h, hh=hh: e.matmul(
                            by.ap[:, hh * 64:(hh + 1) * 64], lhsT=lt.ap[:, h, :], rhs=xd.ap[:, h * 64:(h + 1) * 64], start=True, stop=True),
                            reads=[lt.res, xd.res], writes=[by.res])
                    S.op("pe", lambda e, bo=bo, bc=bc, g=g: e.matmul(bo.ap, lhsT=bc.ap[:, 4 + g, :], rhs=Sb.ap[:, g, :], start=True, stop=True),
                         reads=[bc.res, Sb.res], writes=[bo.res])
                    t5 = t512.next()
                    S.op("dve", lambda e, t5=t5, bo=bo, E_=E_, g=g: e.tensor_tensor(
                        out=t5.ap.rearrange("p (h q) -> p h q", h=8), in0=bo.ap.rearrange("p (h q) -> p h q", h=8),
                        in1=bc_last(E_[:, g * 8:(g + 1) * 8], 64), op=ALU.mult), reads=[bo.res, m.res], writes=[t5.res])
                    S.op("dve", lambda e, t5=t5, by=by, yc=yc, g=g: e.tensor_tensor(
                        out=yc.ap[:, g * 512:(g + 1) * 512], in0=by.ap, in1=t5.ap, op=ALU.add), reads=[by.res, t5.res], writes=[yc.res])
                for g in range(4):
                    bst = self.bank()
                    S.op("pe", lambda e, bst=bst, xt=xt, xe=xe, g=g: e.matmul(
                        bst.ap, lhsT=xt.ap[:, D + g * 128:D + (g + 1) * 128], rhs=xe.ap[:, g * 512:(g + 1) * 512], start=True, stop=True),
                        reads=[xt.res, xe.res], writes=[bst.res])
                    S.op("dve", lambda e, cd_=cd_, g=g: e.tensor_tensor(
                        out=St.ap[:, g, :].rearrange("p (h q) -> p h q", h=8), in0=St.ap[:, g, :].rearrange("p (h q) -> p h q", h=8),
                        in1=bc_last(cd_[:, g * 8:(g + 1) * 8], 64), op=ALU.mult), reads=[St.res, m.res], writes=[St.res])
                    S.op("dve", lambda e, bst=bst, g=g: e.tensor_tensor(out=St.ap[:, g, :], in0=St.ap[:, g, :], in1=bst.ap, op=ALU.add),
                         reads=[St.res, bst.res], writes=[St.res])
                S.op("act", lambda e: e.activation(out=Sb.ap, in_=St.ap, func=AF.Copy), reads=[St.res], writes=[Sb.res])
                if d == 0:
                    S.dma("act", self.YF[tk:tk + 128, :], yc.ap, reads=[yc.res])
                else:
                    y0, sz, yb, s, yT = yf.next(), szt.next(), ybf.next(), s4.next(), yTt.next()
                    S.dma("sp", y0.ap, self.YF[tk:tk + 128, :], writes=[y0.res])
                    S.dma("sp", sz.ap, self.SZ[tk:tk + 128, :], writes=[sz.res])
                    S.op("dve", lambda e, yc=yc, y0=y0: e.tensor_tensor(out=yc.ap, in0=yc.ap, in1=y0.ap, op=ALU.add), reads=[yc.res, y0.res], writes=[yc.res])
                    S.op("dve", lambda e, y0=y0, xt=xt: e.tensor_tensor(
                        out=y0.ap.rearrange("p (h q) -> p h q", h=32), in0=xt.ap[:, 0:D].rearrange("p (h q) -> p h q", h=32),
                        in1=bc_last(Db.ap, 64), op=ALU.mult), reads=[xt.res, Db.res], writes=[y0.res])
                    S.op("dve", lambda e, yc=yc, y0=y0: e.tensor_tensor(out=yc.ap, in0=yc.ap, in1=y0.ap, op=ALU.add), reads=[yc.res, y0.res], writes=[yc.res])
                    S.op("dve", lambda e, yc=yc, sz=sz: e.tensor_tensor(out=yc.ap, in0=yc.ap, in1=sz.ap, op=ALU.mult), reads=[yc.res, sz.res], writes=[yc.res])
                    self.ssq(yc, s, junk)
                    self.rstd_from_ssq(s)
                    S.op("dve", lambda e, yc=yc, yb=yb, s=s: e.scalar_tensor_tensor(
                        out=yb.ap, in0=yc.ap, scalar=s.ap[:, 3:4], in1=gn.ap, op0=ALU.mult, op1=ALU.mult),
                        reads=[yc.res, s.res, gn.res], writes=[yb.res])
                    for j in range(2):
                        bk = self.bank()
                        pb = bk.ap.bitcast(BF16)
                        for kk in range(8):
                            kc = j * 8 + kk
                            S.op("pe", lambda e, pb=pb, kk=kk, kc=kc, yb=yb: e.transpose(
                                out=pb[:, kk * 128:(kk + 1) * 128], in_=yb.ap[:, kc * 128:(kc + 1) * 128], identity=idb),
                                reads=[yb.res, self.cb.res], writes=[bk.res])
                        S.op("act", lambda e, pb=pb, yT=yT, j=j: e.activation(
                            out=yT.ap[:, j * 8:(j + 1) * 8, :], in_=pb.rearrange("p (a b) -> p a b", a=8), func=AF.Copy),
                            reads=[bk.res], writes=[yT.res])
                    S.dma("act", self.YSSD[:, tk:tk + 128].rearrange("(c p) t -> p c t", p=128), yT.ap, reads=[yT.res])
            prev = None
            for ci in order:
                cur = stageA(ci)
                if prev is not None:
                    stageB(prev)
                prev = cur
            stageB(prev)
            if kind == "p":
                for j in range(16):
                    bk = self.bank()
                    S.op("pe", lambda e, bk=bk, j=j: e.transpose(out=bk.ap[:, 0:128], in_=St.ap[:, j // 4, (j % 4) * 128:(j % 4 + 1) * 128], identity=idf),
                         reads=[St.res, self.cf.res], writes=[bk.res])
                    o = f32o.next()
                    S.op("act", lambda e, bk=bk, o=o: e.activation(out=o.ap, in_=bk.ap[:, 0:128], func=AF.Copy), reads=[bk.res], writes=[o.res])
                    S.dma("act", self.ost[si, l, d, j * 128:(j + 1) * 128, :], o.ap, reads=[o.res])
        A.release()

    def phase56(self, l, tile):
        c, S, A = self.cfg, self.S, self.A
        t0, T, r, kind = tile
        nm = T // 128
        A.mark()
        self.wbufs = A.ring(3, [128, 16 * 512], BF16)
        self.wc_n = 0
        pc = self.pc
        idb, idf = self.cblk(C_ID, BF16), self.cblk(C_ID, F32)
        big = A.tl([128, 16 * T], F32)
        merged = Tl(big.ap.rearrange("p (c t) -> p c t", c=16), big.res)
        oacc = Tl(big.ap.rearrange("p (m f) -> p m f", m=nm), big.res)
        yT = A.tl([128, 16, T], BF16)
        Gf = A.tl([128, D], F32)
        A.mark()
        Gm = A.tl([128, D], F32)
        gtmp = A.tl([128, D], F32)
        Acol, Bcol = self.modcols(l, r, 3, 4, PC_GPF)
        self.gate_row(l, r, 2, self.g_post_mix, Gm, gtmp)
        self.gate_row(l, r, 5, self.g_post_ffn, Gf, gtmp)
        gt = A.ring(3, [128, 512], BF16)
        tmp = A.ring(2, [128, 512], F32)
        if kind == "p":
            sq0, sqL = (t0 // 256) * 256, 256
        else:
            sq0, sqL = c.NP, c.LS

        def epi_merge(b):
            def epi(fb, tb, ts, bk):
                g = gt.next()
                S.dma("sp", g.ap[:, 0:ts], self.G[b * D + fb * 128:b * D + (fb + 1) * 128, t0 + tb * 512:t0 + tb * 512 + ts], writes=[g.res])
                dst = merged.ap[:, fb, tb * 512:tb * 512 + ts]
                if b == 0:
                    S.op("dve", lambda e: e.tensor_tensor(out=dst, in0=bk.ap[:, 0:ts], in1=g.ap[:, 0:ts], op=ALU.mult),
                         reads=[bk.res, g.res], writes=[merged.res])
                else:
                    t = tmp.next()
                    S.op("dve", lambda e: e.tensor_tensor(out=t.ap[:, 0:ts], in0=bk.ap[:, 0:ts], in1=g.ap[:, 0:ts], op=ALU.mult),
                         reads=[bk.res, g.res], writes=[t.res])
                    S.op("dve", lambda e: e.tensor_tensor(out=dst, in0=dst, in1=t.ap[:, 0:ts], op=ALU.add),
                         reads=[merged.res, t.res], writes=[merged.res])
            return epi

        for b, (src, W) in enumerate(((self.YATT, self.w_att_out), (self.YSSD, self.w_ssd_out), (None, self.w_sc_out))):
            if src is not None:
                S.dma("sp", yT.ap, src[:, t0:t0 + T].rearrange("(c p) t -> p c t", p=128), writes=[yT.res])
            else:
                A.mark()
                cct = A.ring(2, [128, T + 2], BF16)
                cht = A.ring(2, [128, T + 2], BF16)
                cbt_ = A.ring(2, [128, T], BF16)
                ut = A.ring(2, [128, T + 2], F32)
                at = A.ring(2, [128, T], F32)
                nsq = T // sqL if kind == "p" and T > sqL else 1
                for cc in range(16):
                    cc_, ch_, cb_, u, a = cct.next(), cht.next(), cbt_.next(), ut.next(), at.next()
                    rows = slice(cc * 128, (cc + 1) * 128)
                    S.dma("sp", cb_.ap, self.SCB[rows, t0:t0 + T], writes=[cb_.res])
                    if kind == "p":
                        segs = [(s * 256, 256) for s in range(T // 256)]
                    else:
                        segs = [(0, T)]
                    S.op("dve", lambda e, cc_=cc_: e.memset(cc_.ap[:, 0:1], 0.0), writes=[cc_.res])
                    S.op("dve", lambda e, cc_=cc_: e.memset(cc_.ap[:, T + 1:T + 2], 0.0), writes=[cc_.res])
                    S.op("dve", lambda e, ch_=ch_: e.memset(ch_.ap[:, 0:1], 0.0), writes=[ch_.res])
                    S.op("dve", lambda e, ch_=ch_: e.memset(ch_.ap[:, T + 1:T + 2], 0.0), writes=[ch_.res])
                    lo = max(t0 - 1, sq0) if kind == "s" else t0
                    hi = min(t0 + T + 1, sq0 + sqL) if kind == "s" else t0 + T
                    S.dma("sp", cc_.ap[:, 1 + lo - t0:1 + hi - t0], self.SCC[rows, lo:hi], writes=[cc_.res])
                    S.dma("sp", ch_.ap[:, 1 + lo - t0:1 + hi - t0], self.SCH[rows, lo:hi], writes=[ch_.res])
                    S.op("dve", lambda e, u=u, cc_=cc_, ch_=ch_: e.tensor_tensor(out=u.ap, in0=cc_.ap, in1=ch_.ap, op=ALU.mult),
                         reads=[cc_.res, ch_.res], writes=[u.res])
                    for (o0, ln) in segs:
                        w0 = PC_SCW + cc * 3
                        first_lo = 1 if (kind == "p") else 0
                        S.op("dve", lambda e, a=a, u=u, o0=o0, ln=ln, w0=w0: e.tensor_scalar(
                            out=a.ap[:, o0:o0 + ln], in0=u.ap[:, o0 + 1:o0 + 1 + ln], scalar1=pc.ap[:, w0 + 1:w0 + 2], scalar2=None, op0=ALU.mult),
                            reads=[u.res, pc.res], writes=[a.res])
                        sk = 1 if kind == "p" else 0
                        S.op("dve", lambda e, a=a, u=u, o0=o0, ln=ln, w0=w0, sk=sk: e.scalar_tensor_tensor(
                            out=a.ap[:, o0 + sk:o0 + ln], in0=u.ap[:, o0 + sk:o0 + ln], scalar=pc.ap[:, w0:w0 + 1],
                            in1=a.ap[:, o0 + sk:o0 + ln], op0=ALU.mult, op1=ALU.add), reads=[u.res, pc.res, a.res], writes=[a.res])
                        S.op("dve", lambda e, a=a, u=u, o0=o0, ln=ln, w0=w0, sk=sk: e.scalar_tensor_tensor(
                            out=a.ap[:, o0:o0 + ln - sk], in0=u.ap[:, o0 + 2:o0 + 2 + ln - sk], scalar=pc.ap[:, w0 + 2:w0 + 3],
                            in1=a.ap[:, o0:o0 + ln - sk], op0=ALU.mult, op1=ALU.add), reads=[u.res, pc.res, a.res], writes=[a.res])
                    S.op("dve", lambda e, a=a, cb_=cb_, cc=cc: e.tensor_tensor(out=yT.ap[:, cc, :], in0=a.ap, in1=cb_.ap, op=ALU.mult),
                         reads=[a.res, cb_.res], writes=[yT.res])
                self.end_scope()
            self.gemm_fm(yT, 16, T, W[l], epi_merge(b), cache=True)
        for cc in range(16):
            S.op("act", lambda e, cc=cc: e.activation(out=yT.ap[:, cc, :], in_=merged.ap[:, cc, :], func=AF.Copy),
                 reads=[merged.res], writes=[yT.res])

        def epi_o(blk, bw, m, bk):
            S.op("act", lambda e: e.activation(out=oacc.ap[:, m, blk:blk + bw], in_=bk.ap[:, 0:bw], func=AF.Copy), reads=[bk.res], writes=[oacc.res])

        self.gemm_tm(yT, 16, T, self.w_o[l], epi_o, cache=True)
        A.mark()
        xt = A.ring(2, [128, D], F32)
        xn = A.ring(2, [128, D], BF16)
        junk = A.tl([128, D], BF16)
        tmpn = A.ring(2, [128, 8, 128], F32)
        st = A.ring(4, [128, 4], F32)
        xsrc = self.xin(l, t0, T)
        h2T = yT
        for m in range(nm):
            x, s = xt.next(), st.next()
            S.dma("sp", x.ap, xsrc[m * 128:(m + 1) * 128, :], writes=[x.res])
            o = Tl(oacc.ap[:, m, :], oacc.res)
            self.ssq(o, s, junk)
            self.rstd_from_ssq(s)
            S.op("dve", lambda e, o=o, s=s: e.scalar_tensor_tensor(out=o.ap, in0=o.ap, scalar=s.ap[:, 3:4], in1=Gm.ap, op0=ALU.mult, op1=ALU.mult),
                 reads=[oacc.res, s.res, Gm.res], writes=[oacc.res])
            S.op("dve", lambda e, o=o, x=x: e.tensor_tensor(out=x.ap, in0=o.ap, in1=x.ap, op=ALU.add), reads=[oacc.res, x.res], writes=[x.res])
            S.dma("act", self.XM[t0 + m * 128:t0 + (m + 1) * 128, :], x.ap, reads=[x.res])
            self.norm_T(x, h2T, m, Acol, Bcol, st.next(), junk, xn.next(), tmpn.next())
        A.release()
        self.end_scope()
        actT = A.tl([128, 44, T], BF16)
        sg = A.ring(2, [128, 512], F32)
        Wgu = self.w_gate_up[l]
        for fb4 in range(0, 44, 4):
            nf = min(4, 44 - fb4)
            wg, wgr = self.wload(Wgu[:, fb4 * 128:(fb4 + nf) * 128], 16, nf * 128, True)
            wu, wur = self.wload(Wgu[:, DFF + fb4 * 128:DFF + (fb4 + nf) * 128], 16, nf * 128, True)
            for f in range(nf):
                for tb in range((T + 511) // 512):
                    ts = min(512, T - tb * 512)
                    bg, bu = self.bank(), self.bank()
                    for (bk, wv, wr) in ((bg, wg, wgr), (bu, wu, wur)):
                        for kc in range(16):
                            S.op("pe", lambda e, bk=bk, wv=wv, kc=kc, f=f, tb=tb, ts=ts: e.matmul(
                                bk.ap[:, 0:ts], lhsT=wv[:, kc, f * 128:(f + 1) * 128], rhs=h2T.ap[:, kc, tb * 512:tb * 512 + ts],
                                start=(kc == 0), stop=(kc == 15)), reads=[wr, h2T.res], writes=[bk.res])
                    sgt = sg.next()
                    S.op("act", lambda e, sgt=sgt, bg=bg, ts=ts: e.activation(out=sgt.ap[:, 0:ts], in_=bg.ap[:, 0:ts], func=AF.Silu), reads=[bg.res], writes=[sgt.res])
                    S.op("dve", lambda e, sgt=sgt, bu=bu, ts=ts, fb=fb4 + f, tb=tb: e.tensor_tensor(
                        out=actT.ap[:, fb, tb * 512:tb * 512 + ts], in0=bu.ap[:, 0:ts], in1=sgt.ap[:, 0:ts], op=ALU.mult),
                        reads=[bu.res, sgt.res], writes=[actT.res])
        o2T = A.ring(2, [128, 512], F32)

        def wload_down(fb):
            return self.wload(self.w_down[l][:, fb * 128:(fb + 1) * 128], 44, 128, True)

        for fb in range(16):
            wv, wr = wload_down(fb)
            for tb in range((T + 511) // 512):
                ts = min(512, T - tb * 512)
                bk = self.bank()
                for kc in range(44):
                    S.op("pe", lambda e, bk=bk, wv=wv, kc=kc, tb=tb, ts=ts: e.matmul(
                        bk.ap[:, 0:ts], lhsT=wv[:, kc, :], rhs=actT.ap[:, kc, tb * 512:tb * 512 + ts], start=(kc == 0), stop=(kc == 43)),
                        reads=[wr, actT.res], writes=[bk.res])
                ot = o2T.next()
                S.op("act", lambda e, ot=ot, bk=bk, ts=ts: e.activation(out=ot.ap[:, 0:ts], in_=bk.ap[:, 0:ts], func=AF.Copy), reads=[bk.res], writes=[ot.res])
                b2 = self.bank()
                for s_ in range(ts // 128):
                    S.op("pe", lambda e, b2=b2, ot=ot, s_=s_: e.transpose(out=b2.ap[:, s_ * 128:(s_ + 1) * 128], in_=ot.ap[:, s_ * 128:(s_ + 1) * 128], identity=idf),
                         reads=[ot.res, self.cf.res], writes=[b2.res])
                m0 = tb * 4
                S.op("dve", lambda e, b2=b2, ts=ts, m0=m0, fb=fb: e.tensor_scalar(
                    out=oacc.ap[:, m0:m0 + ts // 128, fb * 128:(fb + 1) * 128], in0=b2.ap[:, 0:ts].rearrange("p (s d) -> p s d", s=ts // 128),
                    scalar1=1.0, scalar2=None, op0=ALU.mult),
                    reads=[b2.res], writes=[oacc.res])
        A.mark()
        xt = A.ring(2, [128, D], F32)
        junk = A.tl([128, D], BF16)
        st = A.ring(2, [128, 4], F32)
        xdst = self.xout(l, t0, T)
        for m in range(nm):
            x, s = xt.next(), st.next()
            S.dma("sp", x.ap, self.XM[t0 + m * 128:t0 + (m + 1) * 128, :], writes=[x.res])
            o = Tl(oacc.ap[:, m, :], oacc.res)
            self.ssq(o, s, junk)
            self.rstd_from_ssq(s)
            S.op("dve", lambda e, o=o, s=s: e.scalar_tensor_tensor(out=o.ap, in0=o.ap, scalar=s.ap[:, 3:4], in1=Gf.ap, op0=ALU.mult, op1=ALU.mult),
                 reads=[oacc.res, s.res, Gf.res], writes=[oacc.res])
            S.op("dve", lambda e, o=o, x=x: e.tensor_tensor(out=x.ap, in0=o.ap, in1=x.ap, op=ALU.add), reads=[oacc.res, x.res], writes=[x.res])
            S.dma("act", xdst[m * 128:(m + 1) * 128, :], x.ap, reads=[x.res])
        A.release()
        A.release()


def make_consts(LS):
    t = np.arange(128)
    cst = np.zeros((C_N, 128, 128), np.float32)
    cst[C_ID] = np.eye(128)
    cst[C_UF] = (t[:, None] <= t[None, :])
    cst[C_UB] = (t[:, None] >= t[None, :])
    cst[C_NUF] = -cst[C_UF]
    cst[C_NUB] = -cst[C_UB]
    cst[C_ONES] = 1.0
    cst[C_MBF] = np.where(t[None, :] >= t[:, None], 0.0, NEG)
    cst[C_MBB] = np.where(t[None, :] <= t[:, None], 0.0, NEG)
    cst[C_MPREV] = (t[:, None] >= t[None, :])
    cst[C_MNEXT] = (t[:, None] <= t[None, :])
    pm = np.zeros((128, 128), np.float32)
    for i in range(64):
        pm[2 * i + 1, 2 * i] = 1.0
        pm[2 * i, 2 * i + 1] = 1.0
    cst[C_PMAT] = pm
    cst = np.ascontiguousarray(cst.transpose(1, 0, 2).reshape(128, C_N * 128))
    pos = np.arange(LS)
    row = (pos // GRID_W).astype(np.float32)
    col = (pos % GRID_W).astype(np.float32)
    n_pairs = HD // 4
    inv = (10000.0 ** (-np.arange(n_pairs, dtype=np.float32) / n_pairs)).astype(np.float32)
    ang = np.concatenate([row[:, None] * inv, col[:, None] * inv], axis=-1).astype(np.float32)
    cos = np.cos(ang).astype(np.float32)
    sin = np.sin(ang).astype(np.float32)
    ropec = np.zeros((128, LS), np.float32)
    ropes = np.zeros((128, LS), np.float32)
    ropec[0::2] = cos.T
    ropec[1::2] = cos.T
    ropes[0::2] = -sin.T
    ropes[1::2] = sin.T
    return cst, ropec, ropes


def col16(v):
    return np.ascontiguousarray(v.reshape(-1, 128).T)


def make_pcol(inp, L):
    pcol = np.zeros((L, 128, PC_N), np.float32)
    for l in range(L):
        pcol[l, :, PC_GPM:PC_GPM + 16] = col16(inp["g_pre_mix"][l])
        pcol[l, :, PC_GPF:PC_GPF + 16] = col16(inp["g_pre_ffn"][l])
        cw = inp["ssm_conv_w"][l]
        for k in range(5):
            pcol[l, :, PC_CW + k:PC_CW + 120:5] = col16(cw[k])
        pcol[l, :, PC_CB:PC_CB + 24] = col16(inp["ssm_conv_b"][l])
        sw = inp["sc_conv_w"][l]
        for k in range(3):
            pcol[l, :, PC_SCW + k:PC_SCW + 48:3] = col16(sw[k])
        pcol[l, :, PC_BMOD:PC_BMOD + 96] = col16(inp["b_mod"][l])
    return pcol


_NC_CACHE = {}
SIM_HOOK = None


def get_nc(cfg_key):
    if cfg_key not in _NC_CACHE:
        _NC_CACHE[cfg_key] = KB(Cfg(*cfg_key))
    return _NC_CACHE[cfg_key]


def run(inp, NPS, LS, DEPTH, TT, TT5, n_cores, debug=False, stop=99):
    f = lambda a: np.ascontiguousarray(np.asarray(a, dtype=np.float32))
    inp = {k: f(v) for k, v in inp.items()}
    kb = KB(Cfg(NPS, LS, DEPTH, TT, TT5, debug, stop))
    cst, ropec, ropes = make_consts(LS)
    pcol = make_pcol(inp, DEPTH)
    nb = inp["x_sample"].shape[0]
    shared = {
        "pcol": pcol, "cst": cst, "ropec": ropec, "ropes": ropes,
        "w_mod": inp["w_mod"], "b_mod": inp["b_mod"], "w_in": inp["w_in"], "sink": inp["sink"],
        "ssm_dt_bias": inp["ssm_dt_bias"].reshape(DEPTH, 64), "ssm_a_log": inp["ssm_a_log"].reshape(DEPTH, 64),
        "ssm_d": inp["ssm_d"], "ssm_norm_g": inp["ssm_norm_g"], "w_att_out": inp["w_att_out"],
        "w_ssd_out": inp["w_ssd_out"], "w_sc_out": inp["w_sc_out"], "w_o": inp["w_o"],
        "g_post_mix": inp["g_post_mix"], "g_post_ffn": inp["g_post_ffn"], "w_gate_up": inp["w_gate_up"],
        "w_down": inp["w_down"],
    }
    in_maps = []
    for i in range(n_cores):
        b = i % nb
        cond = np.stack([inp["c_ctx"], inp["c"][b]], axis=0)
        condT = np.ascontiguousarray(cond.reshape(2, 16, 128).transpose(2, 1, 0))
        m = dict(shared)
        m["xp"] = np.ascontiguousarray(inp["x_prompt"][i * NPS:(i + 1) * NPS].reshape(NPS * 256, D))
        m["xs"] = inp["x_sample"][b]
        m["ck"] = np.ascontiguousarray(inp["cache_k"][b].reshape(DEPTH, 256, 512))
        m["cv"] = np.ascontiguousarray(inp["cache_v"][b].reshape(DEPTH, 256, 512))
        m["st"] = np.ascontiguousarray(inp["state_ssm"][b].reshape(DEPTH, 2, SH * SP_, SN))
        m["condT"] = condT
        in_maps.append(m)
    if SIM_HOOK is not None:
        R = SIM_HOOK(kb.nc, in_maps)
    else:
        res = run_bass_kernel_spmd(kb.nc, in_maps, core_ids=list(range(n_cores)))
        R = res.results
    y_prompt = np.concatenate([R[i]["yp"].reshape(NPS, 256, D) for i in range(n_cores)], axis=0)
    y_sample = np.stack([R[b]["ys"] for b in range(nb)], axis=0)
    nck = np.concatenate([R[i]["ock"].reshape(NPS, DEPTH, 256, NKV, HD) for i in range(n_cores)], axis=0)
    ncv = np.concatenate([R[i]["ocv"].reshape(NPS, DEPTH, 256, NKV, HD) for i in range(n_cores)], axis=0)
    nst = np.concatenate([R[i]["ost"].reshape(NPS, DEPTH, 2, SH, SP_, SN) for i in range(n_cores)], axis=0)
    outs = (y_prompt, y_sample, nck, ncv, nst)
    if debug:
        return outs, R
    return outs


def kernel(**inputs):
    return run(inputs, NPS=4, LS=4096, DEPTH=2, TT=1024, TT5=512, n_cores=8)
```

```python
import math
import numpy as np
import concourse.bass as bass
import concourse.mybir as mybir
from concourse.bass_utils import run_bass_kernel_spmd

F32 = mybir.dt.float32
BF16 = mybir.dt.bfloat16
AF = mybir.ActivationFunctionType
ALU = mybir.AluOpType

D = 2048
NH, NKV, HD = 16, 4, 128
SH, SP_, SN, SG = 32, 64, 128, 4
DFF = 5632
DIN = 20544
O_Q, O_K, O_V, O_Z, O_X, O_DT, O_SCB, O_SCC, O_SCH, O_G = 0, 2048, 2560, 3072, 5120, 8192, 8256, 10304, 12352, 14400
EPS = 1e-6
ATT_SCALE = HD ** -0.5
GRID_W = 64
NEG = -30000.0
import os as _os
POOL_ENG = _os.environ.get('POOL_ENG', 'dve')

PC_GPM, PC_GPF, PC_CW, PC_CB, PC_SCW, PC_BMOD, PC_N = 0, 16, 32, 152, 176, 224, 320
C_ID, C_UF, C_UB, C_NUF, C_NUB, C_ONES, C_MBF, C_MBB, C_MPREV, C_MNEXT, C_PMAT, C_N = range(12)


class Tok:
    __slots__ = ("key", "sem", "val", "eng")

    def __init__(self, key, sem, val, eng):
        self.key, self.sem, self.val, self.eng = key, sem, val, eng


class Res:
    __slots__ = ("w", "r")

    def __init__(self):
        self.w = None
        self.r = {}


class Tl:
    __slots__ = ("ap", "res")

    def __init__(self, ap, res=None):
        self.ap = ap
        self.res = res if res is not None else Res()


class Ring:
    def __init__(self, items):
        self.items = items
        self.i = 0

    def next(self):
        t = self.items[self.i % len(self.items)]
        self.i += 1
        return t


class _Rec:
    def __getattr__(self, name):
        def f(*a, **k):
            return (name, a, k)
        return f


_REC = _Rec()


class EngState:
    def __init__(self, name, sem, skip_self):
        self.name, self.sem, self.skip_self = name, sem, skip_self
        self.count = 0
        self.waited = {}
        self.stream = []
        self.dma_sems = []
        self.dma_n = 0


class Sched:
    ENGS = ("pe", "act", "dve", "pool", "sp")

    def __init__(self, nc, n_dma_slots=8):
        self.nc = nc
        self.ctx = []
        self.E = {}
        for name in self.ENGS:
            cm = nc.semaphore("sem_" + name)
            self.ctx.append(cm)
            self.E[name] = EngState(name, cm.__enter__(), skip_self=(name == "pe"))
        for name in ("sp", "pool", "act"):
            for i in range(n_dma_slots):
                cm = nc.semaphore(f"dsem_{name}_{i}")
                self.ctx.append(cm)
                self.E[name].dma_sems.append(cm.__enter__())
        self.nslot = n_dma_slots

    def _deps(self, E, reads, writes):
        best = {}

        def add(t):
            if t is not None and best.get(t.key, (0, None))[0] < t.val:
                best[t.key] = (t.val, t)

        for r in reads:
            add(r.w)
        for w in writes:
            add(w.w)
            for t in w.r.values():
                add(t)
        waits = []
        for key, (val, t) in best.items():
            if t.eng is E and E.skip_self:
                continue
            if E.waited.get(key, 0) >= val:
                continue
            E.waited[key] = val
            waits.append((t.sem, val))
        return waits

    @staticmethod
    def _update(tok, reads, writes):
        for r in reads:
            old = r.r.get(tok.key)
            if old is None or old.val < tok.val:
                r.r[tok.key] = tok
        for w in writes:
            w.w = tok
            w.r = {}

    def op(self, eng, fn, reads=(), writes=()):
        E = self.E[eng]
        waits = self._deps(E, reads, writes)
        E.count += 1
        tok = Tok(("e", eng), E.sem, E.count, E)
        name, a, k = fn(_REC)
        E.stream.append((waits, lambda e, name=name, a=a, k=k: getattr(e, name)(*a, **k), (E.sem, 1)))
        self._update(tok, reads, writes)
        return tok

    def dma(self, q, out, in_, reads=(), writes=(), **kw):
        E = self.E[q]
        waits = self._deps(E, reads, writes)
        slot = E.dma_n % self.nslot
        gen = E.dma_n // self.nslot
        E.dma_n += 1
        sem = E.dma_sems[slot]
        key = ("d", q, slot)
        if gen > 0 and E.waited.get(key, 0) < 16 * gen:
            waits.append((sem, 16 * gen))
            E.waited[key] = 16 * gen
        tok = Tok(key, sem, 16 * (gen + 1), None)

        def fn(e, out=out, in_=in_, kw=kw):
            return e.dma_start(out=out, in_=in_, **kw)

        E.stream.append((waits, fn, (sem, 16)))
        self._update(tok, reads, writes)
        return tok

    def barrier(self):
        toks = []
        for name, E in self.E.items():
            if E.count > 0:
                toks.append(Tok(("e", name), E.sem, E.count, E))
            for slot in range(min(E.dma_n, self.nslot)):
                n_on = (E.dma_n - 1 - slot) // self.nslot + 1
                toks.append(Tok(("d", name, slot), E.dma_sems[slot], 16 * n_on, None))
        for name, E in self.E.items():
            for t in toks:
                if t.eng is E:
                    continue
                if E.waited.get(t.key, 0) >= t.val:
                    continue
                E.waited[t.key] = t.val
                E.stream.append(([(t.sem, t.val)], None, None))

    def emit(self):
        def run(eng, E):
            for waits, fn, inc in E.stream:
                for sem, val in waits:
                    eng.wait_ge(sem, val)
                if fn is not None:
                    fn(eng).then_inc(inc[0], inc[1])

        S = self
        with self.nc.Block() as block:
            @block.tensor
            def _(e):
                run(e, S.E["pe"])

            @block.scalar
            def _(e):
                run(e, S.E["act"])

            @block.vector
            def _(e):
                run(e, S.E["dve"])

            @block.gpsimd
            def _(e):
                run(e, S.E["pool"])

            @block.sync
            def _(e):
                run(e, S.E["sp"])


class Arena:
    def __init__(self, nc, nbytes):
        self.words = nbytes // 4
        self.t = nc.alloc_sbuf_tensor("arena", [128, self.words], F32)
        self.off = 0
        self.marks = []

    def alloc(self, shape, dtype, P=128):
        esz = 2 if dtype == BF16 else 4
        n = int(np.prod(shape[1:]))
        nwords = ((n * esz + 3) // 4 + 15) // 16 * 16
        off = self.off
        assert off + nwords <= self.words, f"SBUF overflow {off}+{nwords}>{self.words} {shape}"
        self.off += nwords
        a = self.t[0:shape[0], off:off + nwords]
        if dtype != F32:
            a = a.bitcast(dtype)
        a = a[:, 0:n]
        if len(shape) == 3:
            a = a.rearrange("p (a b) -> p a b", a=shape[1])
        elif len(shape) == 4:
            a = a.rearrange("p (a b c) -> p a b c", a=shape[1], b=shape[2])
        return a

    def tl(self, shape, dtype):
        return Tl(self.alloc(shape, dtype))

    def ring(self, n, shape, dtype):
        return Ring([self.tl(shape, dtype) for _ in range(n)])

    def mark(self):
        self.marks.append(self.off)

    def release(self):
        self.off = self.marks.pop()


def bc_last(a, n):
    return bass.AP(a.tensor, a.offset, [list(x) for x in a.ap] + [[0, n]])


def bc_mid(a, n):
    ap = [list(x) for x in a.ap]
    return bass.AP(a.tensor, a.offset, [ap[0], [0, n]] + ap[1:])


def dram_bc(a, P=128):
    ap = [list(x) for x in a.ap]
    return bass.AP(a.tensor, a.offset, [[0, P], ap[-1]])


class Cfg:
    def __init__(self, NPS=4, LS=4096, DEPTH=2, TT=1024, TT5=512, debug=False, stop=99):
        self.NPS, self.LS, self.DEPTH, self.TT, self.TT5, self.debug = NPS, LS, DEPTH, TT, TT5, debug
        self.stop = stop
        self.LP = 256
        self.NP = NPS * 256
        self.NT = self.NP + LS


class KB:
    def __init__(self, cfg):
        self.cfg = cfg
        c = cfg
        nc = self.nc = bass.Bass("TRN2", target_bir_lowering=False)
        L = c.DEPTH

        def inp(name, shape):
            return nc.dram_tensor(name, list(shape), F32, kind="ExternalInput").ap()

        def outp(name, shape):
            return nc.dram_tensor(name, list(shape), F32, kind="ExternalOutput").ap()

        self.dbg = {}

        def scr(name, shape, dt=BF16):
            if c.debug:
                a = nc.dram_tensor(name, list(shape), dt, kind="ExternalOutput").ap()
                self.dbg[name] = a
                return a
            return nc.dram_tensor(name, list(shape), dt).ap()

        self.xp = inp("xp", [c.NP, D])
        self.xs = inp("xs", [c.LS, D])
        self.ck = inp("ck", [L, 256, 512])
        self.cv = inp("cv", [L, 256, 512])
        self.st = inp("st", [L, 2, SH * SP_, SN])
        self.condT = inp("condT", [128, 16, 2])
        self.pcol = inp("pcol", [L, 128, PC_N])
        self.cst = inp("cst", [128, C_N * 128])
        self.ropec = inp("ropec", [128, c.LS])
        self.ropes = inp("ropes", [128, c.LS])
        self.w_mod = inp("w_mod", [L, D, 6 * D])
        self.b_mod = inp("b_mod", [L, 6 * D])
        self.w_in = inp("w_in", [L, D, DIN])
        self.sink = inp("sink", [L, NH])
        self.dt_bias = inp("ssm_dt_bias", [L, 64])
        self.a_log = inp("ssm_a_log", [L, 64])
        self.ssm_d = inp("ssm_d", [L, SH])
        self.ssm_norm_g = inp("ssm_norm_g", [L, D])
        self.w_att_out = inp("w_att_out", [L, D, D])
        self.w_ssd_out = inp("w_ssd_out", [L, D, D])
        self.w_sc_out = inp("w_sc_out", [L, D, D])
        self.w_o = inp("w_o", [L, D, D])
        self.g_post_mix = inp("g_post_mix", [L, D])
        self.g_post_ffn = inp("g_post_ffn", [L, D])
        self.w_gate_up = inp("w_gate_up", [L, D, 2 * DFF])
        self.w_down = inp("w_down", [L, DFF, D])

        self.yp = outp("yp", [c.NP, D])
        self.ys = outp("ys", [c.LS, D])
        self.ock = outp("ock", [c.NPS, L, 256, 512])
        self.ocv = outp("ocv", [c.NPS, L, 256, 512])
        self.ost = outp("ost", [c.NPS, L, 2, SH * SP_, SN])

        NT = c.NT
        self.X1 = scr("X1", [NT, D], F32)
        self.XM = scr("XM", [NT, D], F32)
        self.MOD = scr("MOD", [L, 2, 6 * D], F32)
        self.QT = scr("QT", [D, NT])
        self.KT = scr("KT", [512, NT])
        self.V = scr("V", [NT, 512])
        self.SZ = scr("SZ", [NT, D])
        self.XBC = scr("XBC", [3072, NT])
        self.XCT = scr("XCT", [1024, NT])
        self.XTOK = scr("XTOK", [NT, 2560])
        self.DT = scr("DT", [NT, 64], F32)
        self.SCB = scr("SCB", [D, NT])
        self.SCC = scr("SCC", [D, NT])
        self.SCH = scr("SCH", [D, NT])
        self.G = scr("G", [3 * D, NT])
        self.YATT = scr("YATT", [D, NT])
        self.YSSD = scr("YSSD", [D, NT])
        self.YF = scr("YF", [NT, D], F32)
        self.WC = nc.dram_tensor("WC", [64, 128, 8192], BF16).ap()

        self.S = Sched(nc)
        self.A = Arena(nc, 204 * 1024)
        self.banks = Ring([Tl(nc.alloc_psum_tensor(f"ps{i}", [128, 512], F32)[:]) for i in range(8)])
        self.build()

    def bank(self):
        return self.banks.next()

    def end_scope(self):
        self.A.release()
        self.S.barrier()

    def xin(self, l, t0, T):
        c = self.cfg
        if l == 0:
            return self.xp[t0:t0 + T, :] if t0 < c.NP else self.xs[t0 - c.NP:t0 - c.NP + T, :]
        return self.X1[t0:t0 + T, :]

    def xout(self, l, t0, T):
        c = self.cfg
        if l == c.DEPTH - 1:
            return self.yp[t0:t0 + T, :] if t0 < c.NP else self.ys[t0 - c.NP:t0 - c.NP + T, :]
        return self.X1[t0:t0 + T, :]

    def cblk(self, i, dt=F32):
        a = (self.cf if dt == F32 else self.cb)
        return a.ap[:, i * 128:(i + 1) * 128]

    def wload(self, w_ap, KC, bw, cache=False):
        wt = self.wbufs.next()
        n = KC * bw
        v = wt.ap[:, 0:n].rearrange("p (c n) -> p c n", c=KC)
        if not cache:
            self.S.dma("pool", v, w_ap.rearrange("(c p) n -> p c n", p=128), writes=[wt.res])
            return v, wt.res
        slot = self.wc_n
        self.wc_n += 1
        if self.wc_first:
            self.S.dma("pool", v, w_ap.rearrange("(c p) n -> p c n", p=128), writes=[wt.res])
            self.S.dma("sp", self.WC[slot, :, 0:n], wt.ap[:, 0:n], reads=[wt.res])
        else:
            self.S.dma("pool", wt.ap[:, 0:n], self.WC[slot, :, 0:n], writes=[wt.res])
        return v, wt.res

    def gemm_fm(self, actT, KC, T, w_ap, epi, blkw=512, cache=False):
        S = self.S
        ncols = w_ap.shape[1]
        for blk in range(0, ncols, blkw):
            bw = min(blkw, ncols - blk)
            wv, wres = self.wload(w_ap[:, blk:blk + bw], KC, bw, cache)
            for f in range(bw // 128):
                for tb in range((T + 511) // 512):
                    ts = min(512, T - tb * 512)
                    bk = self.bank()
                    for kc in range(KC):
                        S.op("pe", lambda e, bk=bk, wv=wv, kc=kc, f=f, tb=tb, ts=ts: e.matmul(
                            bk.ap[:, 0:ts], lhsT=wv[:, kc, f * 128:(f + 1) * 128],
                            rhs=actT.ap[:, kc, tb * 512:tb * 512 + ts], start=(kc == 0), stop=(kc == KC - 1)),
                            reads=[wres, actT.res], writes=[bk.res])
                    epi(blk // 128 + f, tb, ts, bk)

    def gemm_tm(self, actT, KC, T, w_ap, epi, blkw=512, cache=False):
        S = self.S
        ncols = w_ap.shape[1]
        for blk in range(0, ncols, blkw):
            bw = min(blkw, ncols - blk)
            wv, wres = self.wload(w_ap[:, blk:blk + bw], KC, bw, cache)
            for m in range(T // 128):
                bk = self.bank()
                for kc in range(KC):
                    S.op("pe", lambda e, bk=bk, wv=wv, kc=kc, m=m, bw=bw: e.matmul(
                        bk.ap[:, 0:bw], lhsT=actT.ap[:, kc, m * 128:(m + 1) * 128],
                        rhs=wv[:, kc, 0:bw], start=(kc == 0), stop=(kc == KC - 1)),
                        reads=[wres, actT.res], writes=[bk.res])
                epi(blk, bw, m, bk)

    def rstd_from_ssq(self, s):
        S = self.S
        S.op("dve", lambda e: e.tensor_scalar(out=s.ap[:, 1:2], in0=s.ap[:, 0:1], scalar1=1.0 / D, scalar2=EPS,
                                              op0=ALU.mult, op1=ALU.add), reads=[s.res], writes=[s.res])
        S.op("act", lambda e: e.activation(out=s.ap[:, 2:3], in_=s.ap[:, 1:2], func=AF.Sqrt), reads=[s.res], writes=[s.res])
        S.op("dve", lambda e: e.reciprocal(out=s.ap[:, 3:4], in_=s.ap[:, 2:3]), reads=[s.res], writes=[s.res])

    def ssq(self, x, s, junk):
        S = self.S
        S.op("dve", lambda e: e.memset(s.ap[:, 0:1], 0.0), writes=[s.res])
        S.op("act", lambda e: e.activation(out=junk.ap, in_=x.ap, func=AF.Square, accum_out=s.ap[:, 0:1]),
             reads=[x.res], writes=[junk.res, s.res])

    def norm_T(self, x, hT, m, Acol, Bcol, s, junk, xn, tmp):
        S = self.S
        self.ssq(x, s, junk)
        self.rstd_from_ssq(s)
        S.op("act", lambda e: e.activation(out=xn.ap, in_=x.ap, func=AF.Identity, scale=s.ap[:, 3:4]),
             reads=[x.res, s.res], writes=[xn.res])
        idb = self.cblk(C_ID, BF16)
        for j in range(2):
            bk = self.bank()
            pb = bk.ap.bitcast(BF16)
            for kk in range(8):
                kc = j * 8 + kk
                S.op("pe", lambda e, pb=pb, kk=kk, kc=kc: e.transpose(
                    out=pb[:, kk * 128:(kk + 1) * 128], in_=xn.ap[:, kc * 128:(kc + 1) * 128], identity=idb),
                    reads=[xn.res, self.cb.res], writes=[bk.res])
            pv = pb.rearrange("p (a b) -> p a b", a=8)
            S.op("dve", lambda e, pv=pv, j=j: e.tensor_tensor(
                out=tmp.ap, in0=pv, in1=bc_last(Acol.ap[:, j * 8:(j + 1) * 8], 128), op=ALU.mult),
                reads=[bk.res, Acol.res], writes=[tmp.res])
            S.op("dve", lambda e, j=j: e.tensor_tensor(
                out=hT.ap[:, j * 8:(j + 1) * 8, m * 128:(m + 1) * 128], in0=tmp.ap,
                in1=bc_last(Bcol.ap[:, j * 8:(j + 1) * 8], 128), op=ALU.add),
                reads=[tmp.res, Bcol.res], writes=[hT.res])

    def modcols(self, l, r, i_shift, i_scale, gcol0):
        S, A = self.S, self.A
        Acol, Bcol = A.tl([128, 16], F32), A.tl([128, 16], F32)
        mc = self.modcol
        S.op("dve", lambda e: e.tensor_scalar(out=Acol.ap, in0=mc.ap[:, i_scale * 16:(i_scale + 1) * 16, r], scalar1=1.0,
                                              scalar2=None, op0=ALU.add), reads=[mc.res], writes=[Acol.res])
        S.op("dve", lambda e: e.tensor_tensor(out=Acol.ap, in0=Acol.ap, in1=self.pc.ap[:, gcol0:gcol0 + 16], op=ALU.mult),
             reads=[Acol.res, self.pc.res], writes=[Acol.res])
        S.op("dve", lambda e: e.tensor_copy(out=Bcol.ap, in_=mc.ap[:, i_shift * 16:(i_shift + 1) * 16, r]),
             reads=[mc.res], writes=[Bcol.res])
        return Acol, Bcol

    def gate_row(self, l, r, i_gate, g_ap, Gt, t2):
        S, A = self.S, self.A
        S.dma("sp", Gt.ap, dram_bc(self.MOD[l, r, i_gate * D:(i_gate + 1) * D]), writes=[Gt.res])
        S.dma("sp", t2.ap, dram_bc(g_ap[l, :]), writes=[t2.res])
        S.op("dve", lambda e: e.tensor_tensor(out=Gt.ap, in0=Gt.ap, in1=t2.ap, op=ALU.mult), reads=[Gt.res, t2.res], writes=[Gt.res])
        return Gt

    def build(self):
        c, S, A = self.cfg, self.S, self.A
        self.cf = A.tl([128, C_N * 128], F32)
        self.cb = A.tl([128, C_N * 128], BF16)
        S.dma("sp", self.cf.ap, self.cst, writes=[self.cf.res])
        S.dma("pool", self.cb.ap, self.cst, writes=[self.cb.res])
        self.wbufs = None
        self.wc_n = 0
        self.wc_first = False
        self.pc = A.tl([128, PC_N], F32)
        self.modcol = A.tl([128, 96, 2], F32)
        self.tiles = []
        if c.NP > 0:
            for t0 in range(0, c.NP, c.TT):
                self.tiles.append((t0, min(c.TT, c.NP - t0), 0, "p"))
        for t0 in range(0, c.LS, c.TT):
            self.tiles.append((c.NP + t0, min(c.TT, c.LS - t0), 1, "s"))
        self.seqs = [(i * 256, 256, "p", i) for i in range(c.NPS)] + [(c.NP, c.LS, "s", 0)]
        for l in range(c.DEPTH):
            self.phase0(l)
            S.barrier()
            if c.stop < 1:
                break
            for tile in self.tiles:
                self.phase1(l, tile)
                S.barrier()
            if c.stop < 2:
                break
            self.attention(l)
            if c.stop < 3:
                break
            self.ssd_conv(l)
            S.barrier()
            if c.stop < 4:
                break
            self.ssd_sweep(l, 0)
            S.barrier()
            self.ssd_sweep(l, 1)
            S.barrier()
            if c.stop < 5:
                break
            self.wc_first = True
            for (t0, T, r, kind) in self.tiles:
                for u0 in range(0, T, c.TT5):
                    self.phase56(l, (t0 + u0, min(c.TT5, T - u0), r, kind))
                    S.barrier()
                    self.wc_first = False
        S.barrier()
        S.emit()

    def phase0(self, l):
        S, A = self.S, self.A
        A.mark()
        self.wbufs = A.ring(3, [128, 16 * 512], BF16)
        S.dma("sp", self.pc.ap, self.pcol[l], writes=[self.pc.res])
        ct = A.tl([128, 16, 2], F32)
        sc = A.tl([128, 16, 2], BF16)
        S.dma("sp", ct.ap, self.condT, writes=[ct.res])
        S.op("act", lambda e: e.activation(out=sc.ap, in_=ct.ap, func=AF.Silu), reads=[ct.res], writes=[sc.res])
        brow = A.ring(2, [2, 512], F32)
        orow = A.ring(2, [2, 512], F32)
        mc = self.modcol
        for n in range(24):
            wv, wres = self.wload(self.w_mod[l][:, n * 512:(n + 1) * 512], 16, 512)
            bk = self.bank()
            for kc in range(16):
                S.op("pe", lambda e, bk=bk, wv=wv, kc=kc: e.matmul(bk.ap[0:2, :], lhsT=sc.ap[:, kc, :], rhs=wv[:, kc, :],
                                                                 start=(kc == 0), stop=(kc == 15)),
                     reads=[wres, sc.res], writes=[bk.res])
            br = brow.next()
            S.dma("sp", br.ap, dram_bc(self.b_mod[l, n * 512:(n + 1) * 512], 2), writes=[br.res])
            orr = orow.next()
            S.op("dve", lambda e, bk=bk, br=br, orr=orr: e.tensor_tensor(out=orr.ap, in0=bk.ap[0:2, :], in1=br.ap, op=ALU.add),
                 reads=[bk.res, br.res], writes=[orr.res])
            S.dma("act", self.MOD[l, :, n * 512:(n + 1) * 512], orr.ap, reads=[orr.res])
            bk2 = self.bank()
            for f in range(4):
                for kc in range(16):
                    S.op("pe", lambda e, bk2=bk2, wv=wv, kc=kc, f=f: e.matmul(
                        bk2.ap[:, f * 2:f * 2 + 2], lhsT=wv[:, kc, f * 128:(f + 1) * 128], rhs=sc.ap[:, kc, :],
                        start=(kc == 0 and f == 0), stop=(kc == 15 and f == 3)), reads=[wres, sc.res], writes=[bk2.res])
            S.op("dve", lambda e, bk2=bk2, n=n: e.tensor_tensor(
                out=mc.ap[:, n * 4:(n + 1) * 4, :], in0=bk2.ap[:, 0:8].rearrange("p (a b) -> p a b", a=4),
                in1=bc_last(self.pc.ap[:, PC_BMOD + n * 4:PC_BMOD + (n + 1) * 4], 2), op=ALU.add),
                reads=[bk2.res, self.pc.res], writes=[mc.res])
        A.release()

    def phase1(self, l, tile):
        c, S, A = self.cfg, self.S, self.A
        t0, T, r, kind = tile
        A.mark()
        self.wbufs = A.ring(3, [128, 16 * 512], BF16)
        hT = A.tl([128, 16, T], BF16)
        Acol, Bcol = self.modcols(l, r, 0, 1, PC_GPM)
        A.mark()
        xt = A.ring(2, [128, D], F32)
        xn = A.ring(2, [128, D], BF16)
        junk = A.tl([128, D], BF16)
        tmp = A.ring(2, [128, 8, 128], F32)
        st = A.ring(2, [128, 4], F32)
        xsrc = self.xin(l, t0, T)
        for m in range(T // 128):
            x = xt.next()
            S.dma("sp", x.ap, xsrc[m * 128:(m + 1) * 128, :], writes=[x.res])
            self.norm_T(x, hT, m, Acol, Bcol, st.next(), junk, xn.next(), tmp.next())
        self.end_scope()
        if kind == "s":
            s0 = t0 - c.NP
            cos, sin = A.tl([128, T], F32), A.tl([128, T], F32)
            S.dma("sp", cos.ap, self.ropec[:, s0:s0 + T], writes=[cos.res])
            S.dma("sp", sin.ap, self.ropes[:, s0:s0 + T], writes=[sin.res])
        stg = A.ring(4, [128, 512], BF16)
        stq = A.ring(2, [128, 512], BF16)
        f32s = A.ring(4, [128, 512], F32)
        W = self.w_in[l]
        pm = self.cblk(C_PMAT, BF16)

        def epi_copy(dst, func=AF.Copy):
            def epi(fb, tb, ts, bk):
                o = stg.next()
                S.op("act", lambda e: e.activation(out=o.ap[:, 0:ts], in_=bk.ap[:, 0:ts], func=func), reads=[bk.res], writes=[o.res])
                S.dma("act", dst[fb * 128:(fb + 1) * 128, t0 + tb * 512:t0 + tb * 512 + ts], o.ap[:, 0:ts], reads=[o.res])
            return epi

        def epi_rope(dst):
            def epi(fb, tb, ts, bk):
                qb = stq.next()
                S.op("act", lambda e: e.activation(out=qb.ap[:, 0:ts], in_=bk.ap[:, 0:ts], func=AF.Copy), reads=[bk.res], writes=[qb.res])
                b2 = self.bank()
                S.op("pe", lambda e: e.matmul(b2.ap[:, 0:ts], lhsT=pm, rhs=qb.ap[:, 0:ts], start=True, stop=True),
                     reads=[qb.res, self.cb.res], writes=[b2.res])
                t1, t2, o = f32s.next(), f32s.next(), stg.next()
                cs = slice(tb * 512, tb * 512 + ts)
                S.op("dve", lambda e: e.tensor_tensor(out=t1.ap[:, 0:ts], in0=qb.ap[:, 0:ts], in1=cos.ap[:, cs], op=ALU.mult),
                     reads=[qb.res, cos.res], writes=[t1.res])
                S.op("dve", lambda e: e.tensor_tensor(out=t2.ap[:, 0:ts], in0=b2.ap[:, 0:ts], in1=sin.ap[:, cs], op=ALU.mult),
                     reads=[b2.res, sin.res], writes=[t2.res])
                S.op("dve", lambda e: e.tensor_tensor(out=o.ap[:, 0:ts], in0=t1.ap[:, 0:ts], in1=t2.ap[:, 0:ts], op=ALU.add),
                     reads=[t1.res, t2.res], writes=[o.res])
                S.dma("act", dst[fb * 128:(fb + 1) * 128, t0 + tb * 512:t0 + tb * 512 + ts], o.ap[:, 0:ts], reads=[o.res])
            return epi

        import os
        en = lambda k: k in os.environ.get("P1", "q,k,x,sc,g,v,kc,z,dt").split(",")
        qk_epi = epi_rope if kind == "s" else epi_copy
        if en("q"):
            self.gemm_fm(hT, 16, T, W[:, O_Q:O_Q + 2048], qk_epi(self.QT))
        if en("k"):
            self.gemm_fm(hT, 16, T, W[:, O_K:O_K + 512], qk_epi(self.KT))
        if en("x"):
            self.gemm_fm(hT, 16, T, W[:, O_X:O_X + 3072], epi_copy(self.XBC))
        if en("sc"):
            self.gemm_fm(hT, 16, T, W[:, O_SCB:O_SCB + 2048], epi_copy(self.SCB))
            self.gemm_fm(hT, 16, T, W[:, O_SCC:O_SCC + 2048], epi_copy(self.SCC))
            self.gemm_fm(hT, 16, T, W[:, O_SCH:O_SCH + 2048], epi_copy(self.SCH))
        if en("g"):
            self.gemm_fm(hT, 16, T, W[:, O_G:O_G + 6144], epi_copy(self.G, AF.Sigmoid))

        def epi_tm(dst, func=AF.Copy, dt=BF16, out32=None):
            def epi(blk, bw, m, bk):
                if dst is not None:
                    o = stg.next() if dt == BF16 else f32s.next()
                    S.op("act", lambda e: e.activation(out=o.ap[:, 0:bw], in_=bk.ap[:, 0:bw], func=func), reads=[bk.res], writes=[o.res])
                    S.dma("act", dst[t0 + m * 128:t0 + (m + 1) * 128, blk:blk + bw], o.ap[:, 0:bw], reads=[o.res])
                if out32 is not None:
                    o2 = f32s.next()
                    S.op("act", lambda e: e.activation(out=o2.ap[:, 0:bw], in_=bk.ap[:, 0:bw], func=AF.Copy), reads=[bk.res], writes=[o2.res])
                    tok = t0 + m * 128
                    S.dma("act", out32[tok // 256, l, tok % 256:tok % 256 + 128, blk:blk + bw], o2.ap[:, 0:bw], reads=[o2.res])
            return epi

        if en("v"):
            self.gemm_tm(hT, 16, T, W[:, O_V:O_V + 512], epi_tm(self.V, out32=self.ocv if kind == "p" else None))
        if kind == "p" and en("kc"):
            self.gemm_tm(hT, 16, T, W[:, O_K:O_K + 512], epi_tm(None, out32=self.ock))
        if en("z"):
            self.gemm_tm(hT, 16, T, W[:, O_Z:O_Z + 2048], epi_tm(self.SZ, AF.Silu))
        if en("dt"):
            self.gemm_tm(hT, 16, T, W[:, O_DT:O_DT + 64], epi_tm(self.DT, dt=F32))
        A.release()

    def attention(self, l):
        c, S, A = self.cfg, self.S, self.A
        A.mark()
        esink = A.tl([128, NH], F32)
        S.dma("sp", esink.ap, dram_bc(self.sink[l, :]), writes=[esink.res])
        S.op("act", lambda e: e.activation(out=esink.ap, in_=esink.ap, func=AF.Exp), reads=[esink.res], writes=[esink.res])
        onesb = self.cblk(C_ONES, BF16)
        idb = self.cblk(C_ID, BF16)
        mprev, mnext = self.cblk(C_MPREV, BF16), self.cblk(C_MNEXT, BF16)
        Lmax = max(L for (_, L, _, _) in self.seqs)
        asets = Ring([dict(KT=A.tl([128, Lmax], BF16), V=A.tl([128, Lmax], BF16), Q=A.tl([128, 4 * Lmax], BF16),
                           Y=A.tl([128, 4 * Lmax], BF16), ckt=A.tl([128, 2, 128], BF16), cvt=A.tl([128, 2, 128], BF16),
                           cKT=A.tl([128, 256], BF16)) for _ in range(2)])
        pT = A.ring(6, [128, 4, 128], BF16)
        dn = A.ring(2, [128, 4, 128], F32)
        for (tok0, L, kind, si) in self.seqs:
            nb = L // 128
            for g in range(NKV):
                st_ = asets.next()
                KTt = Tl(st_["KT"].ap[:, 0:L], st_["KT"].res)
                Vt = Tl(st_["V"].ap[:, 0:L].rearrange("p (b d) -> p b d", d=128), st_["V"].res)
                Q = Tl(st_["Q"].ap[:, 0:4 * L].rearrange("p (h t) -> p h t", h=4), st_["Q"].res)
                Y = Tl(st_["Y"].ap[:, 0:4 * L].rearrange("p (h t) -> p h t", h=4), st_["Y"].res)
                S.dma("sp", KTt.ap, self.KT[g * 128:(g + 1) * 128, tok0:tok0 + L], writes=[KTt.res])
                S.dma("sp", Vt.ap, self.V[tok0:tok0 + L, g * 128:(g + 1) * 128].rearrange("(b p) d -> p b d", p=128), writes=[Vt.res])
                S.dma("sp", Q.ap, self.QT[4 * g * 128:(4 * g + 4) * 128, tok0:tok0 + L].rearrange("(h p) t -> p h t", p=128), writes=[Q.res])
                if kind == "s":
                    ckt, cvt, cKT = st_["ckt"], st_["cvt"], st_["cKT"]
                    S.dma("pool", ckt.ap, self.ck[l, :, g * 128:(g + 1) * 128].rearrange("(b p) d -> p b d", p=128), writes=[ckt.res])
                    S.dma("pool", cvt.ap, self.cv[l, :, g * 128:(g + 1) * 128].rearrange("(b p) d -> p b d", p=128), writes=[cvt.res])
                    bk = self.bank()
                    pb = bk.ap.bitcast(BF16)
                    for b in range(2):
                        S.op("pe", lambda e, b=b, pb=pb: e.transpose(out=pb[:, b * 128:(b + 1) * 128], in_=ckt.ap[:, b, :], identity=idb),
                             reads=[ckt.res, self.cb.res], writes=[bk.res])
                    S.op("act", lambda e, pb=pb: e.activation(out=cKT.ap, in_=pb[:, 0:256], func=AF.Copy), reads=[bk.res], writes=[cKT.res])
                for i in range(nb):
                    kbs = []
                    if kind == "s":
                        if i > 0:
                            kbs.append((KTt.ap[:, (i - 1) * 128:i * 128], Vt.ap[:, i - 1, :], mprev, [KTt.res, Vt.res]))
                        kbs.append((KTt.ap[:, i * 128:(i + 1) * 128], Vt.ap[:, i, :], None, [KTt.res, Vt.res]))
                        if i < nb - 1:
                            kbs.append((KTt.ap[:, (i + 1) * 128:(i + 2) * 128], Vt.ap[:, i + 1, :], mnext, [KTt.res, Vt.res]))
                        for b in range(2):
                            kbs.append((cKT.ap[:, b * 128:(b + 1) * 128], cvt.ap[:, b, :], None, [cKT.res, cvt.res]))
                    else:
                        for b in range(nb):
                            kbs.append((KTt.ap[:, b * 128:(b + 1) * 128], Vt.ap[:, b, :], None, [KTt.res, Vt.res]))
                    qv = Q.ap[:, :, i * 128:(i + 1) * 128]
                    sbanks = []
                    for idx, (kap, vap, mask, rr) in enumerate(kbs):
                        bs = self.bank()
                        bsv = bs.ap.rearrange("p (h q) -> p h q", h=4)
                        S.op("pe", lambda e, bsv=bsv, kap=kap, qv=qv: e.matmul(bsv, lhsT=kap, rhs=qv, start=True, stop=True),
                             reads=rr + [Q.res], writes=[bs.res])
                        sbanks.append((bs, bsv))
                    bo, bd = self.bank(), self.bank()
                    for idx, (kap, vap, mask, rr) in enumerate(kbs):
                        bs, bsv = sbanks[idx]
                        p = pT.next()
                        S.op("act", lambda e, p=p, bsv=bsv: e.activation(out=p.ap, in_=bsv, func=AF.Exp, scale=ATT_SCALE),
                             reads=[bs.res], writes=[p.res])
                        if mask is not None:
                            S.op("dve", lambda e, p=p, mask=mask: e.tensor_tensor(out=p.ap, in0=p.ap, in1=bc_mid(mask, 4), op=ALU.mult),
                                 reads=[p.res, self.cb.res], writes=[p.res])
                        first, last = idx == 0, idx == len(kbs) - 1
                        S.op("pe", lambda e, p=p, vap=vap, bo=bo, first=first, last=last: e.matmul(
                            bo.ap.rearrange("p (h q) -> p h q", h=4), lhsT=vap, rhs=p.ap, start=first, stop=last),
                            reads=rr + [p.res], writes=[bo.res])
                        S.op("pe", lambda e, p=p, bd=bd, first=first, last=last: e.matmul(
                            bd.ap.rearrange("p (h q) -> p h q", h=4), lhsT=onesb, rhs=p.ap, start=first, stop=last),
                            reads=[p.res, self.cb.res], writes=[bd.res])
                    d = dn.next()
                    S.op("dve", lambda e, d=d, bd=bd: e.tensor_tensor(out=d.ap, in0=bd.ap.rearrange("p (h q) -> p h q", h=4),
                                                                      in1=bc_last(esink.ap[:, 4 * g:4 * g + 4], 128), op=ALU.add),
                         reads=[bd.res, esink.res], writes=[d.res])
                    S.op("dve", lambda e, d=d: e.reciprocal(out=d.ap, in_=d.ap), reads=[d.res], writes=[d.res])
                    S.op("dve", lambda e, d=d, bo=bo, i=i: e.tensor_tensor(
                        out=Y.ap[:, :, i * 128:(i + 1) * 128], in0=bo.ap.rearrange("p (h q) -> p h q", h=4), in1=d.ap, op=ALU.mult),
                        reads=[bo.res, d.res], writes=[Y.res])
                S.dma("act", self.YATT[4 * g * 128:(4 * g + 4) * 128, tok0:tok0 + L].rearrange("(h p) t -> p h t", p=128), Y.ap, reads=[Y.res])
        self.end_scope()

    def ssd_conv(self, l):
        c, S, A = self.cfg, self.S, self.A
        A.mark()
        idb = self.cblk(C_ID, BF16)
        pc = self.pc
        xin = A.ring(3, [128, 516], BF16)
        acc = A.ring(2, [128, 512], F32)
        xc = A.ring(3, [128, 512], BF16)
        xtok = A.ring(2, [128, 4, 2560], BF16)
        for (tok0, L, kind, si) in self.seqs:
            for b0 in range(0, L, 512):
                TB = min(512, L - b0)
                ns = TB // 128
                xt = xtok.next()
                for cc in range(24):
                    xi = xin.next()
                    lo = max(b0 - 2, 0)
                    hi = min(b0 + TB + 2, L)
                    if b0 == 0:
                        S.op("dve", lambda e, xi=xi: e.memset(xi.ap[:, 0:2], 0.0), writes=[xi.res])
                    if b0 + TB == L:
                        S.op("dve", lambda e, xi=xi, TB=TB: e.memset(xi.ap[:, TB + 2:TB + 4], 0.0), writes=[xi.res])
                    S.dma("sp", xi.ap[:, lo - (b0 - 2):hi - (b0 - 2)], self.XBC[cc * 128:(cc + 1) * 128, tok0 + lo:tok0 + hi], writes=[xi.res])
                    a = acc.next()
                    S.op("dve", lambda e, a=a, xi=xi, cc=cc, TB=TB: e.tensor_scalar(
                        out=a.ap[:, 0:TB], in0=xi.ap[:, 0:TB], scalar1=pc.ap[:, PC_CW + cc * 5:PC_CW + cc * 5 + 1], scalar2=None, op0=ALU.mult),
                        reads=[xi.res, pc.res], writes=[a.res])
                    for k in range(1, 5):
                        S.op("dve", lambda e, a=a, xi=xi, cc=cc, k=k, TB=TB: e.scalar_tensor_tensor(
                            out=a.ap[:, 0:TB], in0=xi.ap[:, k:k + TB], scalar=pc.ap[:, PC_CW + cc * 5 + k:PC_CW + cc * 5 + k + 1],
                            in1=a.ap[:, 0:TB], op0=ALU.mult, op1=ALU.add), reads=[xi.res, pc.res, a.res], writes=[a.res])
                    x = xc.next()
                    S.op("act", lambda e, a=a, x=x, cc=cc, TB=TB: e.activation(
                        out=x.ap[:, 0:TB], in_=a.ap[:, 0:TB], func=AF.Silu, bias=pc.ap[:, PC_CB + cc:PC_CB + cc + 1]),
                        reads=[a.res, pc.res], writes=[x.res])
                    if cc >= 16:
                        S.dma("act", self.XCT[(cc - 16) * 128:(cc - 15) * 128, tok0 + b0:tok0 + b0 + TB], x.ap[:, 0:TB], reads=[x.res])
                    if cc < 20:
                        bk = self.bank()
                        pb = bk.ap.bitcast(BF16)
                        for s in range(ns):
                            S.op("pe", lambda e, pb=pb, s=s, x=x: e.transpose(out=pb[:, s * 128:(s + 1) * 128], in_=x.ap[:, s * 128:(s + 1) * 128], identity=idb),
                                 reads=[x.res, self.cb.res], writes=[bk.res])
                        S.op("act", lambda e, pb=pb, xt=xt, cc=cc, ns=ns: e.activation(
                            out=xt.ap[:, 0:ns, cc * 128:(cc + 1) * 128], in_=pb[:, 0:ns * 128].rearrange("p (s d) -> p s d", s=ns), func=AF.Copy),
                            reads=[bk.res], writes=[xt.res])
                S.dma("act", self.XTOK[tok0 + b0:tok0 + b0 + TB, :].rearrange("(s p) d -> p s d", p=128), xt.ap[:, 0:ns, :], reads=[xt.res])
        A.release()

    def ssd_sweep(self, l, d):
        c, S, A = self.cfg, self.S, self.A
        A.mark()
        idb, idf = self.cblk(C_ID, BF16), self.cblk(C_ID, F32)
        U = self.cblk(C_UF if d == 0 else C_UB, F32)
        NU = self.cblk(C_NUF if d == 0 else C_NUB, F32)
        MB = self.cblk(C_MBF if d == 0 else C_MBB, BF16)
        onesf = self.cblk(C_ONES, F32)
        cres = [self.cf.res, self.cb.res]
        dtb = A.tl([128, 32], F32)
        acoef = A.tl([128, 32], F32)
        S.dma("sp", dtb.ap, dram_bc(self.dt_bias[l, d * 32:(d + 1) * 32]), writes=[dtb.res])
        S.dma("sp", acoef.ap, dram_bc(self.a_log[l, d * 32:(d + 1) * 32]), writes=[acoef.res])
        S.op("act", lambda e: e.activation(out=acoef.ap, in_=acoef.ap, func=AF.Exp), reads=[acoef.res], writes=[acoef.res])
        S.op("dve", lambda e: e.tensor_scalar(out=acoef.ap, in0=acoef.ap, scalar1=-1.0, scalar2=None, op0=ALU.mult), reads=[acoef.res], writes=[acoef.res])
        if d == 1:
            Db = A.tl([128, 32], F32)
            gn = A.tl([128, D], F32)
            S.dma("sp", Db.ap, dram_bc(self.ssm_d[l, :]), writes=[Db.res])
            S.dma("sp", gn.ap, dram_bc(self.ssm_norm_g[l, :]), writes=[gn.res])
        St = A.tl([128, 4, 512], F32)
        Sb = A.tl([128, 4, 512], BF16)
        xtk = A.ring(2, [128, 2560], BF16)
        bct = A.ring(2, [128, 8, 128], BF16)
        dtr = A.ring(2, [128, 32], F32)
        sm = A.ring(2, [128, 6, 32], F32)
        abc = A.ring(2, [128, 32, 128], F32)
        xdt = A.ring(2, [128, D], BF16)
        xdte = A.ring(2, [128, D], BF16)
        cbT = A.ring(2, [128, 4, 128], F32)
        dec = A.ring(2, [128, 4, 128], F32)
        LT = A.ring(2, [128, 32, 128], BF16)
        ych = A.ring(2, [128, D], F32)
        t512 = A.ring(2, [128, 512], F32)
        if d == 1:
            yf = A.ring(2, [128, D], F32)
            szt = A.ring(2, [128, D], BF16)
            ybf = A.ring(2, [128, D], BF16)
            junk = A.tl([128, D], BF16)
            s4 = A.ring(2, [128, 4], F32)
            yTt = A.ring(2, [128, 16, 128], BF16)
        f32o = A.ring(2, [128, 128], F32)
        for (tok0, L, kind, si) in self.seqs:
            nch = L // 128
            if kind == "p":
                S.op("dve", lambda e: e.memset(St.ap, 0.0), writes=[St.res])
            else:
                for j in range(16):
                    ld = f32o.next()
                    S.dma("sp", ld.ap, self.st[l, d, j * 128:(j + 1) * 128, :], writes=[ld.res])
                    bk = self.bank()
                    S.op("pe", lambda e, bk=bk, ld=ld: e.transpose(out=bk.ap[:, 0:128], in_=ld.ap, identity=idf),
                         reads=[ld.res, self.cf.res], writes=[bk.res])
                    S.op("act", lambda e, bk=bk, j=j: e.activation(out=St.ap[:, j // 4, (j % 4) * 128:(j % 4 + 1) * 128], in_=bk.ap[:, 0:128], func=AF.Copy),
                         reads=[bk.res], writes=[St.res])
            S.op("act", lambda e: e.activation(out=Sb.ap, in_=St.ap, func=AF.Copy), reads=[St.res], writes=[Sb.res])
            order = list(range(nch)) if d == 0 else list(range(nch - 1, -1, -1))

            def stageA(ci):
                tk = tok0 + ci * 128
                xt, bc, dr, m = xtk.next(), bct.next(), dtr.next(), sm.next()
                S.dma("sp", xt.ap, self.XTOK[tk:tk + 128, :], writes=[xt.res])
                S.dma("sp", bc.ap, self.XCT[:, tk:tk + 128].rearrange("(c p) t -> p c t", p=128), writes=[bc.res])
                S.dma("sp", dr.ap, self.DT[tk:tk + 128, d * 32:(d + 1) * 32], writes=[dr.res])
                dt_, a_, ac_, E_, te_, cd_ = (m.ap[:, i, :] for i in range(6))
                S.op("dve", lambda e, dr=dr, dt_=dt_: e.tensor_tensor(out=dt_, in0=dr.ap, in1=dtb.ap, op=ALU.add), reads=[dr.res, dtb.res], writes=[m.res])
                S.op("act", lambda e, dt_=dt_: e.activation(out=dt_, in_=dt_, func=AF.Exp), reads=[m.res], writes=[m.res])
                S.op("act", lambda e, dt_=dt_: e.activation(out=dt_, in_=dt_, func=AF.Ln, bias=1.0), reads=[m.res], writes=[m.res])
                S.op("dve", lambda e, dt_=dt_, a_=a_: e.tensor_tensor(out=a_, in0=dt_, in1=acoef.ap, op=ALU.mult), reads=[m.res, acoef.res], writes=[m.res])
                xd = xdt.next()
                S.op(POOL_ENG, lambda e, xd=xd, xt=xt, dt_=dt_: e.tensor_tensor(
                    out=xd.ap.rearrange("p (h q) -> p h q", h=32), in0=xt.ap[:, 0:D].rearrange("p (h q) -> p h q", h=32),
                    in1=bc_last(dt_, 64), op=ALU.mult), reads=[xt.res, m.res], writes=[xd.res])
                ab = abc.next()
                S.op(POOL_ENG, lambda e, ab=ab, a_=a_: e.tensor_copy(out=ab.ap, in_=bc_last(a_, 128)), reads=[m.res], writes=[ab.res])
                bk = self.bank()
                S.op("pe", lambda e, bk=bk, a_=a_: e.matmul(bk.ap[:, 0:32], lhsT=U, rhs=a_, start=True, stop=True), reads=[m.res] + cres, writes=[bk.res])
                S.op("pe", lambda e, bk=bk, a_=a_: e.matmul(bk.ap[:, 32:64], lhsT=onesf, rhs=a_, start=True, stop=True), reads=[m.res] + cres, writes=[bk.res])
                S.op("dve", lambda e, bk=bk, ac_=ac_: e.tensor_scalar(out=ac_, in0=bk.ap[:, 0:32], scalar1=1.0, scalar2=None, op0=ALU.mult), reads=[bk.res], writes=[m.res])
                S.op("act", lambda e, bk=bk, E_=E_: e.activation(out=E_, in_=bk.ap[:, 0:32], func=AF.Exp), reads=[bk.res], writes=[m.res])
                S.op("act", lambda e, bk=bk, cd_=cd_: e.activation(out=cd_, in_=bk.ap[:, 32:64], func=AF.Exp), reads=[bk.res], writes=[m.res])
                S.op("dve", lambda e, bk=bk, ac_=ac_, te_=te_: e.tensor_tensor(out=te_, in0=bk.ap[:, 32:64], in1=ac_, op=ALU.subtract), reads=[bk.res, m.res], writes=[m.res])
                S.op("act", lambda e, te_=te_: e.activation(out=te_, in_=te_, func=AF.Exp), reads=[m.res], writes=[m.res])
                bkc = self.bank()
                for g in range(4):
                    S.op("pe", lambda e, bkc=bkc, bc=bc, g=g: e.matmul(bkc.ap[:, g * 128:(g + 1) * 128], lhsT=bc.ap[:, g, :], rhs=bc.ap[:, 4 + g, :], start=True, stop=True),
                         reads=[bc.res], writes=[bkc.res])
                cbt = cbT.next()
                S.op("act", lambda e, bkc=bkc, cbt=cbt: e.activation(out=cbt.ap, in_=bkc.ap.rearrange("p (g i) -> p g i", g=4), func=AF.Copy), reads=[bkc.res], writes=[cbt.res])
                lt = LT.next()
                for q in range(8):
                    g = q // 2
                    bs = self.bank()
                    for hh in range(4):
                        h = q * 4 + hh
                        S.op("pe", lambda e, bs=bs, ab=ab, h=h, hh=hh: e.matmul(bs.ap[:, hh * 128:(hh + 1) * 128], lhsT=ab.ap[:, h, :], rhs=U, start=(hh == 0), stop=False),
                             reads=[ab.res] + cres, writes=[bs.res])
                    bsv = bs.ap.rearrange("p (h i) -> p h i", h=4)
                    S.op("pe", lambda e, bsv=bsv, ab=ab, q=q: e.matmul(bsv, lhsT=NU, rhs=ab.ap[:, q * 4:(q + 1) * 4, :], start=False, stop=False),
                         reads=[ab.res] + cres, writes=[bs.res])
                    S.op("pe", lambda e, bsv=bsv: e.matmul(bsv, lhsT=idb, rhs=bc_mid(MB, 4), start=False, stop=True), reads=cres, writes=[bs.res])
                    dc = dec.next()
                    S.op("act", lambda e, dc=dc, bsv=bsv: e.activation(out=dc.ap, in_=bsv, func=AF.Exp), reads=[bs.res], writes=[dc.res])
                    S.op("dve", lambda e, dc=dc, lt=lt, cbt=cbt, q=q, g=g: e.tensor_tensor(
                        out=lt.ap[:, q * 4:(q + 1) * 4, :], in0=dc.ap, in1=bc_mid(cbt.ap[:, g, :], 4), op=ALU.mult),
                        reads=[dc.res, cbt.res], writes=[lt.res])
                xe = xdte.next()
                S.op(POOL_ENG, lambda e, xe=xe, xd=xd, te_=te_: e.tensor_tensor(
                    out=xe.ap.rearrange("p (h q) -> p h q", h=32), in0=xd.ap.rearrange("p (h q) -> p h q", h=32),
                    in1=bc_last(te_, 64), op=ALU.mult), reads=[xd.res, m.res], writes=[xe.res])
                return (tk, xt, bc, m, xd, xe, lt)

            def stageB(pack):
                tk, xt, bc, m, xd, xe, lt = pack
                dt_, a_, ac_, E_, te_, cd_ = (m.ap[:, i, :] for i in range(6))
                yc = ych.next()
                for g in range(4):
                    by, bo = self.bank(), self.bank()
                    for hh in range(8):
                        h = g * 8 + hh
                        S.op("pe", lambda e, by=by, lt=lt, xd=xd, h=h, hh=hh: e.matmul(
                            by.ap[:, hh * 64:(hh + 1) * 64], lhsT=lt.ap[:, h, :], rhs=xd.ap[:, h * 64:(h + 1) * 64], start=True, stop=True),
                            reads=[lt.res, xd.res], writes=[by.res])
                    S.op("pe", lambda e, bo=bo, bc=bc, g=g: e.matmul(bo.ap, lhsT=bc.ap[:, 4 + g, :], rhs=Sb.ap[:, g, :], start=True, stop=True),
                         reads=[bc.res, Sb.res], writes=[bo.res])
                    t5 = t512.next()
                    S.op("dve", lambda e, t5=t5, bo=bo, E_=E_, g=g: e.tensor_tensor(
                        out=t5.ap.rearrange("p (h q) -> p h q", h=8), in0=bo.ap.rearrange("p (h q) -> p h q", h=8),
                        in1=bc_last(E_[:, g * 8:(g + 1) * 8], 64), op=ALU.mult), reads=[bo.res, m.res], writes=[t5.res])
                    S.op("dve", lambda e, t5=t5, by=by, yc=yc, g=g: e.tensor_tensor(
                        out=yc.ap[:, g * 512:(g + 1) * 512], in0=by.ap, in1=t5.ap, op=ALU.add), reads=[by.res, t5.res], writes=[yc.res])
                for g in range(4):
                    bst = self.bank()
                    S.op("pe", lambda e, bst=bst, xt=xt, xe=xe, g=g: e.matmul(
                        bst.ap, lhsT=xt.ap[:, D + g * 128:D + (g + 1) * 128], rhs=xe.ap[:, g * 512:(g + 1) * 512], start=True, stop=True),
                        reads=[xt.res, xe.res], writes=[bst.res])
                    S.op("dve", lambda e, cd_=cd_, g=g: e.tensor_tensor(
                        out=St.ap[:, g, :].rearrange("p (h q) -> p h q", h=8), in0=St.ap[:, g, :].rearrange("p (h q) -> p h q", h=8),
                        in1=bc_last(cd_[:, g * 8:(g + 1) * 8], 64), op=ALU.mult), reads=[St.res, m.res], writes=[St.res])
                    S.op("dve", lambda e, bst=bst, g=g: e.tensor_tensor(out=St.ap[:, g, :], in0=St.ap[:, g, :], in1=bst.ap, op=ALU.add),
                         reads=[St.res, bst.res], writes=[St.res])
                S.op("act", lambda e: e.activation(out=Sb.ap, in_=St.ap, func=AF.Copy), reads=[St.res], writes=[Sb.res])
                if d == 0:
                    S.dma("act", self.YF[tk:tk + 128, :], yc.ap, reads=[yc.res])
                else:
                    y0, sz, yb, s, yT = yf.next(), szt.next(), ybf.next(), s4.next(), yTt.next()
                    S.dma("sp", y0.ap, self.YF[tk:tk + 128, :], writes=[y0.res])
                    S.dma("sp", sz.ap, self.SZ[tk:tk + 128, :], writes=[sz.res])
                    S.op("dve", lambda e, yc=yc, y0=y0: e.tensor_tensor(out=yc.ap, in0=yc.ap, in1=y0.ap, op=ALU.add), reads=[yc.res, y0.res], writes=[yc.res])
                    S.op("dve", lambda e, y0=y0, xt=xt: e.tensor_tensor(
                        out=y0.ap.rearrange("p (h q) -> p h q", h=32), in0=xt.ap[:, 0:D].rearrange("p (h q) -> p h q", h=32),
                        in1=bc_last(Db.ap, 64), op=ALU.mult), reads=[xt.res, Db.res], writes=[y0.res])
                    S.op("dve", lambda e, yc=yc, y0=y0: e.tensor_tensor(out=yc.ap, in0=yc.ap, in1=y0.ap, op=ALU.add), reads=[yc.res, y0.res], writes=[yc.res])
                    S.op("dve", lambda e, yc=yc, sz=sz: e.tensor_tensor(out=yc.ap, in0=yc.ap, in1=sz.ap, op=ALU.mult), reads=[yc.res, sz.res], writes=[yc.res])
                    self.ssq(yc, s, junk)
                    self.rstd_from_ssq(s)
                    S.op("dve", lambda e, yc=yc, yb=yb, s=s: e.scalar_tensor_tensor(
                        out=yb.ap, in0=yc.ap, scalar=s.ap[:, 3:4], in1=gn.ap, op0=ALU.mult, op1=ALU.mult),
                        reads=[yc.res, s.res, gn.res], writes=[yb.res])
                    for j in range(2):
                        bk = self.bank()
                        pb = bk.ap.bitcast(BF16)
                        for kk in range(8):
                            kc = j * 8 + kk
                            S.op("pe", lambda e, pb=pb, kk=kk, kc=kc, yb=yb: e.transpose(
                                out=pb[:, kk * 128:(kk + 1) * 128], in_=yb.ap[:, kc * 128:(kc + 1) * 128], identity=idb),
                                reads=[yb.res, self.cb.res], writes=[bk.res])
                        S.op("act", lambda e, pb=pb, yT=yT, j=j: e.activation(
                            out=yT.ap[:, j * 8:(j + 1) * 8, :], in_=pb.rearrange("p (a b) -> p a b", a=8), func=AF.Copy),
                            reads=[bk.res], writes=[yT.res])
                    S.dma("act", self.YSSD[:, tk:tk + 128].rearrange("(c p) t -> p c t", p=128), yT.ap, reads=[yT.res])
            prev = None
            for ci in order:
                cur = stageA(ci)
                if _os.environ.get("SSD_PIPE", "1") == "0":
                    stageB(cur)
                    continue
                if prev is not None:
                    stageB(prev)
                prev = cur
            if prev is not None:
                stageB(prev)
            if kind == "p":
                for j in range(16):
                    bk = self.bank()
                    S.op("pe", lambda e, bk=bk, j=j: e.transpose(out=bk.ap[:, 0:128], in_=St.ap[:, j // 4, (j % 4) * 128:(j % 4 + 1) * 128], identity=idf),
                         reads=[St.res, self.cf.res], writes=[bk.res])
                    o = f32o.next()
                    S.op("act", lambda e, bk=bk, o=o: e.activation(out=o.ap, in_=bk.ap[:, 0:128], func=AF.Copy), reads=[bk.res], writes=[o.res])
                    S.dma("act", self.ost[si, l, d, j * 128:(j + 1) * 128, :], o.ap, reads=[o.res])
        A.release()

    def phase56(self, l, tile):
        c, S, A = self.cfg, self.S, self.A
        t0, T, r, kind = tile
        nm = T // 128
        A.mark()
        self.wbufs = A.ring(3, [128, 16 * 512], BF16)
        self.wc_n = 0
        pc = self.pc
        idb, idf = self.cblk(C_ID, BF16), self.cblk(C_ID, F32)
        big = A.tl([128, 16 * T], F32)
        merged = Tl(big.ap.rearrange("p (c t) -> p c t", c=16), big.res)
        oacc = Tl(big.ap.rearrange("p (m f) -> p m f", m=nm), big.res)
        yT = A.tl([128, 16, T], BF16)
        Gf = A.tl([128, D], F32)
        A.mark()
        yT1 = A.tl([128, 16, T], BF16)
        yT2 = A.tl([128, 16, T], BF16)
        xt = A.ring(2, [128, D], F32)
        Gm = A.tl([128, D], F32)
        Acol, Bcol = self.modcols(l, r, 3, 4, PC_GPF)
        self.gate_row(l, r, 2, self.g_post_mix, Gm, xt.items[0])
        self.gate_row(l, r, 5, self.g_post_ffn, Gf, xt.items[1])
        gt = A.ring(3, [128, 512], BF16)
        tmp = A.ring(2, [128, 512], F32)
        if kind == "p":
            sq0, sqL = (t0 // 256) * 256, 256
        else:
            sq0, sqL = c.NP, c.LS
        S.dma("sp", yT.ap, self.YATT[:, t0:t0 + T].rearrange("(c p) t -> p c t", p=128), writes=[yT.res])
        S.dma("sp", yT1.ap, self.YSSD[:, t0:t0 + T].rearrange("(c p) t -> p c t", p=128), writes=[yT1.res])
        cct = A.ring(2, [128, T + 2], BF16)
        cht = A.ring(2, [128, T + 2], BF16)
        cbt_ = A.ring(2, [128, T], BF16)
        ut = A.ring(2, [128, T + 2], F32)
        at = A.ring(2, [128, T], F32)

        def sc_step(cc):
            cc_, ch_, cb_, u, a = cct.next(), cht.next(), cbt_.next(), ut.next(), at.next()
            rows = slice(cc * 128, (cc + 1) * 128)
            S.dma("sp", cb_.ap, self.SCB[rows, t0:t0 + T], writes=[cb_.res])
            segs = [(s_ * 256, 256) for s_ in range(T // 256)] if kind == "p" else [(0, T)]
            S.op("dve", lambda e: e.memset(cc_.ap[:, 0:1], 0.0), writes=[cc_.res])
            S.op("dve", lambda e: e.memset(cc_.ap[:, T + 1:T + 2], 0.0), writes=[cc_.res])
            S.op("dve", lambda e: e.memset(ch_.ap[:, 0:1], 0.0), writes=[ch_.res])
            S.op("dve", lambda e: e.memset(ch_.ap[:, T + 1:T + 2], 0.0), writes=[ch_.res])
            lo = max(t0 - 1, sq0) if kind == "s" else t0
            hi = min(t0 + T + 1, sq0 + sqL) if kind == "s" else t0 + T
            S.dma("sp", cc_.ap[:, 1 + lo - t0:1 + hi - t0], self.SCC[rows, lo:hi], writes=[cc_.res])
            S.dma("sp", ch_.ap[:, 1 + lo - t0:1 + hi - t0], self.SCH[rows, lo:hi], writes=[ch_.res])
            S.op("dve", lambda e: e.tensor_tensor(out=u.ap, in0=cc_.ap, in1=ch_.ap, op=ALU.mult),
                 reads=[cc_.res, ch_.res], writes=[u.res])
            sk = 1 if kind == "p" else 0
            w0 = PC_SCW + cc * 3
            for (o0, ln) in segs:
                S.op("dve", lambda e: e.tensor_scalar(
                    out=a.ap[:, o0:o0 + ln], in0=u.ap[:, o0 + 1:o0 + 1 + ln], scalar1=pc.ap[:, w0 + 1:w0 + 2], scalar2=None, op0=ALU.mult),
                    reads=[u.res, pc.res], writes=[a.res])
                S.op("dve", lambda e: e.scalar_tensor_tensor(
                    out=a.ap[:, o0 + sk:o0 + ln], in0=u.ap[:, o0 + sk:o0 + ln], scalar=pc.ap[:, w0:w0 + 1],
                    in1=a.ap[:, o0 + sk:o0 + ln], op0=ALU.mult, op1=ALU.add), reads=[u.res, pc.res, a.res], writes=[a.res])
                S.op("dve", lambda e: e.scalar_tensor_tensor(
                    out=a.ap[:, o0:o0 + ln - sk], in0=u.ap[:, o0 + 2:o0 + 2 + ln - sk], scalar=pc.ap[:, w0 + 2:w0 + 3],
                    in1=a.ap[:, o0:o0 + ln - sk], op0=ALU.mult, op1=ALU.add), reads=[u.res, pc.res, a.res], writes=[a.res])
            S.op("dve", lambda e: e.tensor_tensor(out=yT2.ap[:, cc, :], in0=a.ap, in1=cb_.ap, op=ALU.mult),
                 reads=[a.res, cb_.res], writes=[yT2.res])

        ntb = (T + 511) // 512

        def epi_merge(b):
            def epi(fb, tb, ts, bk):
                g = gt.next()
                S.dma("sp", g.ap[:, 0:ts], self.G[b * D + fb * 128:b * D + (fb + 1) * 128, t0 + tb * 512:t0 + tb * 512 + ts], writes=[g.res])
                dst = merged.ap[:, fb, tb * 512:tb * 512 + ts]
                if b == 0:
                    S.op("dve", lambda e: e.tensor_tensor(out=dst, in0=bk.ap[:, 0:ts], in1=g.ap[:, 0:ts], op=ALU.mult),
                         reads=[bk.res, g.res], writes=[merged.res])
                    if tb == ntb - 1:
                        sc_step(fb)
                else:
                    t = tmp.next()
                    S.op("dve", lambda e: e.tensor_tensor(out=t.ap[:, 0:ts], in0=bk.ap[:, 0:ts], in1=g.ap[:, 0:ts], op=ALU.mult),
                         reads=[bk.res, g.res], writes=[t.res])
                    if b == 1:
                        S.op("dve", lambda e: e.tensor_tensor(out=dst, in0=dst, in1=t.ap[:, 0:ts], op=ALU.add),
                             reads=[merged.res, t.res], writes=[merged.res])
                    else:
                        S.op("dve", lambda e: e.tensor_tensor(out=yT.ap[:, fb, tb * 512:tb * 512 + ts], in0=dst, in1=t.ap[:, 0:ts], op=ALU.add),
                             reads=[merged.res, t.res], writes=[yT.res])
            return epi

        for b, (src, W) in enumerate(((yT, self.w_att_out), (yT1, self.w_ssd_out), (yT2, self.w_sc_out))):
            self.gemm_fm(src, 16, T, W[l], epi_merge(b), cache=True)

        def epi_o(blk, bw, m, bk):
            S.op("act", lambda e: e.activation(out=oacc.ap[:, m, blk:blk + bw], in_=bk.ap[:, 0:bw], func=AF.Copy), reads=[bk.res], writes=[oacc.res])

        self.gemm_tm(yT, 16, T, self.w_o[l], epi_o, cache=True)
        xn = A.ring(1, [128, D], BF16)
        junk = A.tl([128, D], BF16)
        tmpn = A.ring(1, [128, 8, 128], F32)
        st = A.ring(4, [128, 4], F32)
        xsrc = self.xin(l, t0, T)
        h2T = yT
        for m in range(nm):
            x, s = xt.next(), st.next()
            S.dma("sp", x.ap, xsrc[m * 128:(m + 1) * 128, :], writes=[x.res])
            o = Tl(oacc.ap[:, m, :], oacc.res)
            self.ssq(o, s, junk)
            self.rstd_from_ssq(s)
            S.op("dve", lambda e, o=o, s=s: e.scalar_tensor_tensor(out=o.ap, in0=o.ap, scalar=s.ap[:, 3:4], in1=Gm.ap, op0=ALU.mult, op1=ALU.mult),
                 reads=[oacc.res, s.res, Gm.res], writes=[oacc.res])
            S.op("dve", lambda e, o=o, x=x: e.tensor_tensor(out=x.ap, in0=o.ap, in1=x.ap, op=ALU.add), reads=[oacc.res, x.res], writes=[x.res])
            S.dma("act", self.XM[t0 + m * 128:t0 + (m + 1) * 128, :], x.ap, reads=[x.res])
            self.norm_T(x, h2T, m, Acol, Bcol, st.next(), junk, xn.next(), tmpn.next())
        self.end_scope()
        actT = A.tl([128, 44, T], BF16)
        sg = A.ring(2, [128, 512], F32)
        Wgu = self.w_gate_up[l]
        for fb4 in range(0, 44, 4):
            nf = min(4, 44 - fb4)
            wg, wgr = self.wload(Wgu[:, fb4 * 128:(fb4 + nf) * 128], 16, nf * 128, True)
            wu, wur = self.wload(Wgu[:, DFF + fb4 * 128:DFF + (fb4 + nf) * 128], 16, nf * 128, True)
            for f in range(nf):
                for tb in range((T + 511) // 512):
                    ts = min(512, T - tb * 512)
                    bg, bu = self.bank(), self.bank()
                    for (bk, wv, wr) in ((bg, wg, wgr), (bu, wu, wur)):
                        for kc in range(16):
                            S.op("pe", lambda e, bk=bk, wv=wv, kc=kc, f=f, tb=tb, ts=ts: e.matmul(
                                bk.ap[:, 0:ts], lhsT=wv[:, kc, f * 128:(f + 1) * 128], rhs=h2T.ap[:, kc, tb * 512:tb * 512 + ts],
                                start=(kc == 0), stop=(kc == 15)), reads=[wr, h2T.res], writes=[bk.res])
                    sgt = sg.next()
                    S.op("act", lambda e, sgt=sgt, bg=bg, ts=ts: e.activation(out=sgt.ap[:, 0:ts], in_=bg.ap[:, 0:ts], func=AF.Silu), reads=[bg.res], writes=[sgt.res])
                    S.op("dve", lambda e, sgt=sgt, bu=bu, ts=ts, fb=fb4 + f, tb=tb: e.tensor_tensor(
                        out=actT.ap[:, fb, tb * 512:tb * 512 + ts], in0=bu.ap[:, 0:ts], in1=sgt.ap[:, 0:ts], op=ALU.mult),
                        reads=[bu.res, sgt.res], writes=[actT.res])
        o2T = A.ring(2, [128, 512], F32)

        def wload_down(fb):
            return self.wload(self.w_down[l][:, fb * 128:(fb + 1) * 128], 44, 128, True)

        for fb in range(16):
            wv, wr = wload_down(fb)
            for tb in range((T + 511) // 512):
                ts = min(512, T - tb * 512)
                bk = self.bank()
                for kc in range(44):
                    S.op("pe", lambda e, bk=bk, wv=wv, kc=kc, tb=tb, ts=ts: e.matmul(
                        bk.ap[:, 0:ts], lhsT=wv[:, kc, :], rhs=actT.ap[:, kc, tb * 512:tb * 512 + ts], start=(kc == 0), stop=(kc == 43)),
                        reads=[wr, actT.res], writes=[bk.res])
                ot = o2T.next()
                S.op("act", lambda e, ot=ot, bk=bk, ts=ts: e.activation(out=ot.ap[:, 0:ts], in_=bk.ap[:, 0:ts], func=AF.Copy), reads=[bk.res], writes=[ot.res])
                b2 = self.bank()
                for s_ in range(ts // 128):
                    S.op("pe", lambda e, b2=b2, ot=ot, s_=s_: e.transpose(out=b2.ap[:, s_ * 128:(s_ + 1) * 128], in_=ot.ap[:, s_ * 128:(s_ + 1) * 128], identity=idf),
                         reads=[ot.res, self.cf.res], writes=[b2.res])
                m0 = tb * 4
                S.op("dve", lambda e, b2=b2, ts=ts, m0=m0, fb=fb: e.tensor_scalar(
                    out=oacc.ap[:, m0:m0 + ts // 128, fb * 128:(fb + 1) * 128], in0=b2.ap[:, 0:ts].rearrange("p (s d) -> p s d", s=ts // 128),
                    scalar1=1.0, scalar2=None, op0=ALU.mult),
                    reads=[b2.res], writes=[oacc.res])
        A.mark()
        xt = A.ring(2, [128, D], F32)
        junk = A.tl([128, D], BF16)
        st = A.ring(2, [128, 4], F32)
        xdst = self.xout(l, t0, T)
        for m in range(nm):
            x, s = xt.next(), st.next()
            S.dma("sp", x.ap, self.XM[t0 + m * 128:t0 + (m + 1) * 128, :], writes=[x.res])
            o = Tl(oacc.ap[:, m, :], oacc.res)
            self.ssq(o, s, junk)
            self.rstd_from_ssq(s)
            S.op("dve", lambda e, o=o, s=s: e.scalar_tensor_tensor(out=o.ap, in0=o.ap, scalar=s.ap[:, 3:4], in1=Gf.ap, op0=ALU.mult, op1=ALU.mult),
                 reads=[oacc.res, s.res, Gf.res], writes=[oacc.res])
            S.op("dve", lambda e, o=o, x=x: e.tensor_tensor(out=x.ap, in0=o.ap, in1=x.ap, op=ALU.add), reads=[oacc.res, x.res], writes=[x.res])
            S.dma("act", xdst[m * 128:(m + 1) * 128, :], x.ap, reads=[x.res])
        A.release()
        A.release()


def make_consts(LS):
    t = np.arange(128)
    cst = np.zeros((C_N, 128, 128), np.float32)
    cst[C_ID] = np.eye(128)
    cst[C_UF] = (t[:, None] <= t[None, :])
    cst[C_UB] = (t[:, None] >= t[None, :])
    cst[C_NUF] = -cst[C_UF]
    cst[C_NUB] = -cst[C_UB]
    cst[C_ONES] = 1.0
    cst[C_MBF] = np.where(t[None, :] >= t[:, None], 0.0, NEG)
    cst[C_MBB] = np.where(t[None, :] <= t[:, None], 0.0, NEG)
    cst[C_MPREV] = (t[:, None] >= t[None, :])
    cst[C_MNEXT] = (t[:, None] <= t[None, :])
    pm = np.zeros((128, 128), np.float32)
    for i in range(64):
        pm[2 * i + 1, 2 * i] = 1.0
        pm[2 * i, 2 * i + 1] = 1.0
    cst[C_PMAT] = pm
    cst = np.ascontiguousarray(cst.transpose(1, 0, 2).reshape(128, C_N * 128))
    pos = np.arange(LS)
    row = (pos // GRID_W).astype(np.float32)
    col = (pos % GRID_W).astype(np.float32)
    n_pairs = HD // 4
    inv = (10000.0 ** (-np.arange(n_pairs, dtype=np.float32) / n_pairs)).astype(np.float32)
    ang = np.concatenate([row[:, None] * inv, col[:, None] * inv], axis=-1).astype(np.float32)
    cos = np.cos(ang).astype(np.float32)
    sin = np.sin(ang).astype(np.float32)
    ropec = np.zeros((128, LS), np.float32)
    ropes = np.zeros((128, LS), np.float32)
    ropec[0::2] = cos.T
    ropec[1::2] = cos.T
    ropes[0::2] = -sin.T
    ropes[1::2] = sin.T
    return cst, ropec, ropes


def col16(v):
    return np.ascontiguousarray(v.reshape(-1, 128).T)


def make_pcol(inp, L):
    pcol = np.zeros((L, 128, PC_N), np.float32)
    for l in range(L):
        pcol[l, :, PC_GPM:PC_GPM + 16] = col16(inp["g_pre_mix"][l])
        pcol[l, :, PC_GPF:PC_GPF + 16] = col16(inp["g_pre_ffn"][l])
        cw = inp["ssm_conv_w"][l]
        for k in range(5):
            pcol[l, :, PC_CW + k:PC_CW + 120:5] = col16(cw[k])
        pcol[l, :, PC_CB:PC_CB + 24] = col16(inp["ssm_conv_b"][l])
        sw = inp["sc_conv_w"][l]
        for k in range(3):
            pcol[l, :, PC_SCW + k:PC_SCW + 48:3] = col16(sw[k])
        pcol[l, :, PC_BMOD:PC_BMOD + 96] = col16(inp["b_mod"][l])
    return pcol


_NC_CACHE = {}
SIM_HOOK = None


def get_nc(cfg_key):
    if cfg_key not in _NC_CACHE:
        _NC_CACHE[cfg_key] = KB(Cfg(*cfg_key))
    return _NC_CACHE[cfg_key]


def run(inp, NPS, LS, DEPTH, TT, TT5, n_cores, debug=False, stop=99):
    f = lambda a: np.ascontiguousarray(np.asarray(a, dtype=np.float32))
    inp = {k: f(v) for k, v in inp.items()}
    kb = KB(Cfg(NPS, LS, DEPTH, TT, TT5, debug, stop))
    cst, ropec, ropes = make_consts(LS)
    pcol = make_pcol(inp, DEPTH)
    nb = inp["x_sample"].shape[0]
    shared = {
        "pcol": pcol, "cst": cst, "ropec": ropec, "ropes": ropes,
        "w_mod": inp["w_mod"], "b_mod": inp["b_mod"], "w_in": inp["w_in"], "sink": inp["sink"],
        "ssm_dt_bias": inp["ssm_dt_bias"].reshape(DEPTH, 64), "ssm_a_log": inp["ssm_a_log"].reshape(DEPTH, 64),
        "ssm_d": inp["ssm_d"], "ssm_norm_g": inp["ssm_norm_g"], "w_att_out": inp["w_att_out"],
        "w_ssd_out": inp["w_ssd_out"], "w_sc_out": inp["w_sc_out"], "w_o": inp["w_o"],
        "g_post_mix": inp["g_post_mix"], "g_post_ffn": inp["g_post_ffn"], "w_gate_up": inp["w_gate_up"],
        "w_down": inp["w_down"],
    }
    in_maps = []
    for i in range(n_cores):
        b = i % nb
        cond = np.stack([inp["c_ctx"], inp["c"][b]], axis=0)
        condT = np.ascontiguousarray(cond.reshape(2, 16, 128).transpose(2, 1, 0))
        m = dict(shared)
        m["xp"] = np.ascontiguousarray(inp["x_prompt"][i * NPS:(i + 1) * NPS].reshape(NPS * 256, D))
        m["xs"] = inp["x_sample"][b]
        m["ck"] = np.ascontiguousarray(inp["cache_k"][b].reshape(DEPTH, 256, 512))
        m["cv"] = np.ascontiguousarray(inp["cache_v"][b].reshape(DEPTH, 256, 512))
        m["st"] = np.ascontiguousarray(inp["state_ssm"][b].reshape(DEPTH, 2, SH * SP_, SN))
        m["condT"] = condT
        in_maps.append(m)
    if SIM_HOOK is not None:
        R = SIM_HOOK(kb.nc, in_maps)
    else:
        if _os.environ.get("TRACE") == "1":
            res = run_bass_kernel_spmd(kb.nc, in_maps, core_ids=list(range(n_cores)), trace=True)
            print("EXEC_NS", res.exec_time_ns)
        else:
            res = run_bass_kernel_spmd(kb.nc, in_maps, core_ids=list(range(n_cores)))
        R = res.results
    y_prompt = np.concatenate([R[i]["yp"].reshape(NPS, 256, D) for i in range(n_cores)], axis=0)
    y_sample = np.stack([R[b]["ys"] for b in range(nb)], axis=0)
    nck = np.concatenate([R[i]["ock"].reshape(NPS, DEPTH, 256, NKV, HD) for i in range(n_cores)], axis=0)
    ncv = np.concatenate([R[i]["ocv"].reshape(NPS, DEPTH, 256, NKV, HD) for i in range(n_cores)], axis=0)
    nst = np.concatenate([R[i]["ost"].reshape(NPS, DEPTH, 2, SH, SP_, SN) for i in range(n_cores)], axis=0)
    outs = (y_prompt, y_sample, nck, ncv, nst)
    if debug:
        return outs, R
    return outs


def kernel(**inputs):
    return run(inputs, NPS=4, LS=4096, DEPTH=2, TT=1024, TT5=512, n_cores=8)
```

```python
import math
import numpy as np
import concourse.bass as bass
import concourse.mybir as mybir
from concourse.bass_utils import run_bass_kernel_spmd

F32 = mybir.dt.float32
BF16 = mybir.dt.bfloat16
AF = mybir.ActivationFunctionType
ALU = mybir.AluOpType

D = 2048
NH, NKV, HD = 16, 4, 128
SH, SP_, SN, SG = 32, 64, 128, 4
DFF = 5632
DIN = 20544
O_Q, O_K, O_V, O_Z, O_X, O_DT, O_SCB, O_SCC, O_SCH, O_G = 0, 2048, 2560, 3072, 5120, 8192, 8256, 10304, 12352, 14400
EPS = 1e-6
ATT_SCALE = HD ** -0.5
GRID_W = 64
NEG = -30000.0
import os as _os
POOL_ENG = _os.environ.get('POOL_ENG', 'dve')

PC_GPM, PC_GPF, PC_CW, PC_CB, PC_SCW, PC_BMOD, PC_N = 0, 16, 32, 152, 176, 224, 320
C_ID, C_UF, C_UB, C_NUF, C_NUB, C_ONES, C_MBF, C_MBB, C_MPREV, C_MNEXT, C_PMAT, C_N = range(12)


class Tok:
    __slots__ = ("key", "sem", "val", "eng")

    def __init__(self, key, sem, val, eng):
        self.key, self.sem, self.val, self.eng = key, sem, val, eng


class Res:
    __slots__ = ("w", "r")

    def __init__(self):
        self.w = None
        self.r = {}


class Tl:
    __slots__ = ("ap", "res")

    def __init__(self, ap, res=None):
        self.ap = ap
        self.res = res if res is not None else Res()


class Ring:
    def __init__(self, items):
        self.items = items
        self.i = 0

    def next(self):
        t = self.items[self.i % len(self.items)]
        self.i += 1
        return t


class _Rec:
    def __getattr__(self, name):
        def f(*a, **k):
            return (name, a, k)
        return f


_REC = _Rec()


class EngState:
    def __init__(self, name, sem, skip_self):
        self.name, self.sem, self.skip_self = name, sem, skip_self
        self.count = 0
        self.waited = {}
        self.stream = []
        self.dma_sems = []
        self.dma_n = 0


class Sched:
    ENGS = ("pe", "act", "dve", "pool", "sp")

    def __init__(self, nc, n_dma_slots=8):
        self.nc = nc
        self.ctx = []
        self.E = {}
        for name in self.ENGS:
            cm = nc.semaphore("sem_" + name)
            self.ctx.append(cm)
            self.E[name] = EngState(name, cm.__enter__(), skip_self=(name == "pe"))
        for name in ("sp", "pool", "act"):
            for i in range(n_dma_slots):
                cm = nc.semaphore(f"dsem_{name}_{i}")
                self.ctx.append(cm)
                self.E[name].dma_sems.append(cm.__enter__())
        self.nslot = n_dma_slots

    def _deps(self, E, reads, writes):
        best = {}

        def add(t):
            if t is not None and best.get(t.key, (0, None))[0] < t.val:
                best[t.key] = (t.val, t)

        for r in reads:
            add(r.w)
        for w in writes:
            add(w.w)
            for t in w.r.values():
                add(t)
        waits = []
        for key, (val, t) in best.items():
            if t.eng is E and E.skip_self:
                continue
            if E.waited.get(key, 0) >= val:
                continue
            E.waited[key] = val
            waits.append((t.sem, val))
        return waits

    @staticmethod
    def _update(tok, reads, writes):
        for r in reads:
            old = r.r.get(tok.key)
            if old is None or old.val < tok.val:
                r.r[tok.key] = tok
        for w in writes:
            w.w = tok
            w.r = {}

    def op(self, eng, fn, reads=(), writes=()):
        E = self.E[eng]
        waits = self._deps(E, reads, writes)
        E.count += 1
        tok = Tok(("e", eng), E.sem, E.count, E)
        name, a, k = fn(_REC)
        E.stream.append((waits, lambda e, name=name, a=a, k=k: getattr(e, name)(*a, **k), (E.sem, 1)))
        self._update(tok, reads, writes)
        return tok

    def dma(self, q, out, in_, reads=(), writes=(), **kw):
        E = self.E[q]
        waits = self._deps(E, reads, writes)
        slot = E.dma_n % self.nslot
        gen = E.dma_n // self.nslot
        E.dma_n += 1
        sem = E.dma_sems[slot]
        key = ("d", q, slot)
        if gen > 0 and E.waited.get(key, 0) < 16 * gen:
            waits.append((sem, 16 * gen))
            E.waited[key] = 16 * gen
        tok = Tok(key, sem, 16 * (gen + 1), None)

        def fn(e, out=out, in_=in_, kw=kw):
            return e.dma_start(out=out, in_=in_, **kw)

        E.stream.append((waits, fn, (sem, 16)))
        self._update(tok, reads, writes)
        return tok

    def barrier(self):
        toks = []
        for name, E in self.E.items():
            if E.count > 0:
                toks.append(Tok(("e", name), E.sem, E.count, E))
            for slot in range(min(E.dma_n, self.nslot)):
                n_on = (E.dma_n - 1 - slot) // self.nslot + 1
                toks.append(Tok(("d", name, slot), E.dma_sems[slot], 16 * n_on, None))
        for name, E in self.E.items():
            for t in toks:
                if t.eng is E:
                    continue
                if E.waited.get(t.key, 0) >= t.val:
                    continue
                E.waited[t.key] = t.val
                E.stream.append(([(t.sem, t.val)], None, None))

    def emit(self):
        def run(eng, E):
            for waits, fn, inc in E.stream:
                for sem, val in waits:
                    eng.wait_ge(sem, val)
                if fn is not None:
                    fn(eng).then_inc(inc[0], inc[1])

        S = self
        with self.nc.Block() as block:
            @block.tensor
            def _(e):
                run(e, S.E["pe"])

            @block.scalar
            def _(e):
                run(e, S.E["act"])

            @block.vector
            def _(e):
                run(e, S.E["dve"])

            @block.gpsimd
            def _(e):
                run(e, S.E["pool"])

            @block.sync
            def _(e):
                run(e, S.E["sp"])


class Arena:
    def __init__(self, nc, nbytes):
        self.words = nbytes // 4
        self.t = nc.alloc_sbuf_tensor("arena", [128, self.words], F32)
        self.off = 0
        self.marks = []

    def alloc(self, shape, dtype, P=128):
        esz = 2 if dtype == BF16 else 4
        n = int(np.prod(shape[1:]))
        nwords = ((n * esz + 3) // 4 + 15) // 16 * 16
        off = self.off
        assert off + nwords <= self.words, f"SBUF overflow {off}+{nwords}>{self.words} {shape}"
        self.off += nwords
        a = self.t[0:shape[0], off:off + nwords]
        if dtype != F32:
            a = a.bitcast(dtype)
        a = a[:, 0:n]
        if len(shape) == 3:
            a = a.rearrange("p (a b) -> p a b", a=shape[1])
        elif len(shape) == 4:
            a = a.rearrange("p (a b c) -> p a b c", a=shape[1], b=shape[2])
        return a

    def tl(self, shape, dtype):
        return Tl(self.alloc(shape, dtype))

    def ring(self, n, shape, dtype):
        return Ring([self.tl(shape, dtype) for _ in range(n)])

    def mark(self):
        self.marks.append(self.off)

    def release(self):
        self.off = self.marks.pop()


def bc_last(a, n):
    return bass.AP(a.tensor, a.offset, [list(x) for x in a.ap] + [[0, n]])


def bc_mid(a, n):
    ap = [list(x) for x in a.ap]
    return bass.AP(a.tensor, a.offset, [ap[0], [0, n]] + ap[1:])


def dram_bc(a, P=128):
    ap = [list(x) for x in a.ap]
    return bass.AP(a.tensor, a.offset, [[0, P], ap[-1]])


class Cfg:
    def __init__(self, NPS=4, LS=4096, DEPTH=2, TT=1024, TT5=512, debug=False, stop=99):
        self.NPS, self.LS, self.DEPTH, self.TT, self.TT5, self.debug = NPS, LS, DEPTH, TT, TT5, debug
        self.stop = stop
        self.LP = 256
        self.NP = NPS * 256
        self.NT = self.NP + LS


class KB:
    def __init__(self, cfg):
        self.cfg = cfg
        c = cfg
        nc = self.nc = bass.Bass("TRN2", target_bir_lowering=False)
        L = c.DEPTH

        def inp(name, shape):
            return nc.dram_tensor(name, list(shape), F32, kind="ExternalInput").ap()

        def outp(name, shape):
            return nc.dram_tensor(name, list(shape), F32, kind="ExternalOutput").ap()

        self.dbg = {}

        def scr(name, shape, dt=BF16):
            if c.debug:
                a = nc.dram_tensor(name, list(shape), dt, kind="ExternalOutput").ap()
                self.dbg[name] = a
                return a
            return nc.dram_tensor(name, list(shape), dt).ap()

        self.xp = inp("xp", [c.NP, D])
        self.xs = inp("xs", [c.LS, D])
        self.ck = inp("ck", [L, 256, 512])
        self.cv = inp("cv", [L, 256, 512])
        self.st = inp("st", [L, 2, SH * SP_, SN])
        self.condT = inp("condT", [128, 16, 2])
        self.pcol = inp("pcol", [L, 128, PC_N])
        self.cst = inp("cst", [128, C_N * 128])
        self.ropec = inp("ropec", [128, c.LS])
        self.ropes = inp("ropes", [128, c.LS])
        self.w_mod = inp("w_mod", [L, D, 6 * D])
        self.b_mod = inp("b_mod", [L, 6 * D])
        self.w_in = inp("w_in", [L, D, DIN])
        self.sink = inp("sink", [L, NH])
        self.dt_bias = inp("ssm_dt_bias", [L, 64])
        self.a_log = inp("ssm_a_log", [L, 64])
        self.ssm_d = inp("ssm_d", [L, SH])
        self.ssm_norm_g = inp("ssm_norm_g", [L, D])
        self.w_att_out = inp("w_att_out", [L, D, D])
        self.w_ssd_out = inp("w_ssd_out", [L, D, D])
        self.w_sc_out = inp("w_sc_out", [L, D, D])
        self.w_o = inp("w_o", [L, D, D])
        self.g_post_mix = inp("g_post_mix", [L, D])
        self.g_post_ffn = inp("g_post_ffn", [L, D])
        self.w_gate_up = inp("w_gate_up", [L, D, 2 * DFF])
        self.w_down = inp("w_down", [L, DFF, D])

        self.yp = outp("yp", [c.NP, D])
        self.ys = outp("ys", [c.LS, D])
        self.ock = outp("ock", [c.NPS, L, 256, 512])
        self.ocv = outp("ocv", [c.NPS, L, 256, 512])
        self.ost = outp("ost", [c.NPS, L, 2, SH * SP_, SN])

        NT = c.NT
        self.X1 = scr("X1", [NT, D], F32)
        self.XM = scr("XM", [NT, D], F32)
        self.MOD = scr("MOD", [L, 2, 6 * D], F32)
        self.QT = scr("QT", [D, NT])
        self.KT = scr("KT", [512, NT])
        self.V = scr("V", [NT, 512])
        self.SZ = scr("SZ", [NT, D])
        self.XBC = scr("XBC", [3072, NT])
        self.XCT = scr("XCT", [1024, NT])
        self.XTOK = scr("XTOK", [NT, 2560])
        self.DT = scr("DT", [NT, 64], F32)
        self.SCB = scr("SCB", [D, NT])
        self.SCC = scr("SCC", [D, NT])
        self.SCH = scr("SCH", [D, NT])
        self.G = scr("G", [3 * D, NT])
        self.YATT = scr("YATT", [D, NT])
        self.YSSD = scr("YSSD", [D, NT])
        self.YF = scr("YF", [NT, D], F32)
        self.WC = nc.dram_tensor("WC", [64, 128, 8192], BF16).ap()

        self.S = Sched(nc)
        self.A = Arena(nc, 204 * 1024)
        self.banks = Ring([Tl(nc.alloc_psum_tensor(f"ps{i}", [128, 512], F32)[:]) for i in range(8)])
        self.build()

    def bank(self):
        return self.banks.next()

    def end_scope(self):
        self.A.release()
        self.S.barrier()

    def xin(self, l, t0, T):
        c = self.cfg
        if l == 0:
            return self.xp[t0:t0 + T, :] if t0 < c.NP else self.xs[t0 - c.NP:t0 - c.NP + T, :]
        return self.X1[t0:t0 + T, :]

    def xout(self, l, t0, T):
        c = self.cfg
        if l == c.DEPTH - 1:
            return self.yp[t0:t0 + T, :] if t0 < c.NP else self.ys[t0 - c.NP:t0 - c.NP + T, :]
        return self.X1[t0:t0 + T, :]

    def cblk(self, i, dt=F32):
        a = (self.cf if dt == F32 else self.cb)
        return a.ap[:, i * 128:(i + 1) * 128]

    def wload(self, w_ap, KC, bw, cache=False):
        wt = self.wbufs.next()
        n = KC * bw
        v = wt.ap[:, 0:n].rearrange("p (c n) -> p c n", c=KC)
        if not cache:
            self.S.dma("pool", v, w_ap.rearrange("(c p) n -> p c n", p=128), writes=[wt.res])
            return v, wt.res
        slot = self.wc_n
        self.wc_n += 1
        if self.wc_first:
            self.S.dma("pool", v, w_ap.rearrange("(c p) n -> p c n", p=128), writes=[wt.res])
            self.S.dma("sp", self.WC[slot, :, 0:n], wt.ap[:, 0:n], reads=[wt.res])
        else:
            self.S.dma("pool", wt.ap[:, 0:n], self.WC[slot, :, 0:n], writes=[wt.res])
        return v, wt.res

    def gemm_fm(self, actT, KC, T, w_ap, epi, blkw=512, cache=False):
        S = self.S
        ncols = w_ap.shape[1]
        for blk in range(0, ncols, blkw):
            bw = min(blkw, ncols - blk)
            wv, wres = self.wload(w_ap[:, blk:blk + bw], KC, bw, cache)
            for f in range(bw // 128):
                for tb in range((T + 511) // 512):
                    ts = min(512, T - tb * 512)
                    bk = self.bank()
                    for kc in range(KC):
                        S.op("pe", lambda e, bk=bk, wv=wv, kc=kc, f=f, tb=tb, ts=ts: e.matmul(
                            bk.ap[:, 0:ts], lhsT=wv[:, kc, f * 128:(f + 1) * 128],
                            rhs=actT.ap[:, kc, tb * 512:tb * 512 + ts], start=(kc == 0), stop=(kc == KC - 1)),
                            reads=[wres, actT.res], writes=[bk.res])
                    epi(blk // 128 + f, tb, ts, bk)

    def gemm_tm(self, actT, KC, T, w_ap, epi, blkw=512, cache=False):
        S = self.S
        ncols = w_ap.shape[1]
        for blk in range(0, ncols, blkw):
            bw = min(blkw, ncols - blk)
            wv, wres = self.wload(w_ap[:, blk:blk + bw], KC, bw, cache)
            for m in range(T // 128):
                bk = self.bank()
                for kc in range(KC):
                    S.op("pe", lambda e, bk=bk, wv=wv, kc=kc, m=m, bw=bw: e.matmul(
                        bk.ap[:, 0:bw], lhsT=actT.ap[:, kc, m * 128:(m + 1) * 128],
                        rhs=wv[:, kc, 0:bw], start=(kc == 0), stop=(kc == KC - 1)),
                        reads=[wres, actT.res], writes=[bk.res])
                epi(blk, bw, m, bk)

    def rstd_from_ssq(self, s):
        S = self.S
        S.op("dve", lambda e: e.tensor_scalar(out=s.ap[:, 1:2], in0=s.ap[:, 0:1], scalar1=1.0 / D, scalar2=EPS,
                                              op0=ALU.mult, op1=ALU.add), reads=[s.res], writes=[s.res])
        S.op("act", lambda e: e.activation(out=s.ap[:, 2:3], in_=s.ap[:, 1:2], func=AF.Sqrt), reads=[s.res], writes=[s.res])
        S.op("dve", lambda e: e.reciprocal(out=s.ap[:, 3:4], in_=s.ap[:, 2:3]), reads=[s.res], writes=[s.res])

    def ssq(self, x, s, junk):
        S = self.S
        S.op("dve", lambda e: e.memset(s.ap[:, 0:1], 0.0), writes=[s.res])
        S.op("act", lambda e: e.activation(out=junk.ap, in_=x.ap, func=AF.Square, accum_out=s.ap[:, 0:1]),
             reads=[x.res], writes=[junk.res, s.res])

    def norm_T(self, x, hT, m, Acol, Bcol, s, junk, xn, tmp):
        S = self.S
        self.ssq(x, s, junk)
        self.rstd_from_ssq(s)
        S.op("act", lambda e: e.activation(out=xn.ap, in_=x.ap, func=AF.Identity, scale=s.ap[:, 3:4]),
             reads=[x.res, s.res], writes=[xn.res])
        idb = self.cblk(C_ID, BF16)
        for j in range(2):
            bk = self.bank()
            pb = bk.ap.bitcast(BF16)
            for kk in range(8):
                kc = j * 8 + kk
                S.op("pe", lambda e, pb=pb, kk=kk, kc=kc: e.transpose(
                    out=pb[:, kk * 128:(kk + 1) * 128], in_=xn.ap[:, kc * 128:(kc + 1) * 128], identity=idb),
                    reads=[xn.res, self.cb.res], writes=[bk.res])
            pv = pb.rearrange("p (a b) -> p a b", a=8)
            S.op("dve", lambda e, pv=pv, j=j: e.tensor_tensor(
                out=tmp.ap, in0=pv, in1=bc_last(Acol.ap[:, j * 8:(j + 1) * 8], 128), op=ALU.mult),
                reads=[bk.res, Acol.res], writes=[tmp.res])
            S.op("dve", lambda e, j=j: e.tensor_tensor(
                out=hT.ap[:, j * 8:(j + 1) * 8, m * 128:(m + 1) * 128], in0=tmp.ap,
                in1=bc_last(Bcol.ap[:, j * 8:(j + 1) * 8], 128), op=ALU.add),
                reads=[tmp.res, Bcol.res], writes=[hT.res])

    def modcols(self, l, r, i_shift, i_scale, gcol0):
        S, A = self.S, self.A
        Acol, Bcol = A.tl([128, 16], F32), A.tl([128, 16], F32)
        mc = self.modcol
        S.op("dve", lambda e: e.tensor_scalar(out=Acol.ap, in0=mc.ap[:, i_scale * 16:(i_scale + 1) * 16, r], scalar1=1.0,
                                              scalar2=None, op0=ALU.add), reads=[mc.res], writes=[Acol.res])
        S.op("dve", lambda e: e.tensor_tensor(out=Acol.ap, in0=Acol.ap, in1=self.pc.ap[:, gcol0:gcol0 + 16], op=ALU.mult),
             reads=[Acol.res, self.pc.res], writes=[Acol.res])
        S.op("dve", lambda e: e.tensor_copy(out=Bcol.ap, in_=mc.ap[:, i_shift * 16:(i_shift + 1) * 16, r]),
             reads=[mc.res], writes=[Bcol.res])
        return Acol, Bcol

    def gate_row(self, l, r, i_gate, g_ap, Gt, t2):
        S, A = self.S, self.A
        S.dma("sp", Gt.ap, dram_bc(self.MOD[l, r, i_gate * D:(i_gate + 1) * D]), writes=[Gt.res])
        S.dma("sp", t2.ap, dram_bc(g_ap[l, :]), writes=[t2.res])
        S.op("dve", lambda e: e.tensor_tensor(out=Gt.ap, in0=Gt.ap, in1=t2.ap, op=ALU.mult), reads=[Gt.res, t2.res], writes=[Gt.res])
        return Gt

    def build(self):
        c, S, A = self.cfg, self.S, self.A
        self.cf = A.tl([128, C_N * 128], F32)
        self.cb = A.tl([128, C_N * 128], BF16)
        S.dma("sp", self.cf.ap, self.cst, writes=[self.cf.res])
        S.dma("pool", self.cb.ap, self.cst, writes=[self.cb.res])
        self.wbufs = None
        self.wc_n = 0
        self.wc_first = False
        self.pc = A.tl([128, PC_N], F32)
        self.modcol = A.tl([128, 96, 2], F32)
        self.tiles = []
        if c.NP > 0:
            for t0 in range(0, c.NP, c.TT):
                self.tiles.append((t0, min(c.TT, c.NP - t0), 0, "p"))
        for t0 in range(0, c.LS, c.TT):
            self.tiles.append((c.NP + t0, min(c.TT, c.LS - t0), 1, "s"))
        self.seqs = [(i * 256, 256, "p", i) for i in range(c.NPS)] + [(c.NP, c.LS, "s", 0)]
        for l in range(c.DEPTH):
            self.phase0(l)
            S.barrier()
            if c.stop < 1:
                break
            for tile in self.tiles:
                self.phase1(l, tile)
                S.barrier()
            if c.stop < 2:
                break
            self.attention(l)
            if c.stop < 3:
                break
            self.ssd_conv(l)
            S.barrier()
            if c.stop < 4:
                break
            self.ssd_sweep(l, 0)
            S.barrier()
            self.ssd_sweep(l, 1)
            S.barrier()
            if c.stop < 5:
                break
            self.wc_first = True
            for (t0, T, r, kind) in self.tiles:
                for u0 in range(0, T, c.TT5):
                    self.phase56(l, (t0 + u0, min(c.TT5, T - u0), r, kind))
                    S.barrier()
                    self.wc_first = False
        S.barrier()
        S.emit()

    def phase0(self, l):
        S, A = self.S, self.A
        A.mark()
        self.wbufs = A.ring(3, [128, 16 * 512], BF16)
        S.dma("sp", self.pc.ap, self.pcol[l], writes=[self.pc.res])
        ct = A.tl([128, 16, 2], F32)
        sc = A.tl([128, 16, 2], BF16)
        S.dma("sp", ct.ap, self.condT, writes=[ct.res])
        S.op("act", lambda e: e.activation(out=sc.ap, in_=ct.ap, func=AF.Silu), reads=[ct.res], writes=[sc.res])
        brow = A.ring(2, [2, 512], F32)
        orow = A.ring(2, [2, 512], F32)
        mc = self.modcol
        for n in range(24):
            wv, wres = self.wload(self.w_mod[l][:, n * 512:(n + 1) * 512], 16, 512)
            bk = self.bank()
            for kc in range(16):
                S.op("pe", lambda e, bk=bk, wv=wv, kc=kc: e.matmul(bk.ap[0:2, :], lhsT=sc.ap[:, kc, :], rhs=wv[:, kc, :],
                                                                 start=(kc == 0), stop=(kc == 15)),
                     reads=[wres, sc.res], writes=[bk.res])
            br = brow.next()
            S.dma("sp", br.ap, dram_bc(self.b_mod[l, n * 512:(n + 1) * 512], 2), writes=[br.res])
            orr = orow.next()
            S.op("dve", lambda e, bk=bk, br=br, orr=orr: e.tensor_tensor(out=orr.ap, in0=bk.ap[0:2, :], in1=br.ap, op=ALU.add),
                 reads=[bk.res, br.res], writes=[orr.res])
            S.dma("act", self.MOD[l, :, n * 512:(n + 1) * 512], orr.ap, reads=[orr.res])
            bk2 = self.bank()
            for f in range(4):
                for kc in range(16):
                    S.op("pe", lambda e, bk2=bk2, wv=wv, kc=kc, f=f: e.matmul(
                        bk2.ap[:, f * 2:f * 2 + 2], lhsT=wv[:, kc, f * 128:(f + 1) * 128], rhs=sc.ap[:, kc, :],
                        start=(kc == 0 and f == 0), stop=(kc == 15 and f == 3)), reads=[wres, sc.res], writes=[bk2.res])
            S.op("dve", lambda e, bk2=bk2, n=n: e.tensor_tensor(
                out=mc.ap[:, n * 4:(n + 1) * 4, :], in0=bk2.ap[:, 0:8].rearrange("p (a b) -> p a b", a=4),
                in1=bc_last(self.pc.ap[:, PC_BMOD + n * 4:PC_BMOD + (n + 1) * 4], 2), op=ALU.add),
                reads=[bk2.res, self.pc.res], writes=[mc.res])
        A.release()

    def phase1(self, l, tile):
        c, S, A = self.cfg, self.S, self.A
        t0, T, r, kind = tile
        A.mark()
        self.wbufs = A.ring(3, [128, 16 * 512], BF16)
        hT = A.tl([128, 16, T], BF16)
        Acol, Bcol = self.modcols(l, r, 0, 1, PC_GPM)
        A.mark()
        xt = A.ring(2, [128, D], F32)
        xn = A.ring(2, [128, D], BF16)
        junk = A.tl([128, D], BF16)
        tmp = A.ring(2, [128, 8, 128], F32)
        st = A.ring(2, [128, 4], F32)
        xsrc = self.xin(l, t0, T)
        for m in range(T // 128):
            x = xt.next()
            S.dma("sp", x.ap, xsrc[m * 128:(m + 1) * 128, :], writes=[x.res])
            self.norm_T(x, hT, m, Acol, Bcol, st.next(), junk, xn.next(), tmp.next())
        self.end_scope()
        if kind == "s":
            s0 = t0 - c.NP
            cos, sin = A.tl([128, T], F32), A.tl([128, T], F32)
            S.dma("sp", cos.ap, self.ropec[:, s0:s0 + T], writes=[cos.res])
            S.dma("sp", sin.ap, self.ropes[:, s0:s0 + T], writes=[sin.res])
        stg = A.ring(4, [128, 512], BF16)
        stq = A.ring(2, [128, 512], BF16)
        f32s = A.ring(4, [128, 512], F32)
        W = self.w_in[l]
        pm = self.cblk(C_PMAT, BF16)

        def epi_copy(dst, func=AF.Copy):
            def epi(fb, tb, ts, bk):
                o = stg.next()
                S.op("act", lambda e: e.activation(out=o.ap[:, 0:ts], in_=bk.ap[:, 0:ts], func=func), reads=[bk.res], writes=[o.res])
                S.dma("act", dst[fb * 128:(fb + 1) * 128, t0 + tb * 512:t0 + tb * 512 + ts], o.ap[:, 0:ts], reads=[o.res])
            return epi

        def epi_rope(dst):
            def epi(fb, tb, ts, bk):
                qb = stq.next()
                S.op("act", lambda e: e.activation(out=qb.ap[:, 0:ts], in_=bk.ap[:, 0:ts], func=AF.Copy), reads=[bk.res], writes=[qb.res])
                b2 = self.bank()
                S.op("pe", lambda e: e.matmul(b2.ap[:, 0:ts], lhsT=pm, rhs=qb.ap[:, 0:ts], start=True, stop=True),
                     reads=[qb.res, self.cb.res], writes=[b2.res])
                t1, t2, o = f32s.next(), f32s.next(), stg.next()
                cs = slice(tb * 512, tb * 512 + ts)
                S.op("dve", lambda e: e.tensor_tensor(out=t1.ap[:, 0:ts], in0=qb.ap[:, 0:ts], in1=cos.ap[:, cs], op=ALU.mult),
                     reads=[qb.res, cos.res], writes=[t1.res])
                S.op("dve", lambda e: e.tensor_tensor(out=t2.ap[:, 0:ts], in0=b2.ap[:, 0:ts], in1=sin.ap[:, cs], op=ALU.mult),
                     reads=[b2.res, sin.res], writes=[t2.res])
                S.op("dve", lambda e: e.tensor_tensor(out=o.ap[:, 0:ts], in0=t1.ap[:, 0:ts], in1=t2.ap[:, 0:ts], op=ALU.add),
                     reads=[t1.res, t2.res], writes=[o.res])
                S.dma("act", dst[fb * 128:(fb + 1) * 128, t0 + tb * 512:t0 + tb * 512 + ts], o.ap[:, 0:ts], reads=[o.res])
            return epi

        import os
        en = lambda k: k in os.environ.get("P1", "q,k,x,sc,g,v,kc,z,dt").split(",")
        qk_epi = epi_rope if kind == "s" else epi_copy
        if en("q"):
            self.gemm_fm(hT, 16, T, W[:, O_Q:O_Q + 2048], qk_epi(self.QT))
        if en("k"):
            self.gemm_fm(hT, 16, T, W[:, O_K:O_K + 512], qk_epi(self.KT))
        if en("x"):
            self.gemm_fm(hT, 16, T, W[:, O_X:O_X + 3072], epi_copy(self.XBC))
        if en("sc"):
            self.gemm_fm(hT, 16, T, W[:, O_SCB:O_SCB + 2048], epi_copy(self.SCB))
            self.gemm_fm(hT, 16, T, W[:, O_SCC:O_SCC + 2048], epi_copy(self.SCC))
            self.gemm_fm(hT, 16, T, W[:, O_SCH:O_SCH + 2048], epi_copy(self.SCH))
        if en("g"):
            self.gemm_fm(hT, 16, T, W[:, O_G:O_G + 6144], epi_copy(self.G, AF.Sigmoid))

        def epi_tm(dst, func=AF.Copy, dt=BF16, out32=None):
            def epi(blk, bw, m, bk):
                if dst is not None:
                    o = stg.next() if dt == BF16 else f32s.next()
                    S.op("act", lambda e: e.activation(out=o.ap[:, 0:bw], in_=bk.ap[:, 0:bw], func=func), reads=[bk.res], writes=[o.res])
                    S.dma("act", dst[t0 + m * 128:t0 + (m + 1) * 128, blk:blk + bw], o.ap[:, 0:bw], reads=[o.res])
                if out32 is not None:
                    o2 = f32s.next()
                    S.op("act", lambda e: e.activation(out=o2.ap[:, 0:bw], in_=bk.ap[:, 0:bw], func=AF.Copy), reads=[bk.res], writes=[o2.res])
                    tok = t0 + m * 128
                    S.dma("act", out32[tok // 256, l, tok % 256:tok % 256 + 128, blk:blk + bw], o2.ap[:, 0:bw], reads=[o2.res])
            return epi

        if en("v"):
            self.gemm_tm(hT, 16, T, W[:, O_V:O_V + 512], epi_tm(self.V, out32=self.ocv if kind == "p" else None))
        if kind == "p" and en("kc"):
            self.gemm_tm(hT, 16, T, W[:, O_K:O_K + 512], epi_tm(None, out32=self.ock))
        if en("z"):
            self.gemm_tm(hT, 16, T, W[:, O_Z:O_Z + 2048], epi_tm(self.SZ, AF.Silu))
        if en("dt"):
            self.gemm_tm(hT, 16, T, W[:, O_DT:O_DT + 64], epi_tm(self.DT, dt=F32))
        A.release()

    def attention(self, l):
        c, S, A = self.cfg, self.S, self.A
        A.mark()
        esink = A.tl([128, NH], F32)
        S.dma("sp", esink.ap, dram_bc(self.sink[l, :]), writes=[esink.res])
        S.op("act", lambda e: e.activation(out=esink.ap, in_=esink.ap, func=AF.Exp), reads=[esink.res], writes=[esink.res])
        onesb = self.cblk(C_ONES, BF16)
        idb = self.cblk(C_ID, BF16)
        mprev, mnext = self.cblk(C_MPREV, BF16), self.cblk(C_MNEXT, BF16)
        Lmax = max(L for (_, L, _, _) in self.seqs)
        asets = Ring([dict(KT=A.tl([128, Lmax], BF16), V=A.tl([128, Lmax], BF16), Q=A.tl([128, 4 * Lmax], BF16),
                           Y=A.tl([128, 4 * Lmax], BF16), ckt=A.tl([128, 2, 128], BF16), cvt=A.tl([128, 2, 128], BF16),
                           cKT=A.tl([128, 256], BF16)) for _ in range(2)])
        pT = A.ring(6, [128, 4, 128], BF16)
        dn = A.ring(2, [128, 4, 128], F32)
        for (tok0, L, kind, si) in self.seqs:
            nb = L // 128
            for g in range(NKV):
                st_ = asets.next()
                KTt = Tl(st_["KT"].ap[:, 0:L], st_["KT"].res)
                Vt = Tl(st_["V"].ap[:, 0:L].rearrange("p (b d) -> p b d", d=128), st_["V"].res)
                Q = Tl(st_["Q"].ap[:, 0:4 * L].rearrange("p (h t) -> p h t", h=4), st_["Q"].res)
                Y = Tl(st_["Y"].ap[:, 0:4 * L].rearrange("p (h t) -> p h t", h=4), st_["Y"].res)
                S.dma("sp", KTt.ap, self.KT[g * 128:(g + 1) * 128, tok0:tok0 + L], writes=[KTt.res])
                S.dma("sp", Vt.ap, self.V[tok0:tok0 + L, g * 128:(g + 1) * 128].rearrange("(b p) d -> p b d", p=128), writes=[Vt.res])
                S.dma("sp", Q.ap, self.QT[4 * g * 128:(4 * g + 4) * 128, tok0:tok0 + L].rearrange("(h p) t -> p h t", p=128), writes=[Q.res])
                if kind == "s":
                    ckt, cvt, cKT = st_["ckt"], st_["cvt"], st_["cKT"]
                    S.dma("pool", ckt.ap, self.ck[l, :, g * 128:(g + 1) * 128].rearrange("(b p) d -> p b d", p=128), writes=[ckt.res])
                    S.dma("pool", cvt.ap, self.cv[l, :, g * 128:(g + 1) * 128].rearrange("(b p) d -> p b d", p=128), writes=[cvt.res])
                    bk = self.bank()
                    pb = bk.ap.bitcast(BF16)
                    for b in range(2):
                        S.op("pe", lambda e, b=b, pb=pb: e.transpose(out=pb[:, b * 128:(b + 1) * 128], in_=ckt.ap[:, b, :], identity=idb),
                             reads=[ckt.res, self.cb.res], writes=[bk.res])
                    S.op("act", lambda e, pb=pb: e.activation(out=cKT.ap, in_=pb[:, 0:256], func=AF.Copy), reads=[bk.res], writes=[cKT.res])
                for i in range(nb):
                    kbs = []
                    if kind == "s":
                        if i > 0:
                            kbs.append((KTt.ap[:, (i - 1) * 128:i * 128], Vt.ap[:, i - 1, :], mprev, [KTt.res, Vt.res]))
                        kbs.append((KTt.ap[:, i * 128:(i + 1) * 128], Vt.ap[:, i, :], None, [KTt.res, Vt.res]))
                        if i < nb - 1:
                            kbs.append((KTt.ap[:, (i + 1) * 128:(i + 2) * 128], Vt.ap[:, i + 1, :], mnext, [KTt.res, Vt.res]))
                        for b in range(2):
                            kbs.append((cKT.ap[:, b * 128:(b + 1) * 128], cvt.ap[:, b, :], None, [cKT.res, cvt.res]))
                    else:
                        for b in range(nb):
                            kbs.append((KTt.ap[:, b * 128:(b + 1) * 128], Vt.ap[:, b, :], None, [KTt.res, Vt.res]))
                    qv = Q.ap[:, :, i * 128:(i + 1) * 128]
                    sbanks = []
                    for idx, (kap, vap, mask, rr) in enumerate(kbs):
                        bs = self.bank()
                        bsv = bs.ap.rearrange("p (h q) -> p h q", h=4)
                        S.op("pe", lambda e, bsv=bsv, kap=kap, qv=qv: e.matmul(bsv, lhsT=kap, rhs=qv, start=True, stop=True),
                             reads=rr + [Q.res], writes=[bs.res])
                        sbanks.append((bs, bsv))
                    bo, bd = self.bank(), self.bank()
                    for idx, (kap, vap, mask, rr) in enumerate(kbs):
                        bs, bsv = sbanks[idx]
                        p = pT.next()
                        S.op("act", lambda e, p=p, bsv=bsv: e.activation(out=p.ap, in_=bsv, func=AF.Exp, scale=ATT_SCALE),
                             reads=[bs.res], writes=[p.res])
                        if mask is not None:
                            S.op("dve", lambda e, p=p, mask=mask: e.tensor_tensor(out=p.ap, in0=p.ap, in1=bc_mid(mask, 4), op=ALU.mult),
                                 reads=[p.res, self.cb.res], writes=[p.res])
                        first, last = idx == 0, idx == len(kbs) - 1
                        S.op("pe", lambda e, p=p, vap=vap, bo=bo, first=first, last=last: e.matmul(
                            bo.ap.rearrange("p (h q) -> p h q", h=4), lhsT=vap, rhs=p.ap, start=first, stop=last),
                            reads=rr + [p.res], writes=[bo.res])
                        S.op("pe", lambda e, p=p, bd=bd, first=first, last=last: e.matmul(
                            bd.ap.rearrange("p (h q) -> p h q", h=4), lhsT=onesb, rhs=p.ap, start=first, stop=last),
                            reads=[p.res, self.cb.res], writes=[bd.res])
                    d = dn.next()
                    S.op("dve", lambda e, d=d, bd=bd: e.tensor_tensor(out=d.ap, in0=bd.ap.rearrange("p (h q) -> p h q", h=4),
                                                                      in1=bc_last(esink.ap[:, 4 * g:4 * g + 4], 128), op=ALU.add),
                         reads=[bd.res, esink.res], writes=[d.res])
                    S.op("dve", lambda e, d=d: e.reciprocal(out=d.ap, in_=d.ap), reads=[d.res], writes=[d.res])
                    S.op("dve", lambda e, d=d, bo=bo, i=i: e.tensor_tensor(
                        out=Y.ap[:, :, i * 128:(i + 1) * 128], in0=bo.ap.rearrange("p (h q) -> p h q", h=4), in1=d.ap, op=ALU.mult),
                        reads=[bo.res, d.res], writes=[Y.res])
                S.dma("act", self.YATT[4 * g * 128:(4 * g + 4) * 128, tok0:tok0 + L].rearrange("(h p) t -> p h t", p=128), Y.ap, reads=[Y.res])
        self.end_scope()

    def ssd_conv(self, l):
        c, S, A = self.cfg, self.S, self.A
        A.mark()
        idb = self.cblk(C_ID, BF16)
        pc = self.pc
        xin = A.ring(3, [128, 516], BF16)
        acc = A.ring(2, [128, 512], F32)
        xc = A.ring(3, [128, 512], BF16)
        xtok = A.ring(2, [128, 4, 2560], BF16)
        for (tok0, L, kind, si) in self.seqs:
            for b0 in range(0, L, 512):
                TB = min(512, L - b0)
                ns = TB // 128
                xt = xtok.next()
                for cc in range(24):
                    xi = xin.next()
                    lo = max(b0 - 2, 0)
                    hi = min(b0 + TB + 2, L)
                    if b0 == 0:
                        S.op("dve", lambda e, xi=xi: e.memset(xi.ap[:, 0:2], 0.0), writes=[xi.res])
                    if b0 + TB == L:
                        S.op("dve", lambda e, xi=xi, TB=TB: e.memset(xi.ap[:, TB + 2:TB + 4], 0.0), writes=[xi.res])
                    S.dma("sp", xi.ap[:, lo - (b0 - 2):hi - (b0 - 2)], self.XBC[cc * 128:(cc + 1) * 128, tok0 + lo:tok0 + hi], writes=[xi.res])
                    a = acc.next()
                    S.op("act", lambda e, a=a, xi=xi, cc=cc, TB=TB: e.activation(
                        out=a.ap[:, 0:TB], in_=xi.ap[:, 0:TB], func=AF.Identity, scale=pc.ap[:, PC_CW + cc * 5:PC_CW + cc * 5 + 1]),
                        reads=[xi.res, pc.res], writes=[a.res])
                    for k in range(1, 5):
                        S.op("dve", lambda e, a=a, xi=xi, cc=cc, k=k, TB=TB: e.scalar_tensor_tensor(
                            out=a.ap[:, 0:TB], in0=xi.ap[:, k:k + TB], scalar=pc.ap[:, PC_CW + cc * 5 + k:PC_CW + cc * 5 + k + 1],
                            in1=a.ap[:, 0:TB], op0=ALU.mult, op1=ALU.add), reads=[xi.res, pc.res, a.res], writes=[a.res])
                    x = xc.next()
                    S.op("act", lambda e, a=a, x=x, cc=cc, TB=TB: e.activation(
                        out=x.ap[:, 0:TB], in_=a.ap[:, 0:TB], func=AF.Silu, bias=pc.ap[:, PC_CB + cc:PC_CB + cc + 1]),
                        reads=[a.res, pc.res], writes=[x.res])
                    if cc >= 16:
                        S.dma("act", self.XCT[(cc - 16) * 128:(cc - 15) * 128, tok0 + b0:tok0 + b0 + TB], x.ap[:, 0:TB], reads=[x.res])
                    if cc < 20:
                        bk = self.bank()
                        pb = bk.ap.bitcast(BF16)
                        for s in range(ns):
                            S.op("pe", lambda e, pb=pb, s=s, x=x: e.transpose(out=pb[:, s * 128:(s + 1) * 128], in_=x.ap[:, s * 128:(s + 1) * 128], identity=idb),
                                 reads=[x.res, self.cb.res], writes=[bk.res])
                        S.op("act", lambda e, pb=pb, xt=xt, cc=cc, ns=ns: e.activation(
                            out=xt.ap[:, 0:ns, cc * 128:(cc + 1) * 128], in_=pb[:, 0:ns * 128].rearrange("p (s d) -> p s d", s=ns), func=AF.Copy),
                            reads=[bk.res], writes=[xt.res])
                S.dma("act", self.XTOK[tok0 + b0:tok0 + b0 + TB, :].rearrange("(s p) d -> p s d", p=128), xt.ap[:, 0:ns, :], reads=[xt.res])
        A.release()

    def ssd_sweep(self, l, d):
        c, S, A = self.cfg, self.S, self.A
        A.mark()
        idb, idf = self.cblk(C_ID, BF16), self.cblk(C_ID, F32)
        U = self.cblk(C_UF if d == 0 else C_UB, F32)
        Ub = self.cblk(C_UF if d == 0 else C_UB, BF16)
        NUb = self.cblk(C_NUF if d == 0 else C_NUB, BF16)
        MB = self.cblk(C_MBF if d == 0 else C_MBB, BF16)
        onesf = self.cblk(C_ONES, F32)
        cres = [self.cf.res, self.cb.res]
        dtb = A.tl([128, 32], F32)
        acoef = A.tl([128, 32], F32)
        S.dma("sp", dtb.ap, dram_bc(self.dt_bias[l, d * 32:(d + 1) * 32]), writes=[dtb.res])
        S.dma("sp", acoef.ap, dram_bc(self.a_log[l, d * 32:(d + 1) * 32]), writes=[acoef.res])
        S.op("act", lambda e: e.activation(out=acoef.ap, in_=acoef.ap, func=AF.Exp), reads=[acoef.res], writes=[acoef.res])
        S.op("dve", lambda e: e.tensor_scalar(out=acoef.ap, in0=acoef.ap, scalar1=-1.0, scalar2=None, op0=ALU.mult), reads=[acoef.res], writes=[acoef.res])
        if d == 1:
            Db = A.tl([128, 32], F32)
            gn = A.tl([128, D], F32)
            S.dma("sp", Db.ap, dram_bc(self.ssm_d[l, :]), writes=[Db.res])
            S.dma("sp", gn.ap, dram_bc(self.ssm_norm_g[l, :]), writes=[gn.res])
        St = A.tl([128, 4, 512], F32)
        Sb = A.tl([128, 4, 512], BF16)
        xtk = A.ring(2, [128, 2560], BF16)
        bct = A.ring(2, [128, 8, 128], BF16)
        dtr = A.ring(2, [128, 32], F32)
        sm = A.ring(2, [128, 6, 32], F32)
        abc = A.ring(2, [128, 2, 32 * 128], BF16)
        ahl = A.ring(2, [128, 2, 32], BF16)
        xdt = A.ring(2, [128, D], BF16)
        xdte = A.ring(2, [128, D], BF16)
        cbT = A.ring(2, [128, 4, 128], F32)
        dec = A.ring(2, [128, 4, 128], F32)
        LT = A.ring(2, [128, 32, 128], BF16)
        ych = A.ring(2, [128, D], F32)
        t512 = A.ring(2, [128, 512], F32)
        if d == 1:
            yf = A.ring(2, [128, D], F32)
            szt = A.ring(2, [128, D], BF16)
            ybf = A.ring(2, [128, D], BF16)
            junk = A.tl([128, D], BF16)
            s4 = A.ring(2, [128, 4], F32)
            yTt = A.ring(2, [128, 16, 128], BF16)
        f32o = A.ring(2, [128, 128], F32)
        for (tok0, L, kind, si) in self.seqs:
            nch = L // 128
            if kind == "p":
                S.op("dve", lambda e: e.memset(St.ap, 0.0), writes=[St.res])
            else:
                for j in range(16):
                    ld = f32o.next()
                    S.dma("sp", ld.ap, self.st[l, d, j * 128:(j + 1) * 128, :], writes=[ld.res])
                    bk = self.bank()
                    S.op("pe", lambda e, bk=bk, ld=ld: e.transpose(out=bk.ap[:, 0:128], in_=ld.ap, identity=idf),
                         reads=[ld.res, self.cf.res], writes=[bk.res])
                    S.op("act", lambda e, bk=bk, j=j: e.activation(out=St.ap[:, j // 4, (j % 4) * 128:(j % 4 + 1) * 128], in_=bk.ap[:, 0:128], func=AF.Copy),
                         reads=[bk.res], writes=[St.res])
            S.op("act", lambda e: e.activation(out=Sb.ap, in_=St.ap, func=AF.Copy), reads=[St.res], writes=[Sb.res])
            order = list(range(nch)) if d == 0 else list(range(nch - 1, -1, -1))

            def stageA(ci):
                tk = tok0 + ci * 128
                xt, bc, dr, m = xtk.next(), bct.next(), dtr.next(), sm.next()
                S.dma("sp", xt.ap, self.XTOK[tk:tk + 128, :], writes=[xt.res])
                S.dma("sp", bc.ap, self.XCT[:, tk:tk + 128].rearrange("(c p) t -> p c t", p=128), writes=[bc.res])
                S.dma("sp", dr.ap, self.DT[tk:tk + 128, d * 32:(d + 1) * 32], writes=[dr.res])
                dt_, a_, ac_, E_, te_, cd_ = (m.ap[:, i, :] for i in range(6))
                S.op("dve", lambda e, dr=dr, dt_=dt_: e.tensor_tensor(out=dt_, in0=dr.ap, in1=dtb.ap, op=ALU.add), reads=[dr.res, dtb.res], writes=[m.res])
                S.op("act", lambda e, dt_=dt_: e.activation(out=dt_, in_=dt_, func=AF.Exp), reads=[m.res], writes=[m.res])
                S.op("act", lambda e, dt_=dt_: e.activation(out=dt_, in_=dt_, func=AF.Ln, bias=1.0), reads=[m.res], writes=[m.res])
                S.op("dve", lambda e, dt_=dt_, a_=a_: e.tensor_tensor(out=a_, in0=dt_, in1=acoef.ap, op=ALU.mult), reads=[m.res, acoef.res], writes=[m.res])
                xd = xdt.next()
                S.op(POOL_ENG, lambda e, xd=xd, xt=xt, dt_=dt_: e.tensor_tensor(
                    out=xd.ap.rearrange("p (h q) -> p h q", h=32), in0=xt.ap[:, 0:D].rearrange("p (h q) -> p h q", h=32),
                    in1=bc_last(dt_, 64), op=ALU.mult), reads=[xt.res, m.res], writes=[xd.res])
                ab, hl = abc.next(), ahl.next()
                S.op("dve", lambda e: e.tensor_copy(out=hl.ap[:, 0, :], in_=a_), reads=[m.res], writes=[hl.res])
                S.op("dve", lambda e: e.tensor_tensor(out=hl.ap[:, 1, :], in0=a_, in1=hl.ap[:, 0, :], op=ALU.subtract), reads=[m.res, hl.res], writes=[hl.res])
                abv = [ab.ap[:, k, :].rearrange("p (h j) -> p h j", h=32) for k in range(2)]
                for k in range(2):
                    S.op("dve", lambda e, k=k: e.tensor_copy(out=abv[k], in_=bc_last(hl.ap[:, k, :], 128)), reads=[hl.res], writes=[ab.res])
                bk = self.bank()
                S.op("pe", lambda e, bk=bk, a_=a_: e.matmul(bk.ap[:, 0:32], lhsT=U, rhs=a_, start=True, stop=True), reads=[m.res] + cres, writes=[bk.res])
                S.op("pe", lambda e, bk=bk, a_=a_: e.matmul(bk.ap[:, 32:64], lhsT=onesf, rhs=a_, start=True, stop=True), reads=[m.res] + cres, writes=[bk.res])
                S.op("dve", lambda e, bk=bk, ac_=ac_: e.tensor_scalar(out=ac_, in0=bk.ap[:, 0:32], scalar1=1.0, scalar2=None, op0=ALU.mult), reads=[bk.res], writes=[m.res])
                S.op("act", lambda e, bk=bk, E_=E_: e.activation(out=E_, in_=bk.ap[:, 0:32], func=AF.Exp), reads=[bk.res], writes=[m.res])
                S.op("act", lambda e, bk=bk, cd_=cd_: e.activation(out=cd_, in_=bk.ap[:, 32:64], func=AF.Exp), reads=[bk.res], writes=[m.res])
                S.op("dve", lambda e, bk=bk, ac_=ac_, te_=te_: e.tensor_tensor(out=te_, in0=bk.ap[:, 32:64], in1=ac_, op=ALU.subtract), reads=[bk.res, m.res], writes=[m.res])
                S.op("act", lambda e, te_=te_: e.activation(out=te_, in_=te_, func=AF.Exp), reads=[m.res], writes=[m.res])
                bkc = self.bank()
                for g in range(4):
                    S.op("pe", lambda e, bkc=bkc, bc=bc, g=g: e.matmul(bkc.ap[:, g * 128:(g + 1) * 128], lhsT=bc.ap[:, g, :], rhs=bc.ap[:, 4 + g, :], start=True, stop=True),
                         reads=[bc.res], writes=[bkc.res])
                cbt = cbT.next()
                S.op("act", lambda e, bkc=bkc, cbt=cbt: e.activation(out=cbt.ap, in_=bkc.ap.rearrange("p (g i) -> p g i", g=4), func=AF.Copy), reads=[bkc.res], writes=[cbt.res])
                xe = xdte.next()
                S.op(POOL_ENG, lambda e, xe=xe, xd=xd, te_=te_: e.tensor_tensor(
                    out=xe.ap.rearrange("p (h q) -> p h q", h=32), in0=xd.ap.rearrange("p (h q) -> p h q", h=32),
                    in1=bc_last(te_, 64), op=ALU.mult), reads=[xd.res, m.res], writes=[xe.res])
                return (tk, xt, bc, m, xd, xe, ab, abv, cbt)

            def stageA2(pack):
                tk, xt, bc, m, xd, xe, ab, abv, cbt = pack
                lt = LT.next()
                for q in range(8):
                    g = q // 2
                    bs = self.bank()
                    for hh in range(4):
                        h = q * 4 + hh
                        for k in range(2):
                            S.op("pe", lambda e, h=h, hh=hh, k=k: e.matmul(bs.ap[:, hh * 128:(hh + 1) * 128], lhsT=abv[k][:, h, :], rhs=Ub,
                                                                           start=(hh == 0 and k == 0), stop=False),
                                 reads=[ab.res] + cres, writes=[bs.res])
                    bsv = bs.ap.rearrange("p (h i) -> p h i", h=4)
                    for k in range(2):
                        S.op("pe", lambda e, k=k: e.matmul(bsv, lhsT=NUb, rhs=abv[k][:, q * 4:(q + 1) * 4, :], start=False, stop=False),
                             reads=[ab.res] + cres, writes=[bs.res])
                    S.op("pe", lambda e, bsv=bsv: e.matmul(bsv, lhsT=idb, rhs=bc_mid(MB, 4), start=False, stop=True), reads=cres, writes=[bs.res])
                    dc = dec.next()
                    S.op("act", lambda e, dc=dc, bsv=bsv: e.activation(out=dc.ap, in_=bsv, func=AF.Exp), reads=[bs.res], writes=[dc.res])
                    S.op("dve", lambda e, dc=dc, lt=lt, cbt=cbt, q=q, g=g: e.tensor_tensor(
                        out=lt.ap[:, q * 4:(q + 1) * 4, :], in0=dc.ap, in1=bc_mid(cbt.ap[:, g, :], 4), op=ALU.mult),
                        reads=[dc.res, cbt.res], writes=[lt.res])
                return (tk, xt, bc, m, xd, xe, lt)

            def stageB(pack):
                tk, xt, bc, m, xd, xe, lt = pack
                dt_, a_, ac_, E_, te_, cd_ = (m.ap[:, i, :] for i in range(6))
                yc = ych.next()
                for g in range(4):
                    by, bo = self.bank(), self.bank()
                    for hh in range(8):
                        h = g * 8 + hh
                        S.op("pe", lambda e, by=by, lt=lt, xd=xd, h=h, hh=hh: e.matmul(
                            by.ap[:, hh * 64:(hh + 1) * 64], lhsT=lt.ap[:, h, :], rhs=xd.ap[:, h * 64:(h + 1) * 64], start=True, stop=True),
                            reads=[lt.res, xd.res], writes=[by.res])
                    S.op("pe", lambda e, bo=bo, bc=bc, g=g: e.matmul(bo.ap, lhsT=bc.ap[:, 4 + g, :], rhs=Sb.ap[:, g, :], start=True, stop=True),
                         reads=[bc.res, Sb.res], writes=[bo.res])
                    t5 = t512.next()
                    S.op("dve", lambda e, t5=t5, bo=bo, E_=E_, g=g: e.tensor_tensor(
                        out=t5.ap.rearrange("p (h q) -> p h q", h=8), in0=bo.ap.rearrange("p (h q) -> p h q", h=8),
                        in1=bc_last(E_[:, g * 8:(g + 1) * 8], 64), op=ALU.mult), reads=[bo.res, m.res], writes=[t5.res])
                    S.op("dve", lambda e, t5=t5, by=by, yc=yc, g=g: e.tensor_tensor(
                        out=yc.ap[:, g * 512:(g + 1) * 512], in0=by.ap, in1=t5.ap, op=ALU.add), reads=[by.res, t5.res], writes=[yc.res])
                for g in range(4):
                    bst = self.bank()
                    S.op("pe", lambda e, bst=bst, xt=xt, xe=xe, g=g: e.matmul(
                        bst.ap, lhsT=xt.ap[:, D + g * 128:D + (g + 1) * 128], rhs=xe.ap[:, g * 512:(g + 1) * 512], start=True, stop=True),
                        reads=[xt.res, xe.res], writes=[bst.res])
                    S.op("dve", lambda e, cd_=cd_, g=g: e.tensor_tensor(
                        out=St.ap[:, g, :].rearrange("p (h q) -> p h q", h=8), in0=St.ap[:, g, :].rearrange("p (h q) -> p h q", h=8),
                        in1=bc_last(cd_[:, g * 8:(g + 1) * 8], 64), op=ALU.mult), reads=[St.res, m.res], writes=[St.res])
                    S.op("dve", lambda e, bst=bst, g=g: e.tensor_tensor(out=St.ap[:, g, :], in0=St.ap[:, g, :], in1=bst.ap, op=ALU.add),
                         reads=[St.res, bst.res], writes=[St.res])
                S.op("act", lambda e: e.activation(out=Sb.ap, in_=St.ap, func=AF.Copy), reads=[St.res], writes=[Sb.res])
                if d == 0:
                    S.dma("act", self.YF[tk:tk + 128, :], yc.ap, reads=[yc.res])
                else:
                    y0, sz, yb, s, yT = yf.next(), szt.next(), ybf.next(), s4.next(), yTt.next()
                    S.dma("sp", y0.ap, self.YF[tk:tk + 128, :], writes=[y0.res])
                    S.dma("sp", sz.ap, self.SZ[tk:tk + 128, :], writes=[sz.res])
                    S.op("dve", lambda e, yc=yc, y0=y0: e.tensor_tensor(out=yc.ap, in0=yc.ap, in1=y0.ap, op=ALU.add), reads=[yc.res, y0.res], writes=[yc.res])
                    S.op("dve", lambda e, y0=y0, xt=xt: e.tensor_tensor(
                        out=y0.ap.rearrange("p (h q) -> p h q", h=32), in0=xt.ap[:, 0:D].rearrange("p (h q) -> p h q", h=32),
                        in1=bc_last(Db.ap, 64), op=ALU.mult), reads=[xt.res, Db.res], writes=[y0.res])
                    S.op("dve", lambda e, yc=yc, y0=y0: e.tensor_tensor(out=yc.ap, in0=yc.ap, in1=y0.ap, op=ALU.add), reads=[yc.res, y0.res], writes=[yc.res])
                    S.op("dve", lambda e, yc=yc, sz=sz: e.tensor_tensor(out=yc.ap, in0=yc.ap, in1=sz.ap, op=ALU.mult), reads=[yc.res, sz.res], writes=[yc.res])
                    self.ssq(yc, s, junk)
                    self.rstd_from_ssq(s)
                    S.op("dve", lambda e, yc=yc, yb=yb, s=s: e.scalar_tensor_tensor(
                        out=yb.ap, in0=yc.ap, scalar=s.ap[:, 3:4], in1=gn.ap, op0=ALU.mult, op1=ALU.mult),
                        reads=[yc.res, s.res, gn.res], writes=[yb.res])
                    for j in range(2):
                        bk = self.bank()
                        pb = bk.ap.bitcast(BF16)
                        for kk in range(8):
                            kc = j * 8 + kk
                            S.op("pe", lambda e, pb=pb, kk=kk, kc=kc, yb=yb: e.transpose(
                                out=pb[:, kk * 128:(kk + 1) * 128], in_=yb.ap[:, kc * 128:(kc + 1) * 128], identity=idb),
                                reads=[yb.res, self.cb.res], writes=[bk.res])
                        S.op("act", lambda e, pb=pb, yT=yT, j=j: e.activation(
                            out=yT.ap[:, j * 8:(j + 1) * 8, :], in_=pb.rearrange("p (a b) -> p a b", a=8), func=AF.Copy),
                            reads=[bk.res], writes=[yT.res])
                    S.dma("act", self.YSSD[:, tk:tk + 128].rearrange("(c p) t -> p c t", p=128), yT.ap, reads=[yT.res])
            nxt = stageA(order[0])
            for idx, ci in enumerate(order):
                cur = nxt
                if idx + 1 < len(order):
                    nxt = stageA(order[idx + 1])
                stageB(stageA2(cur))
            if kind == "p":
                for j in range(16):
                    bk = self.bank()
                    S.op("pe", lambda e, bk=bk, j=j: e.transpose(out=bk.ap[:, 0:128], in_=St.ap[:, j // 4, (j % 4) * 128:(j % 4 + 1) * 128], identity=idf),
                         reads=[St.res, self.cf.res], writes=[bk.res])
                    o = f32o.next()
                    S.op("act", lambda e, bk=bk, o=o: e.activation(out=o.ap, in_=bk.ap[:, 0:128], func=AF.Copy), reads=[bk.res], writes=[o.res])
                    S.dma("act", self.ost[si, l, d, j * 128:(j + 1) * 128, :], o.ap, reads=[o.res])
        A.release()

    def phase56(self, l, tile):
        c, S, A = self.cfg, self.S, self.A
        t0, T, r, kind = tile
        nm = T // 128
        A.mark()
        self.wbufs = A.ring(3, [128, 16 * 512], BF16)
        self.wc_n = 0
        pc = self.pc
        idb, idf = self.cblk(C_ID, BF16), self.cblk(C_ID, F32)
        big = A.tl([128, 16 * T], F32)
        merged = Tl(big.ap.rearrange("p (c t) -> p c t", c=16), big.res)
        oacc = Tl(big.ap.rearrange("p (m f) -> p m f", m=nm), big.res)
        yT = A.tl([128, 16, T], BF16)
        Gf = A.tl([128, D], F32)
        A.mark()
        yT1 = A.tl([128, 16, T], BF16)
        yT2 = A.tl([128, 16, T], BF16)
        xt = A.ring(2, [128, D], F32)
        Gm = A.tl([128, D], F32)
        Acol, Bcol = self.modcols(l, r, 3, 4, PC_GPF)
        self.gate_row(l, r, 2, self.g_post_mix, Gm, xt.items[0])
        self.gate_row(l, r, 5, self.g_post_ffn, Gf, xt.items[1])
        gt = A.ring(3, [128, 512], BF16)
        tmp = A.ring(2, [128, 512], F32)
        if kind == "p":
            sq0, sqL = (t0 // 256) * 256, 256
        else:
            sq0, sqL = c.NP, c.LS
        S.dma("sp", yT.ap, self.YATT[:, t0:t0 + T].rearrange("(c p) t -> p c t", p=128), writes=[yT.res])
        S.dma("sp", yT1.ap, self.YSSD[:, t0:t0 + T].rearrange("(c p) t -> p c t", p=128), writes=[yT1.res])
        cct = A.ring(2, [128, T + 2], BF16)
        cht = A.ring(2, [128, T + 2], BF16)
        cbt_ = A.ring(2, [128, T], BF16)
        ut = A.ring(2, [128, T + 2], F32)
        at = A.ring(2, [128, T], F32)

        def sc_step(cc):
            cc_, ch_, cb_, u, a = cct.next(), cht.next(), cbt_.next(), ut.next(), at.next()
            rows = slice(cc * 128, (cc + 1) * 128)
            S.dma("sp", cb_.ap, self.SCB[rows, t0:t0 + T], writes=[cb_.res])
            segs = [(s_ * 256, 256) for s_ in range(T // 256)] if kind == "p" else [(0, T)]
            S.op("dve", lambda e: e.memset(cc_.ap[:, 0:1], 0.0), writes=[cc_.res])
            S.op("dve", lambda e: e.memset(cc_.ap[:, T + 1:T + 2], 0.0), writes=[cc_.res])
            S.op("dve", lambda e: e.memset(ch_.ap[:, 0:1], 0.0), writes=[ch_.res])
            S.op("dve", lambda e: e.memset(ch_.ap[:, T + 1:T + 2], 0.0), writes=[ch_.res])
            lo = max(t0 - 1, sq0) if kind == "s" else t0
            hi = min(t0 + T + 1, sq0 + sqL) if kind == "s" else t0 + T
            S.dma("sp", cc_.ap[:, 1 + lo - t0:1 + hi - t0], self.SCC[rows, lo:hi], writes=[cc_.res])
            S.dma("sp", ch_.ap[:, 1 + lo - t0:1 + hi - t0], self.SCH[rows, lo:hi], writes=[ch_.res])
            S.op("dve", lambda e: e.tensor_tensor(out=u.ap, in0=cc_.ap, in1=ch_.ap, op=ALU.mult),
                 reads=[cc_.res, ch_.res], writes=[u.res])
            sk = 1 if kind == "p" else 0
            w0 = PC_SCW + cc * 3
            for (o0, ln) in segs:
                S.op("dve", lambda e: e.tensor_scalar(
                    out=a.ap[:, o0:o0 + ln], in0=u.ap[:, o0 + 1:o0 + 1 + ln], scalar1=pc.ap[:, w0 + 1:w0 + 2], scalar2=None, op0=ALU.mult),
                    reads=[u.res, pc.res], writes=[a.res])
                S.op("dve", lambda e: e.scalar_tensor_tensor(
                    out=a.ap[:, o0 + sk:o0 + ln], in0=u.ap[:, o0 + sk:o0 + ln], scalar=pc.ap[:, w0:w0 + 1],
                    in1=a.ap[:, o0 + sk:o0 + ln], op0=ALU.mult, op1=ALU.add), reads=[u.res, pc.res, a.res], writes=[a.res])
                S.op("dve", lambda e: e.scalar_tensor_tensor(
                    out=a.ap[:, o0:o0 + ln - sk], in0=u.ap[:, o0 + 2:o0 + 2 + ln - sk], scalar=pc.ap[:, w0 + 2:w0 + 3],
                    in1=a.ap[:, o0:o0 + ln - sk], op0=ALU.mult, op1=ALU.add), reads=[u.res, pc.res, a.res], writes=[a.res])
            S.op("dve", lambda e: e.tensor_tensor(out=yT2.ap[:, cc, :], in0=a.ap, in1=cb_.ap, op=ALU.mult),
                 reads=[a.res, cb_.res], writes=[yT2.res])

        ntb = (T + 511) // 512

        def epi_merge(b):
            def epi(fb, tb, ts, bk):
                g = gt.next()
                S.dma("sp", g.ap[:, 0:ts], self.G[b * D + fb * 128:b * D + (fb + 1) * 128, t0 + tb * 512:t0 + tb * 512 + ts], writes=[g.res])
                dst = merged.ap[:, fb, tb * 512:tb * 512 + ts]
                if b == 0:
                    S.op("dve", lambda e: e.tensor_tensor(out=dst, in0=bk.ap[:, 0:ts], in1=g.ap[:, 0:ts], op=ALU.mult),
                         reads=[bk.res, g.res], writes=[merged.res])
                    if tb == ntb - 1:
                        sc_step(fb)
                else:
                    t = tmp.next()
                    S.op("dve", lambda e: e.tensor_tensor(out=t.ap[:, 0:ts], in0=bk.ap[:, 0:ts], in1=g.ap[:, 0:ts], op=ALU.mult),
                         reads=[bk.res, g.res], writes=[t.res])
                    if b == 1:
                        S.op("dve", lambda e: e.tensor_tensor(out=dst, in0=dst, in1=t.ap[:, 0:ts], op=ALU.add),
                             reads=[merged.res, t.res], writes=[merged.res])
                    else:
                        S.op("dve", lambda e: e.tensor_tensor(out=yT.ap[:, fb, tb * 512:tb * 512 + ts], in0=dst, in1=t.ap[:, 0:ts], op=ALU.add),
                             reads=[merged.res, t.res], writes=[yT.res])
            return epi

        for b, (src, W) in enumerate(((yT, self.w_att_out), (yT1, self.w_ssd_out), (yT2, self.w_sc_out))):
            self.gemm_fm(src, 16, T, W[l], epi_merge(b), cache=True)

        def epi_o(blk, bw, m, bk):
            S.op("act", lambda e: e.activation(out=oacc.ap[:, m, blk:blk + bw], in_=bk.ap[:, 0:bw], func=AF.Copy), reads=[bk.res], writes=[oacc.res])

        self.gemm_tm(yT, 16, T, self.w_o[l], epi_o, cache=True)
        xn = A.ring(1, [128, D], BF16)
        junk = A.tl([128, D], BF16)
        tmpn = A.ring(1, [128, 8, 128], F32)
        st = A.ring(4, [128, 4], F32)
        xsrc = self.xin(l, t0, T)
        h2T = yT
        for m in range(nm):
            x, s = xt.next(), st.next()
            S.dma("sp", x.ap, xsrc[m * 128:(m + 1) * 128, :], writes=[x.res])
            o = Tl(oacc.ap[:, m, :], oacc.res)
            self.ssq(o, s, junk)
            self.rstd_from_ssq(s)
            S.op("dve", lambda e, o=o, s=s: e.scalar_tensor_tensor(out=o.ap, in0=o.ap, scalar=s.ap[:, 3:4], in1=Gm.ap, op0=ALU.mult, op1=ALU.mult),
                 reads=[oacc.res, s.res, Gm.res], writes=[oacc.res])
            S.op("dve", lambda e, o=o, x=x: e.tensor_tensor(out=x.ap, in0=o.ap, in1=x.ap, op=ALU.add), reads=[oacc.res, x.res], writes=[x.res])
            S.dma("act", self.XM[t0 + m * 128:t0 + (m + 1) * 128, :], x.ap, reads=[x.res])
            self.norm_T(x, h2T, m, Acol, Bcol, st.next(), junk, xn.next(), tmpn.next())
        self.end_scope()
        actT = A.tl([128, 44, T], BF16)
        sg = A.ring(2, [128, 512], F32)
        Wgu = self.w_gate_up[l]
        for fb4 in range(0, 44, 4):
            nf = min(4, 44 - fb4)
            wg, wgr = self.wload(Wgu[:, fb4 * 128:(fb4 + nf) * 128], 16, nf * 128, True)
            wu, wur = self.wload(Wgu[:, DFF + fb4 * 128:DFF + (fb4 + nf) * 128], 16, nf * 128, True)
            for f in range(nf):
                for tb in range((T + 511) // 512):
                    ts = min(512, T - tb * 512)
                    bg, bu = self.bank(), self.bank()
                    for (bk, wv, wr) in ((bg, wg, wgr), (bu, wu, wur)):
                        for kc in range(16):
                            S.op("pe", lambda e, bk=bk, wv=wv, kc=kc, f=f, tb=tb, ts=ts: e.matmul(
                                bk.ap[:, 0:ts], lhsT=wv[:, kc, f * 128:(f + 1) * 128], rhs=h2T.ap[:, kc, tb * 512:tb * 512 + ts],
                                start=(kc == 0), stop=(kc == 15)), reads=[wr, h2T.res], writes=[bk.res])
                    sgt = sg.next()
                    S.op("act", lambda e, sgt=sgt, bg=bg, ts=ts: e.activation(out=sgt.ap[:, 0:ts], in_=bg.ap[:, 0:ts], func=AF.Silu), reads=[bg.res], writes=[sgt.res])
                    S.op("dve", lambda e, sgt=sgt, bu=bu, ts=ts, fb=fb4 + f, tb=tb: e.tensor_tensor(
                        out=actT.ap[:, fb, tb * 512:tb * 512 + ts], in0=bu.ap[:, 0:ts], in1=sgt.ap[:, 0:ts], op=ALU.mult),
                        reads=[bu.res, sgt.res], writes=[actT.res])
        o2T = A.ring(2, [128, 512], F32)

        def wload_down(fb):
            return self.wload(self.w_down[l][:, fb * 128:(fb + 1) * 128], 44, 128, True)

        for fb in range(16):
            wv, wr = wload_down(fb)
            for tb in range((T + 511) // 512):
                ts = min(512, T - tb * 512)
                bk = self.bank()
                for kc in range(44):
                    S.op("pe", lambda e, bk=bk, wv=wv, kc=kc, tb=tb, ts=ts: e.matmul(
                        bk.ap[:, 0:ts], lhsT=wv[:, kc, :], rhs=actT.ap[:, kc, tb * 512:tb * 512 + ts], start=(kc == 0), stop=(kc == 43)),
                        reads=[wr, actT.res], writes=[bk.res])
                ot = o2T.next()
                S.op("act", lambda e, ot=ot, bk=bk, ts=ts: e.activation(out=ot.ap[:, 0:ts], in_=bk.ap[:, 0:ts], func=AF.Copy), reads=[bk.res], writes=[ot.res])
                b2 = self.bank()
                for s_ in range(ts // 128):
                    S.op("pe", lambda e, b2=b2, ot=ot, s_=s_: e.transpose(out=b2.ap[:, s_ * 128:(s_ + 1) * 128], in_=ot.ap[:, s_ * 128:(s_ + 1) * 128], identity=idf),
                         reads=[ot.res, self.cf.res], writes=[b2.res])
                m0 = tb * 4
                S.op("dve", lambda e, b2=b2, ts=ts, m0=m0, fb=fb: e.tensor_scalar(
                    out=oacc.ap[:, m0:m0 + ts // 128, fb * 128:(fb + 1) * 128], in0=b2.ap[:, 0:ts].rearrange("p (s d) -> p s d", s=ts // 128),
                    scalar1=1.0, scalar2=None, op0=ALU.mult),
                    reads=[b2.res], writes=[oacc.res])
        A.mark()
        xt = A.ring(2, [128, D], F32)
        junk = A.tl([128, D], BF16)
        st = A.ring(2, [128, 4], F32)
        xdst = self.xout(l, t0, T)
        for m in range(nm):
            x, s = xt.next(), st.next()
            S.dma("sp", x.ap, self.XM[t0 + m * 128:t0 + (m + 1) * 128, :], writes=[x.res])
            o = Tl(oacc.ap[:, m, :], oacc.res)
            self.ssq(o, s, junk)
            self.rstd_from_ssq(s)
            S.op("dve", lambda e, o=o, s=s: e.scalar_tensor_tensor(out=o.ap, in0=o.ap, scalar=s.ap[:, 3:4], in1=Gf.ap, op0=ALU.mult, op1=ALU.mult),
                 reads=[oacc.res, s.res, Gf.res], writes=[oacc.res])
            S.op("dve", lambda e, o=o, x=x: e.tensor_tensor(out=x.ap, in0=o.ap, in1=x.ap, op=ALU.add), reads=[oacc.res, x.res], writes=[x.res])
            S.dma("act", xdst[m * 128:(m + 1) * 128, :], x.ap, reads=[x.res])
        A.release()
        A.release()


def make_consts(LS):
    t = np.arange(128)
    cst = np.zeros((C_N, 128, 128), np.float32)
    cst[C_ID] = np.eye(128)
    cst[C_UF] = (t[:, None] <= t[None, :])
    cst[C_UB] = (t[:, None] >= t[None, :])
    cst[C_NUF] = -cst[C_UF]
    cst[C_NUB] = -cst[C_UB]
    cst[C_ONES] = 1.0
    cst[C_MBF] = np.where(t[None, :] >= t[:, None], 0.0, NEG)
    cst[C_MBB] = np.where(t[None, :] <= t[:, None], 0.0, NEG)
    cst[C_MPREV] = (t[:, None] >= t[None, :])
    cst[C_MNEXT] = (t[:, None] <= t[None, :])
    pm = np.zeros((128, 128), np.float32)
    for i in range(64):
        pm[2 * i + 1, 2 * i] = 1.0
        pm[2 * i, 2 * i + 1] = 1.0
    cst[C_PMAT] = pm
    cst = np.ascontiguousarray(cst.transpose(1, 0, 2).reshape(128, C_N * 128))
    pos = np.arange(LS)
    row = (pos // GRID_W).astype(np.float32)
    col = (pos % GRID_W).astype(np.float32)
    n_pairs = HD // 4
    inv = (10000.0 ** (-np.arange(n_pairs, dtype=np.float32) / n_pairs)).astype(np.float32)
    ang = np.concatenate([row[:, None] * inv, col[:, None] * inv], axis=-1).astype(np.float32)
    cos = np.cos(ang).astype(np.float32)
    sin = np.sin(ang).astype(np.float32)
    ropec = np.zeros((128, LS), np.float32)
    ropes = np.zeros((128, LS), np.float32)
    ropec[0::2] = cos.T
    ropec[1::2] = cos.T
    ropes[0::2] = -sin.T
    ropes[1::2] = sin.T
    return cst, ropec, ropes


def col16(v):
    return np.ascontiguousarray(v.reshape(-1, 128).T)


def make_pcol(inp, L):
    pcol = np.zeros((L, 128, PC_N), np.float32)
    for l in range(L):
        pcol[l, :, PC_GPM:PC_GPM + 16] = col16(inp["g_pre_mix"][l])
        pcol[l, :, PC_GPF:PC_GPF + 16] = col16(inp["g_pre_ffn"][l])
        cw = inp["ssm_conv_w"][l]
        for k in range(5):
            pcol[l, :, PC_CW + k:PC_CW + 120:5] = col16(cw[k])
        pcol[l, :, PC_CB:PC_CB + 24] = col16(inp["ssm_conv_b"][l])
        sw = inp["sc_conv_w"][l]
        for k in range(3):
            pcol[l, :, PC_SCW + k:PC_SCW + 48:3] = col16(sw[k])
        pcol[l, :, PC_BMOD:PC_BMOD + 96] = col16(inp["b_mod"][l])
    return pcol


_NC_CACHE = {}
SIM_HOOK = None


def get_nc(cfg_key):
    if cfg_key not in _NC_CACHE:
        _NC_CACHE[cfg_key] = KB(Cfg(*cfg_key))
    return _NC_CACHE[cfg_key]


def run(inp, NPS, LS, DEPTH, TT, TT5, n_cores, debug=False, stop=99):
    f = lambda a: np.ascontiguousarray(np.asarray(a, dtype=np.float32))
    inp = {k: f(v) for k, v in inp.items()}
    kb = KB(Cfg(NPS, LS, DEPTH, TT, TT5, debug, stop))
    cst, ropec, ropes = make_consts(LS)
    pcol = make_pcol(inp, DEPTH)
    nb = inp["x_sample"].shape[0]
    shared = {
        "pcol": pcol, "cst": cst, "ropec": ropec, "ropes": ropes,
        "w_mod": inp["w_mod"], "b_mod": inp["b_mod"], "w_in": inp["w_in"], "sink": inp["sink"],
        "ssm_dt_bias": inp["ssm_dt_bias"].reshape(DEPTH, 64), "ssm_a_log": inp["ssm_a_log"].reshape(DEPTH, 64),
        "ssm_d": inp["ssm_d"], "ssm_norm_g": inp["ssm_norm_g"], "w_att_out": inp["w_att_out"],
        "w_ssd_out": inp["w_ssd_out"], "w_sc_out": inp["w_sc_out"], "w_o": inp["w_o"],
        "g_post_mix": inp["g_post_mix"], "g_post_ffn": inp["g_post_ffn"], "w_gate_up": inp["w_gate_up"],
        "w_down": inp["w_down"],
    }
    in_maps = []
    for i in range(n_cores):
        b = i % nb
        cond = np.stack([inp["c_ctx"], inp["c"][b]], axis=0)
        condT = np.ascontiguousarray(cond.reshape(2, 16, 128).transpose(2, 1, 0))
        m = dict(shared)
        m["xp"] = np.ascontiguousarray(inp["x_prompt"][i * NPS:(i + 1) * NPS].reshape(NPS * 256, D))
        m["xs"] = inp["x_sample"][b]
        m["ck"] = np.ascontiguousarray(inp["cache_k"][b].reshape(DEPTH, 256, 512))
        m["cv"] = np.ascontiguousarray(inp["cache_v"][b].reshape(DEPTH, 256, 512))
        m["st"] = np.ascontiguousarray(inp["state_ssm"][b].reshape(DEPTH, 2, SH * SP_, SN))
        m["condT"] = condT
        in_maps.append(m)
    if SIM_HOOK is not None:
        R = SIM_HOOK(kb.nc, in_maps)
    else:
        if _os.environ.get("TRACE") == "1":
            res = run_bass_kernel_spmd(kb.nc, in_maps, core_ids=list(range(n_cores)), trace=True)
            print("EXEC_NS", res.exec_time_ns)
        else:
            res = run_bass_kernel_spmd(kb.nc, in_maps, core_ids=list(range(n_cores)))
        R = res.results
    y_prompt = np.concatenate([R[i]["yp"].reshape(NPS, 256, D) for i in range(n_cores)], axis=0)
    y_sample = np.stack([R[b]["ys"] for b in range(nb)], axis=0)
    nck = np.concatenate([R[i]["ock"].reshape(NPS, DEPTH, 256, NKV, HD) for i in range(n_cores)], axis=0)
    ncv = np.concatenate([R[i]["ocv"].reshape(NPS, DEPTH, 256, NKV, HD) for i in range(n_cores)], axis=0)
    nst = np.concatenate([R[i]["ost"].reshape(NPS, DEPTH, 2, SH, SP_, SN) for i in range(n_cores)], axis=0)
    outs = (y_prompt, y_sample, nck, ncv, nst)
    if debug:
        return outs, R
    return outs


def kernel(**inputs):
    return run(inputs, NPS=4, LS=4096, DEPTH=2, TT=1024, TT5=512, n_cores=8)
```
